# Optimizing a Trainium2 kernel written in Bass

```python
import math
import jax, jax.numpy as jnp
from jax import lax
import numpy as np

D_MODEL = 1024
BATCH = 8
SEQ = 2048
DEPTH = 2

HEAD_DIM = 64
DSA_HEADS = 4
DSA_TOPK = 256
IDX_HEADS = 8
IDX_DIM = 32
FOX_HEADS = 6
NSA_HEADS = 6
NSA_KV_GROUPS = 2
NSA_HPG = NSA_HEADS // NSA_KV_GROUPS
NSA_CMP_LEN = 32
NSA_CMP_STRIDE = 16
NSA_SEL_BLOCK = 64
NSA_SEL_N = 16
NSA_WINDOW = 512
NSA_Q_BLOCK = 64

Q_BLOCK = 128
ROPE_THETA = 10000.0
LN_EPS = 1e-5
ALPHA = (2.0 * DEPTH) ** 0.25
BETA = (8.0 * DEPTH) ** -0.25

DSA_W = DSA_HEADS * HEAD_DIM
FOX_W = FOX_HEADS * HEAD_DIM
NSA_W = NSA_HEADS * HEAD_DIM
NSA_KV_W = NSA_KV_GROUPS * HEAD_DIM
MIX_W = DSA_W + FOX_W + NSA_W

IN_SPLITS = (
    ("dsa_q", DSA_W), ("dsa_k", HEAD_DIM), ("dsa_v", HEAD_DIM),
    ("idx_q", IDX_HEADS * IDX_DIM), ("idx_k", IDX_DIM), ("idx_w", IDX_HEADS),
    ("fox_q", FOX_W), ("fox_k", FOX_W), ("fox_v", FOX_W), ("fox_f", FOX_HEADS),
    ("nsa_q", NSA_W),
    ("nsa_kc", NSA_KV_W), ("nsa_vc", NSA_KV_W),
    ("nsa_ks", NSA_KV_W), ("nsa_vs", NSA_KV_W),
    ("nsa_kw", NSA_KV_W), ("nsa_vw", NSA_KV_W),
    ("nsa_g", 3 * NSA_HEADS),
    ("gate", MIX_W),
)
IN_WIDTH = sum(w for _, w in IN_SPLITS)

kernel_name = "hybrid_dsa_fox_nsa_parallel_heads"


def split_cols(h):
    out, off = {}, 0
    for name, w in IN_SPLITS:
        out[name] = h[..., off:off + w]
        off += w
    return out


def layer_norm(x, g, b):
    xf = x.astype(jnp.float32)
    mu = jnp.mean(xf, -1, keepdims=True)
    var = jnp.mean(jnp.square(xf - mu), -1, keepdims=True)
    return ((xf - mu) * lax.rsqrt(var + LN_EPS) * g + b).astype(x.dtype)


def rope(x, pos):
    half = x.shape[-1] // 2
    inv = ROPE_THETA ** (-jnp.arange(half, dtype=jnp.float32) / half)
    ang = pos.astype(jnp.float32)[:, None] * inv[None, :]
    cos = jnp.cos(ang)[:, None, :]
    sin = jnp.sin(ang)[:, None, :]
    x1, x2 = x[..., :half], x[..., half:]
    return jnp.concatenate([x1 * cos - x2 * sin, x1 * sin + x2 * cos], -1).astype(x.dtype)


def masked_softmax(s, mask):
    s = jnp.where(mask, s.astype(jnp.float32), -jnp.inf)
    m = jnp.max(s, -1, keepdims=True)
    m = jnp.where(jnp.isfinite(m), m, 0.0)
    e = jnp.where(mask, jnp.exp(s - m), 0.0)
    return e / jnp.maximum(jnp.sum(e, -1, keepdims=True), 1e-30)


def stack_blocks(out):
    nb, B, T, H, D = out.shape
    return out.transpose(1, 0, 2, 3, 4).reshape(B, nb * T, H, D)


def dsa_mixer(q, k, v, iq, ik, iw):
    B, S, H, D = q.shape
    topk = min(DSA_TOPK, S // 4)
    kpos = jnp.arange(S)
    scale = D ** -0.5
    gather = jax.vmap(lambda kb, ib: kb[ib])

    def block(i):
        t0 = i * Q_BLOCK
        tpos = t0 + jnp.arange(Q_BLOCK)
        qb = lax.dynamic_slice_in_dim(q, t0, Q_BLOCK, 1)
        iqb = lax.dynamic_slice_in_dim(iq, t0, Q_BLOCK, 1)
        iwb = lax.dynamic_slice_in_dim(iw, t0, Q_BLOCK, 1).astype(jnp.float32)
        isc = jnp.einsum("bthd,bsd->bths", iqb, ik).astype(jnp.float32)
        isc = jnp.einsum("bth,bths->bts", iwb, jax.nn.relu(isc))
        causal = kpos[None, :] <= tpos[:, None]
        isc = jnp.where(causal[None], isc, -jnp.inf)
        _, idx = lax.top_k(isc, topk)
        kg = gather(k, idx)
        vg = gather(v, idx)
        valid = idx <= tpos[None, :, None]
        s = jnp.einsum("bthd,btkd->bthk", qb, kg) * scale
        p = masked_softmax(s, valid[:, :, None, :])
        return jnp.einsum("bthk,btkd->bthd", p.astype(v.dtype), vg)

    return stack_blocks(lax.map(block, jnp.arange(S // Q_BLOCK)))


def fox_mixer(q, k, v, logf):
    B, S, H, D = q.shape
    cum = jnp.cumsum(logf.astype(jnp.float32), axis=1).transpose(0, 2, 1)
    kpos = jnp.arange(S)
    scale = D ** -0.5

    def block(i):
        t0 = i * Q_BLOCK
        tpos = t0 + jnp.arange(Q_BLOCK)
        qb = lax.dynamic_slice_in_dim(q, t0, Q_BLOCK, 1)
        cb = lax.dynamic_slice_in_dim(cum, t0, Q_BLOCK, 2)
        s = jnp.einsum("bthd,bshd->bhts", qb, k).astype(jnp.float32) * scale
        s = s + cb[:, :, :, None] - cum[:, :, None, :]
        causal = kpos[None, :] <= tpos[:, None]
        p = masked_softmax(s, causal)
        return jnp.einsum("bhts,bshd->bthd", p.astype(v.dtype), v)

    return stack_blocks(lax.map(block, jnp.arange(S // Q_BLOCK)))


def nsa_compress(kv, pe, w1, w2):
    B, S, G, D = kv.shape
    n_c = (S - NSA_CMP_LEN) // NSA_CMP_STRIDE + 1
    idx = jnp.arange(n_c)[:, None] * NSA_CMP_STRIDE + jnp.arange(NSA_CMP_LEN)[None, :]
    blocks = kv[:, idx] + pe[None, None, :, None, :]
    blocks = blocks.transpose(0, 1, 3, 2, 4).reshape(B, n_c, G, NSA_CMP_LEN * D)
    return jax.nn.silu(blocks @ w1) @ w2


def nsa_mixer(q, q_rot, kc, vc, ks, vs, kw, vw, gates):
    B, S, G, J, D = q.shape
    n_c = kc.shape[1]
    nsb = S // NSA_SEL_BLOCK
    n_sel = min(NSA_SEL_N, nsb)
    scale = D ** -0.5
    cstart = jnp.arange(n_c) * NSA_CMP_STRIDE
    cend = cstart + NSA_CMP_LEN - 1
    bstart = jnp.arange(nsb) * NSA_SEL_BLOCK
    overlap = ((cstart[:, None] < bstart[None, :] + NSA_SEL_BLOCK)
               & (cstart[:, None] + NSA_CMP_LEN > bstart[None, :])).astype(jnp.float32)
    ks_t = ks.transpose(0, 2, 1, 3)
    vs_t = vs.transpose(0, 2, 1, 3)
    pad = ((0, 0), (NSA_WINDOW, 0), (0, 0), (0, 0))
    kw_pad = jnp.pad(kw, pad)
    vw_pad = jnp.pad(vw, pad)
    gather = jax.vmap(jax.vmap(lambda kk, ii: kk[ii]))
    jb = jnp.arange(nsb)
    tok_off = jnp.arange(NSA_SEL_BLOCK)

    def block(i):
        T = NSA_Q_BLOCK
        t0 = i * T
        tpos = t0 + jnp.arange(T)
        qb = lax.dynamic_slice_in_dim(q, t0, T, 1)
        qrb = lax.dynamic_slice_in_dim(q_rot, t0, T, 1)
        gb = lax.dynamic_slice_in_dim(gates, t0, T, 1)
        s = jnp.einsum("btgjd,bcgd->bgjtc", qb, kc) * scale
        p_cmp = masked_softmax(s, cend[None, :] <= tpos[:, None])
        o_cmp = jnp.einsum("bgjtc,bcgd->btgjd", p_cmp.astype(vc.dtype), vc)
        score = jnp.einsum("bgjtc,cn->bgtn", p_cmp, overlap)
        cur = tpos // NSA_SEL_BLOCK
        forced = (jb[None, :] == 0) | (jb[None, :] == cur[:, None]) | (jb[None, :] == cur[:, None] - 1)
        future = bstart[None, :] > tpos[:, None]
        score = jnp.where(forced, jnp.inf, jnp.where(future, -jnp.inf, score))
        _, blk = lax.top_k(score, n_sel)
        tok = (blk[..., None] * NSA_SEL_BLOCK + tok_off).reshape(B, G, T, n_sel * NSA_SEL_BLOCK)
        kg = gather(ks_t, tok)
        vg = gather(vs_t, tok)
        s = jnp.einsum("btgjd,bgtnd->bgjtn", qrb, kg) * scale
        p = masked_softmax(s, (tok <= tpos[None, None, :, None])[:, :, None])
        o_slc = jnp.einsum("bgjtn,bgtnd->btgjd", p.astype(vs.dtype), vg)
        kwb = lax.dynamic_slice_in_dim(kw_pad, t0, NSA_WINDOW + T, 1)
        vwb = lax.dynamic_slice_in_dim(vw_pad, t0, NSA_WINDOW + T, 1)
        kpos = t0 - NSA_WINDOW + jnp.arange(NSA_WINDOW + T)
        wmask = ((kpos[None, :] <= tpos[:, None]) & (kpos[None, :] > tpos[:, None] - NSA_WINDOW)
                 & (kpos[None, :] >= 0))
        s = jnp.einsum("btgjd,bkgd->bgjtk", qrb, kwb) * scale
        p = masked_softmax(s, wmask)
        o_win = jnp.einsum("bgjtk,bkgd->btgjd", p.astype(vw.dtype), vwb)
        o = (gb[:, :, 0, :, :, None] * o_cmp + gb[:, :, 1, :, :, None] * o_slc
             + gb[:, :, 2, :, :, None] * o_win)
        return o.reshape(B, T, G * J, D)

    return stack_blocks(lax.map(block, jnp.arange(S // NSA_Q_BLOCK)))


def hybrid_layer(x, c, w_ada, b_ada, w_in, b_f, cmp_pe, cmp_w1, cmp_w2, w_out, ln_g, ln_b):
    B, S, _ = x.shape
    pos = jnp.arange(S)
    shift, scale, gate = jnp.split(c @ w_ada + b_ada, 3, axis=-1)
    u = x * (1.0 + scale[:, None, :]) + shift[:, None, :]
    h = split_cols(u @ w_in)

    dq = rope(h["dsa_q"].reshape(B, S, DSA_HEADS, HEAD_DIM), pos)
    dk = rope(h["dsa_k"][:, :, None, :], pos)[:, :, 0]
    iq = rope(h["idx_q"].reshape(B, S, IDX_HEADS, IDX_DIM), pos)
    ik = rope(h["idx_k"][:, :, None, :], pos)[:, :, 0]
    iw = h["idx_w"] * (IDX_HEADS ** -0.5)
    o_dsa = dsa_mixer(dq, dk, h["dsa_v"], iq, ik, iw)

    fq = h["fox_q"].reshape(B, S, FOX_HEADS, HEAD_DIM)
    fk = h["fox_k"].reshape(B, S, FOX_HEADS, HEAD_DIM)
    fv = h["fox_v"].reshape(B, S, FOX_HEADS, HEAD_DIM)
    logf = jax.nn.log_sigmoid(h["fox_f"].astype(jnp.float32) + b_f)
    o_fox = fox_mixer(fq, fk, fv, logf)

    kvs = lambda name: h[name].reshape(B, S, NSA_KV_GROUPS, HEAD_DIM)
    nq = h["nsa_q"].reshape(B, S, NSA_HEADS, HEAD_DIM)
    nq_rot = rope(nq, pos)
    kc = nsa_compress(kvs("nsa_kc"), cmp_pe[0], cmp_w1[0], cmp_w2[0])
    vc = nsa_compress(kvs("nsa_vc"), cmp_pe[1], cmp_w1[1], cmp_w2[1])
    ks = rope(kvs("nsa_ks"), pos)
    kw = rope(kvs("nsa_kw"), pos)
    gates = jax.nn.sigmoid(h["nsa_g"].reshape(B, S, 3, NSA_KV_GROUPS, NSA_HPG))
    grp = lambda t: t.reshape(B, S, NSA_KV_GROUPS, NSA_HPG, HEAD_DIM)
    o_nsa = nsa_mixer(grp(nq), grp(nq_rot), kc, vc, ks, kvs("nsa_vs"), kw, kvs("nsa_vw"), gates)

    mix = jnp.concatenate([o_dsa.reshape(B, S, DSA_W), o_fox.reshape(B, S, FOX_W),
                           o_nsa.reshape(B, S, NSA_W)], axis=-1)
    y = (mix * jax.nn.silu(h["gate"])) @ w_out
    return layer_norm(ALPHA * x + (1.0 + gate[:, None, :]) * y, ln_g, ln_b)


def setup_inputs(seed: int = 0) -> dict:
    key = jax.random.key(seed)
    ks = jax.random.split(key, 12)
    D = D_MODEL
    L = NSA_CMP_LEN
    nrm = lambda k, shape, s: jax.random.normal(k, shape, jnp.float32) * s
    return {
        "x": nrm(ks[0], (BATCH, SEQ, D), 1.0),
        "c": nrm(ks[1], (BATCH, D), 1.0),
        "w_ada": nrm(ks[2], (DEPTH, D, 3 * D), 0.1 * D ** -0.5),
        "b_ada": nrm(ks[3], (DEPTH, 3 * D), 0.01),
        "w_in": nrm(ks[4], (DEPTH, D, IN_WIDTH), D ** -0.5),
        "b_f": jax.random.uniform(ks[5], (DEPTH, FOX_HEADS), jnp.float32, 1.0, 4.0),
        "cmp_pe": nrm(ks[6], (DEPTH, 2, L, HEAD_DIM), 0.1),
        "cmp_w1": nrm(ks[7], (DEPTH, 2, L * HEAD_DIM, HEAD_DIM), (L * HEAD_DIM) ** -0.5),
        "cmp_w2": nrm(ks[8], (DEPTH, 2, HEAD_DIM, HEAD_DIM), HEAD_DIM ** -0.5),
        "w_out": nrm(ks[9], (DEPTH, MIX_W, D), BETA * MIX_W ** -0.5),
        "ln_g": 1.0 + nrm(ks[10], (DEPTH, D), 0.01),
        "ln_b": nrm(ks[11], (DEPTH, D), 0.01),
    }


def reference(x, c, w_ada, b_ada, w_in, b_f, cmp_pe, cmp_w1, cmp_w2, w_out, ln_g, ln_b):
    for l in range(DEPTH):
        x = hybrid_layer(x, c, w_ada[l], b_ada[l], w_in[l], b_f[l], cmp_pe[l], cmp_w1[l],
                         cmp_w2[l], w_out[l], ln_g[l], ln_b[l])
    return x
```

```python
import numpy as np
import ml_dtypes
from contextlib import ExitStack
import concourse.bass as bass
import concourse.mybir as mybir
from concourse.bass_utils import run_bass_kernel_spmd

F32 = mybir.dt.float32
BF16 = mybir.dt.bfloat16
AF = mybir.ActivationFunctionType
ALU = mybir.AluOpType
AX = mybir.AxisListType

P = 128
S = 2048
NT = 16
D = 1024
KC = 8
DEPTH = 2
HD = 64
ALPHA = (2.0 * DEPTH) ** 0.25
LN_EPS = 1e-5
NEG = -1.0e30
MASKV = 30000.0
NC_CMP = 127

_SPL = (("dsa_q", 256), ("dsa_k", 64), ("dsa_v", 64), ("idx_q", 256), ("idx_k", 32), ("idx_w", 8),
        ("fox_q", 384), ("fox_k", 384), ("fox_v", 384), ("fox_f", 6), ("nsa_q", 384),
        ("nsa_kc", 128), ("nsa_vc", 128), ("nsa_ks", 128), ("nsa_vs", 128), ("nsa_kw", 128),
        ("nsa_vw", 128), ("nsa_g", 18), ("gate", 1024))
OFF = {}
_o = 0
for _n, _w in _SPL:
    OFF[_n] = _o
    _o += _w
IN_WIDTH = _o


def _unit(name, h, dim=64):
    return np.arange(OFF[name] + h * dim, OFF[name] + (h + 1) * dim)


def _swap(cols):
    h = len(cols) // 2
    return np.concatenate([cols[h:], cols[:h]])


def _cat(*a):
    return np.concatenate(a)


def _build_units():
    U = {}
    for i in range(3):
        U[f"fq{i}"] = _cat(_unit("fox_q", 2 * i), _unit("fox_q", 2 * i + 1))
        U[f"fk{i}"] = _cat(_unit("fox_k", 2 * i), _unit("fox_k", 2 * i + 1))
        U[f"fv{i}"] = _cat(_unit("fox_v", 2 * i), _unit("fox_v", 2 * i + 1))
    U["ff"] = np.arange(OFF["fox_f"], OFF["fox_f"] + 6)
    for i in range(4):
        a = _unit("dsa_q", i)
        U[f"dq{i}"] = a
    for i in range(3):
        a = [_unit("idx_q", 3 * i + j, 32) for j in range(3) if 3 * i + j < 8]
        U[f"iq{i}"] = _cat(*a)
    k = _unit("dsa_k", 0)
    U["dk"] = k
    k = _unit("idx_k", 0, 32)
    U["ik"] = _cat(k, k, k)
    U["dvw"] = _cat(_unit("dsa_v", 0), np.arange(OFF["idx_w"], OFF["idx_w"] + 8), _unit("dsa_v", 0)[0:56])
    for i in range(3):
        a, b = _unit("nsa_q", i), _unit("nsa_q", i + 3)
        U[f"nq{i}"] = _cat(a, b)
    for nm in ("kc", "vc", "vs", "vw"):
        U["n" + nm] = np.arange(OFF["nsa_" + nm], OFF["nsa_" + nm] + 128)
    for nm in ("ks", "kw"):
        a, b = _unit("nsa_" + nm, 0), _unit("nsa_" + nm, 1)
        U["n" + nm] = _cat(a, b)
    U["ng"] = _cat(np.arange(OFF["nsa_g"], OFF["nsa_g"] + 18), np.arange(OFF["nsa_g"], OFF["nsa_g"] + 14))
    for i in range(8):
        U[f"g{i}"] = np.arange(OFF["gate"] + 128 * i, OFF["gate"] + 128 * (i + 1))
    return U


UNITS = _build_units()
UOFF = {}
_o = 0
for _k, _v in UNITS.items():
    UOFF[_k] = (_o, len(_v))
    _o += len(_v)
TOTC = _o
PERM = np.concatenate(list(UNITS.values()))


class Tr:
    def __init__(self, nc, es):
        self.nc = nc
        self.es = es
        self.sems = {}
        self.eng = {}
        for name, h in (("pe", nc.tensor), ("act", nc.scalar), ("dve", nc.vector),
                        ("pool", nc.gpsimd), ("sp", nc.sync)):
            sn = "e_" + name
            self.sems[sn] = es.enter_context(nc.semaphore(sn))
            self.eng[name] = dict(h=h, sn=sn, n=0, seen={})
        self.lw = {}
        self.rd = {}
        self.dcnt = {}
        self.hist = {}

    def _deps(self, reads, writes, nowaw=False):
        deps = []
        for k in reads:
            t = self.lw.get(k)
            if t:
                deps.append((t, True))
            if k.startswith("ps"):
                for t2 in self.rd.get(k, {}).items():
                    deps.append((t2, False))
        for k in writes:
            t = self.lw.get(k)
            if t and not nowaw:
                deps.append((t, False))
            for t2 in self.rd.get(k, {}).items():
                deps.append((t2, False))
        return deps

    def _emit_waits(self, e, deps, attach_ok=False):
        import itertools
        E = self.eng[e]
        need = {}
        for ((sn, v), raw) in deps:
            if sn == E["sn"] and e == "pe":
                continue
            if v > E["seen"].get(sn, 0) and v > need.get(sn, 0):
                need[sn] = v
        items = list(need.items())
        best = None
        perms = itertools.permutations(items) if len(items) <= 4 else [items]
        for order in perms:
            known = dict(E["seen"])
            res = []
            for sn, v in order:
                if known.get(sn, 0) >= v:
                    continue
                res.append((sn, v))
                known[sn] = v
                snap = self.hist.get((sn, v))
                if snap:
                    for k2, v2 in snap.items():
                        if v2 > known.get(k2, 0):
                            known[k2] = v2
            if best is None or len(res) < len(best[0]):
                best = (res, known)
        if best is None:
            return None
        items, known = best
        E["seen"] = known
        attach = None
        if attach_ok and items:
            attach = items.pop()
        for sn, v in items:
            E["h"].wait_ge(self.sems[sn], v)
        return attach

    def _record(self, tok, reads, writes):
        for k in reads:
            d = self.rd.setdefault(k, {})
            d[tok[0]] = max(d.get(tok[0], 0), tok[1])
        for k in writes:
            self.lw[k] = tok
            self.rd[k] = {}

    def op(self, e, fn, reads=(), writes=()):
        attach = self._emit_waits(e, self._deps(reads, writes), attach_ok=True)
        E = self.eng[e]
        ins = fn()
        if attach is not None:
            ins._wait_ge(self.sems[attach[0]], attach[1])
        E["n"] += 1
        ins.then_inc(self.sems[E["sn"]], 1)
        self.hist[(E["sn"], E["n"])] = dict(E["seen"])
        self._record((E["sn"], E["n"]), reads, writes)

    def dma(self, fn, semkey, reads=(), writes=(), nowaw=False):
        self._emit_waits("sp", self._deps(reads, writes, nowaw))
        sn = "d_" + semkey
        if sn not in self.sems:
            self.sems[sn] = self.es.enter_context(self.nc.semaphore(sn))
            self.dcnt[sn] = 0
        ins = fn()
        self.dcnt[sn] += 16
        ins.then_inc(self.sems[sn], 16)
        self._record((sn, self.dcnt[sn]), reads, writes)

    def barrier(self):
        toks = [(E["sn"], E["n"]) for E in self.eng.values() if E["n"] > 0]
        toks += [(sn, c) for sn, c in self.dcnt.items() if c > 0]
        for e in self.eng:
            self._emit_waits(e, [(t, True) for t in toks if t[0] != self.eng[e]["sn"]])

    def final(self):
        toks = [(sn, c) for sn, c in self.dcnt.items() if c > 0]
        toks += [(E["sn"], E["n"]) for n_, E in self.eng.items() if E["n"] > 0 and n_ != "sp"]
        self._emit_waits("sp", [(t, True) for t in toks])


class _Stop(Exception):
    pass


STOP = [None]


_DUMP = [None]


def _chk(name):
    if STOP[0] == name:
        if _DUMP[0] is not None:
            _DUMP[0]()
        raise _Stop()


def build_program(n_layers=DEPTH, dbg=()):
    nc = bass.Bass("TRN2", target_bir_lowering=False)
    es = ExitStack()
    T = Tr(nc, es)
    stopped = False
    try:
        _build_body(nc, es, T, n_layers, dbg)
    except _Stop:
        stopped = True
    T.final()
    print("instr counts:", {k: v["n"] for k, v in T.eng.items()}, "nsems", len(T.sems))
    if not stopped:
        es.close()
    return nc


def _build_body(nc, es, T, n_layers, dbg):

    _uid = [0]

    def sbt(name, shape, dt):
        _uid[0] += 1
        return nc.sbuf_tensor(f"{name}_{_uid[0]}", list(shape), dt)

    def dram(name, shape, dt=F32, kind="ExternalInput"):
        return nc.dram_tensor(name, list(shape), dt, kind=kind).ap()

    x_d = dram("x", [S, D])
    cbc_d = dram("c_bc", [P, KC, P])
    ccol_d = dram("c_col", [P, KC])
    wada_d = dram("w_ada", [DEPTH, D, 3 * D])
    bada_d = dram("b_ada", [DEPTH, 3 * D])
    bgate_d = dram("b_gate_bc", [DEPTH, P, D])
    wperm_d = dram("w_perm", [DEPTH, D, TOTC])
    bf_d = dram("b_f", [DEPTH, P, 8])
    peT_d = dram("cmp_peT", [DEPTH, 2, P, 32])
    w1_d = dram("cmp_w1r", [DEPTH, 2, P, 32 * 64])
    w2_d = dram("cmp_w2r", [DEPTH, 2, P, 64])
    wout_d = dram("w_out", [DEPTH, D, D])
    lng_d = dram("ln_g_bc", [DEPTH, P, D])
    lnb_d = dram("ln_b_bc", [DEPTH, P, D])
    cos64_d = dram("cos64", [P, S], BF16)
    sin64_d = dram("sin64", [P, S], BF16)
    cos32_d = dram("cos32", [P, S], BF16)
    sin32_d = dram("sin32", [P, S], BF16)
    cst_d = dram("cst", [P, 10, P])
    cstb_d = dram("cst_b", [P, 10, P], BF16)
    cmpm_d = dram("cmp_mask", [NT, P, P])
    fb_d = dram("sel_fb", [NT, P, 32])
    out_d = dram("out", [S, D], kind="ExternalOutput")
    x1_d = dram("x1_scratch", [S, D], kind="Internal")
    dbg_d = {}
    if "mix" in dbg:
        dbg_d["mix"] = dram("dbg_mix", [S, D], BF16, kind="ExternalOutput")

    def sb(name, shape, dt=F32):
        return es.enter_context(sbt(name, list(shape), dt))

    def pst(name):
        return es.enter_context(nc.psum_tensor(name, [P, 512], F32))

    PS = [pst(f"ps{i}") for i in range(8)]
    pskey = [f"ps{i}" for i in range(8)]

    uT = sb("uT", [P, KC, S], BF16)
    mix = sb("mix", [P, NT, D], BF16)
    cstf = sb("cstf", [P, 10, P], F32)
    ident_f = cstf[:, 0, :]
    causneg = cstf[:, 3, :]
    cstb = sb("cstb", [P, 10, P], BF16)
    ident_b = cstb[:, 0, :]
    caus01 = cstb[:, 1, :]
    band01 = cstb[:, 2, :]
    identbig = cstb[:, 4, :]
    ones_b = cstb[:, 5, :]
    perm64 = cstb[:, 7, :]
    perm32 = cstb[:, 8, :]
    shiftm = cstb[:, 9, :]
    cos64 = sb("cos64s", [P, S], BF16)
    sin64 = sb("sin64s", [P, S], BF16)
    cos32 = sb("cos32s", [P, S], BF16)
    sin32 = sb("sin32s", [P, S], BF16)
    modT = sb("modT", [P, DEPTH, 24], F32)
    g1bc = sb("g1bc", [P, DEPTH, D], F32)
    ccol = sb("ccol", [P, KC], F32)

    T.dma(lambda: nc.sync.dma_start(out=cstf[:], in_=cst_d[:, :, :]), "cstf", writes=["cstf"])
    T.dma(lambda: nc.sync.dma_start(out=cstb[:], in_=cstb_d[:, :, :]), "cstb", writes=["cstb"])
    T.dma(lambda: nc.sync.dma_start(out=ccol[:], in_=ccol_d[:, :]), "ccol", writes=["ccol"])

    xin_cm = sbt("xin", [P, NT, D], F32)
    xin = xin_cm.__enter__()
    for q4 in range(4):
        T.dma(lambda q4=q4: nc.sync.dma_start(out=xin[:, 4 * q4:4 * q4 + 4, :],
                                              in_=x_d[q4 * 512:(q4 + 1) * 512, :].rearrange("(t p) d -> p t d", p=P)),
              f"xin{q4}", writes=[f"xin{q4}"])

    for src, dst, nm in ((cos64_d, cos64, "cos64"), (sin64_d, sin64, "sin64"), (cos32_d, cos32, "cos32"), (sin32_d, sin32, "sin32")):
        T.dma(lambda src=src, dst=dst: nc.sync.dma_start(out=dst[:], in_=src[:, :]), nm, writes=[nm])

    with ExitStack() as es2:
        cbc = es2.enter_context(sbt("cbc", [P, KC, P], F32))
        wst = es2.enter_context(sbt("wadast", [P, 2, KC, 512], F32))
        brow = es2.enter_context(sbt("brow", [1, 512], F32))
        mrow = es2.enter_context(sbt("mrow", [1, 512], F32))
        bg = es2.enter_context(sbt("bgate", [P, D], F32))
        T.dma(lambda: nc.sync.dma_start(out=cbc[:], in_=cbc_d[:, :, :]), "cbc", writes=["cbc"])
        blk = 0
        for l in range(n_layers):
            T.dma(lambda l=l: nc.sync.dma_start(out=bg[:], in_=bgate_d[l, :, :]), "bg", writes=["bg"])
            for b6 in range(6):
                slot = blk % 2
                blk += 1
                src = wada_d[l, :, b6 * 512:(b6 + 1) * 512].rearrange("(kc k) n -> k kc n", k=P)
                T.dma(lambda src=src, slot=slot: nc.sync.dma_start(out=wst[:, slot, :, :], in_=src),
                      f"wst{slot}", writes=[f"wst{slot}"])
                if b6 < 4:
                    for kc in range(KC):
                        T.op("pe", lambda slot=slot, kc=kc: nc.tensor.matmul(
                            PS[3][0:1, :], lhsT=ccol[:, kc:kc + 1], rhs=wst[:, slot, kc, :],
                            start=(kc == 0), stop=(kc == KC - 1)),
                            reads=[f"wst{slot}", "ccol"], writes=["ps3"])
                    T.dma(lambda l=l, b6=b6: nc.sync.dma_start(out=brow[0:1, :], in_=bada_d[l:l + 1, b6 * 512:(b6 + 1) * 512]),
                          "brow", writes=["brow"])
                    T.op("dve", lambda: nc.vector.tensor_tensor(out=mrow[0:1, :], in0=PS[3][0:1, :], in1=brow[0:1, :], op=ALU.add),
                         reads=["ps3", "brow"], writes=["mrow"])
                    for f in range(4):
                        j = b6 * 4 + f
                        T.op("pe", lambda f=f, j=j: nc.tensor.matmul(
                            PS[0][:, j:j + 1], lhsT=mrow[0:1, f * P:(f + 1) * P], rhs=cstf[0:1, 0, 0:1],
                            start=True, stop=True),
                            reads=["mrow", "cstf"], writes=["ps0"])
                else:
                    g = b6 - 4
                    for kc in range(KC):
                        T.op("pe", lambda slot=slot, kc=kc, g=g: nc.tensor.matmul(
                            PS[1 + g][:, :], lhsT=cbc[:, kc, :], rhs=wst[:, slot, kc, :],
                            start=(kc == 0), stop=(kc == KC - 1)),
                            reads=[f"wst{slot}", "cbc"], writes=[pskey[1 + g]])
                    T.op("dve", lambda l=l, g=g: nc.vector.tensor_tensor(
                        out=g1bc[:, l, g * 512:(g + 1) * 512], in0=PS[1 + g][:, :],
                        in1=bg[:, g * 512:(g + 1) * 512], op=ALU.add),
                        reads=[pskey[1 + g], "bg"], writes=["g1bc"])
            T.op("dve", lambda l=l: nc.vector.tensor_copy(out=modT[:, l, 0:16], in_=PS[0][:, 0:16]),
                 reads=["ps0"], writes=["modT"])
            T.op("dve", lambda l=l: nc.vector.tensor_scalar(out=modT[:, l, 8:16], in0=modT[:, l, 8:16],
                                                            scalar1=1.0, scalar2=None, op0=ALU.add),
                 reads=["modT"], writes=["modT"])
            T.op("dve", lambda l=l: nc.vector.tensor_scalar(out=g1bc[:, l, :], in0=g1bc[:, l, :],
                                                            scalar1=1.0, scalar2=None, op0=ALU.add),
                 reads=["g1bc"], writes=["g1bc"])
        T.barrier()
    _chk("mod")

    def _dump():
        if "mix" in dbg:
            T.barrier()
            for i in range(NT):
                T.dma(lambda i=i: nc.sync.dma_start(out=dbg_d["mix"][i * P:(i + 1) * P, :], in_=mix[:, i, :]),
                      "store", reads=[f"mix{i}"], writes=[f"dbgmix{i}"])
    _DUMP[0] = _dump

    def make_uT_tile(src_tile, src_key, l, it, bank0=2):
        for half in range(2):
            bank = bank0 + half
            for q in range(4):
                kc = half * 4 + q
                T.op("pe", lambda kc=kc, q=q, bank=bank: nc.tensor.transpose(
                    PS[bank][:, q * P:(q + 1) * P], src_tile[:, kc * P:(kc + 1) * P], ident_f),
                    reads=[src_key, "cstf"], writes=[pskey[bank]])
            for q in range(4):
                kc = half * 4 + q
                if q % 2 == 0 or bank0 != 2:
                    T.op("act", lambda kc=kc, q=q, bank=bank: nc.scalar.activation(
                        out=uT[:, kc, it * P:(it + 1) * P], in_=PS[bank][:, q * P:(q + 1) * P],
                        func=AF.Identity, bias=modT[:, l, kc:kc + 1], scale=modT[:, l, 8 + kc:9 + kc]),
                        reads=[pskey[bank], "modT"], writes=[f"uT{it}"])
                else:
                    T.op("dve", lambda kc=kc, q=q, bank=bank: nc.vector.tensor_scalar(
                        out=uT[:, kc, it * P:(it + 1) * P], in0=PS[bank][:, q * P:(q + 1) * P],
                        scalar1=modT[:, l, 8 + kc:9 + kc], scalar2=modT[:, l, kc:kc + 1], op0=ALU.mult, op1=ALU.add),
                        reads=[pskey[bank], "modT"], writes=[f"uT{it}"])

    wslot_ctr = [0]

    wplan = []
    wready = {}

    def plan_w(names):
        wplan[:] = list(names)

    NSLOT = 2
    wdma = {}
    wcast_ctr = [0]

    def _dma_w(l, wst, name):
        slot = wslot_ctr[0] % NSLOT
        wslot_ctr[0] += 1
        o0, tot = UOFF[name]
        assert tot <= 128
        src = wperm_d[l, :, o0:o0 + tot].rearrange("(kc k) n -> k kc n", k=P)
        T.dma(lambda: nc.sync.dma_start(out=wst[:, slot, :, 0:tot], in_=src), f"wst{slot}", writes=[f"wst{slot}"])
        wdma[name] = (slot, tot)

    def _cast_w(wst, wbf, name):
        slot, tot = wdma.pop(name)
        bs = wcast_ctr[0] % 2
        wcast_ctr[0] += 1
        T.op("act", lambda: nc.scalar.copy(out=wbf[:, bs, :, 0:tot], in_=wst[:, slot, :, 0:tot]),
             reads=[f"wst{slot}"], writes=[f"wbf{bs}"])
        wready[name] = (bs, f"wbf{bs}", {name: (0, tot)})

    def load_w(l, wst, wbf, names):
        name = names[0]
        assert len(names) == 1
        if name not in wready:
            if name not in wdma:
                _dma_w(l, wst, name)
            _cast_w(wst, wbf, name)
        res = wready.pop(name)
        if wplan and wplan[0] == name:
            wplan.pop(0)
        for k_, nm in enumerate(wplan[:2]):
            if nm not in wready and nm not in wdma:
                _dma_w(l, wst, nm)
        if wplan and wplan[0] in wdma:
            _cast_w(wst, wbf, wplan[0])
        return res

    pbank = [0]

    def projT(wbf, slot, wkey, off, ncols, tc, bank):
        assert off == 0
        for kc in range(KC):
            T.op("pe", lambda kc=kc: nc.tensor.matmul(
                PS[bank][:, :], lhsT=wbf[:, slot, kc, :],
                rhs=uT[:, kc, tc * 512:(tc + 1) * 512], start=(kc == 0), stop=(kc == KC - 1)),
                reads=[wkey] + [f"uT{4 * tc + i}" for i in range(4)], writes=[pskey[bank]])

    def proj_plain(l, wst, wbf, name, dst, dkey):
        slot, wkey, offs = load_w(l, wst, wbf, [name])
        off, ncols = offs[name]
        for tc in range(4):
            bank = pbank[0] % 4
            pbank[0] += 1
            projT(wbf, slot, wkey, off, ncols, tc, bank)
            T.op("act", lambda tc=tc, bank=bank: nc.scalar.copy(
                out=dst[0:ncols, tc * 512:(tc + 1) * 512], in_=PS[bank][0:ncols, :]),
                reads=[pskey[bank]], writes=[dkey])

    def proj_rope(l, wst, wbf, name, dst, dkey, cosT, sinT, tkeys, tmp, raw_dst=None, raw_key=None, halves=False):
        slot, wkey, offs = load_w(l, wst, wbf, [name])
        perm = perm64 if tkeys[0] == "cos64" else perm32
        n_ = offs[name][1]
        banks = {}

        def stage1(tc):
            b0 = pbank[0] % 4
            b1 = (pbank[0] + 1) % 4
            pbank[0] += 2
            banks[tc] = (b0, b1)
            projT(wbf, slot, wkey, offs[name][0], n_, tc, b0)
            xs = tc % 2
            T.op("act", lambda: nc.scalar.copy(out=XB[:, xs, :], in_=PS[b0][:, :]),
                 reads=[pskey[b0]], writes=[f"XB{xs}"])

        def stage2(tc):
            b0, b1 = banks[tc]
            xs = tc % 2
            sl = slice(tc * 512, (tc + 1) * 512)
            T.op("pe", lambda: nc.tensor.matmul(PS[b1][:, :], lhsT=perm, rhs=XB[:, xs, :], start=True, stop=True),
                 reads=[f"XB{xs}", "cstb"], writes=[pskey[b1]])
            if raw_dst is not None:
                T.op("act", lambda: nc.scalar.copy(out=raw_dst[0:n_, sl], in_=XB[0:n_, xs, :]),
                     reads=[f"XB{xs}"], writes=[raw_key])
            T.op("dve", lambda: nc.vector.tensor_tensor(out=tmp[0:n_, 0, :], in0=PS[b0][0:n_, :], in1=cosT[0:n_, sl], op=ALU.mult),
                 reads=[pskey[b0], tkeys[0]], writes=["ropetmp0"])
            T.op("dve", lambda: nc.vector.tensor_tensor(out=tmp[0:n_, 1, :], in0=PS[b1][0:n_, :], in1=sinT[0:n_, sl], op=ALU.mult),
                 reads=[pskey[b1], tkeys[1]], writes=["ropetmp1"])
            if halves:
                T.op("dve", lambda: nc.vector.tensor_tensor(out=dst[0:64, 0, sl], in0=tmp[0:64, 0, :], in1=tmp[0:64, 1, :], op=ALU.add),
                     reads=["ropetmp0", "ropetmp1"], writes=[dkey])
                T.op("dve", lambda: nc.vector.tensor_tensor(out=dst[64:128, 1, sl], in0=tmp[64:128, 0, :], in1=tmp[64:128, 1, :], op=ALU.add),
                     reads=["ropetmp0", "ropetmp1"], writes=[dkey])
            else:
                T.op("dve", lambda: nc.vector.tensor_tensor(out=dst[0:n_, sl], in0=tmp[0:n_, 0, :], in1=tmp[0:n_, 1, :], op=ALU.add),
                     reads=["ropetmp0", "ropetmp1"], writes=[dkey])

        stage1(0)
        for tc in range(4):
            if tc + 1 < 4:
                stage1(tc + 1)
            stage2(tc)

    def proj_tok(l, wst, wbf, name, evac):
        slot, wkey, offs = load_w(l, wst, wbf, [name])
        off, ncols = offs[name]
        for g in range(4):
            bank = pbank[0] % 4
            pbank[0] += 1
            for q in range(4):
                it = 4 * g + q
                for kc in range(KC):
                    T.op("pe", lambda kc=kc, q=q, it=it: nc.tensor.matmul(
                        PS[bank][:, q * P:q * P + ncols], lhsT=uT[:, kc, it * P:(it + 1) * P],
                        rhs=wbf[:, slot, kc, off:off + ncols], start=(kc == 0), stop=(kc == KC - 1)),
                        reads=[wkey, f"uT{it}"], writes=[pskey[bank]])
            view = PS[bank][:, :].rearrange("p (q c) -> p q c", q=4)[:, :, 0:ncols]
            evac(g, view, pskey[bank])

    acc_ctr = [0]
    sb_ctr = [0]

    def attn_core(PT, i, ktiles, qk, extra, vext, vkey, finish):
        for _ in attn_core_g(PT, i, ktiles, qk, extra, vext, vkey, finish):
            pass

    def interleave(a, b):
        a = iter(a)
        b = iter(b)
        da = db = False
        while not (da and db):
            if not da:
                try:
                    next(a)
                except StopIteration:
                    da = True
            if not db:
                try:
                    next(b)
                except StopIteration:
                    db = True

    def _ov(out_ap, rhs):
        if len(rhs.shape) == 3:
            return out_ap.rearrange("p (h t) -> p h t", h=rhs.shape[1])
        return out_ap

    pending_fin = []

    def flush_fin():
        while pending_fin:
            pending_fin.pop(0)()

    def attn_wide_g(PT, ncol, ktl, qk, extra, vext, vkey, finish):
        abank = 6 + (acc_ctr[0] % 2)
        acc_ctr[0] += 1
        nk = len(ktl)
        slots = []

        def scores(n):
            j, c0 = ktl[n]
            sbank = 3 + (sb_ctr[0] % 3)
            ptslot = sb_ctr[0] % 3
            sb_ctr[0] += 1
            slots.append(ptslot)
            lhsT, rhs, keys = qk(j, c0)
            ex = extra(j, c0)
            T.op("pe", lambda: nc.tensor.matmul(_ov(PS[sbank][:, c0:ncol], rhs), lhsT=lhsT, rhs=rhs, start=True,
                                                stop=(len(ex) == 0)),
                 reads=keys, writes=[pskey[sbank]])
            for n_, (l2, r2, k2, (e0, e1)) in enumerate(ex):
                T.op("pe", lambda: nc.tensor.matmul(_ov(PS[sbank][:, e0:e1], r2), lhsT=l2, rhs=r2, start=False,
                                                    stop=(n_ == len(ex) - 1)),
                     reads=k2, writes=[pskey[sbank]])
            T.op("act", lambda: nc.scalar.activation(out=PT[:, ptslot, c0:ncol], in_=PS[sbank][:, c0:ncol], func=AF.Exp, scale=0.125),
                 reads=[pskey[sbank]], writes=[f"PT{ptslot}"])

        def pv(n):
            j, c0 = ktl[n]
            ptslot = slots[n]
            T.op("pe", lambda: nc.tensor.matmul(PS[abank][0:65, c0:ncol], lhsT=vext(j), rhs=PT[:, ptslot, c0:ncol],
                                                start=(n == 0), stop=(n == nk - 1)),
                 reads=[f"PT{ptslot}", vkey], writes=[pskey[abank]])

        scores(0)
        if nk > 1:
            scores(1)
        for n in range(nk):
            if n + 2 < nk:
                scores(n + 2)
            pv(n)
            if n == 0:
                flush_fin()
            yield
        finish(abank)
        yield

    ot_ctr = [0]
    fw_banks = [[0, 1, 2]]

    def finish_wide(abank, ncol, nb, emit_block):
        os_ = ot_ctr[0] % 2
        c = ot_ctr[0] % 8
        ot_ctr[0] += 1
        T.op("dve", lambda: nc.vector.tensor_copy(out=OTs[0:65, os_, 0:ncol], in_=PS[abank][0:65, 0:ncol]),
             reads=[pskey[abank]], writes=[f"OTs{os_}"])

        def part2():
            tb = fw_banks[0][pbank[0] % len(fw_banks[0])]
            pbank[0] += 1
            for b in range(nb):
                T.op("pe", lambda: nc.tensor.transpose(PS[tb][:, b * 65:(b + 1) * 65], OTs[0:65, os_, b * P:(b + 1) * P],
                                                       ident_f[0:65, 0:65]),
                     reads=[f"OTs{os_}", "cstf"], writes=[pskey[tb]])
            T.op("dve", lambda: nc.vector.reciprocal(out=rsm[:, 4 * c:4 * c + nb], in_=PS[tb][:, 64:64 + 65 * (nb - 1) + 1:65]),
                 reads=[pskey[tb]], writes=[f"rsm{c}"])
            for b in range(nb):
                emit_block(b, PS[tb][:, b * 65:b * 65 + 64], pskey[tb], rsm[:, 4 * c + b:4 * c + b + 1], f"rsm{c}", 4 * c + b)
        pending_fin.append(part2)

    def attn_core_g(PT, i, ktiles, qk, extra, vext, vkey, finish):
        abank = 6 + (acc_ctr[0] % 2)
        acc_ctr[0] += 1
        nk = len(ktiles)
        for g0 in range(0, nk, 4):
            grp = ktiles[g0:g0 + 4]
            sbank = 4 + (sb_ctr[0] % 2)
            ptslot = sb_ctr[0] % 2
            sb_ctr[0] += 1
            for q, j in enumerate(grp):
                lhsT, rhs, keys = qk(j)
                ex = extra(j)
                T.op("pe", lambda lhsT=lhsT, rhs=rhs, q=q, ex=ex: nc.tensor.matmul(
                    PS[sbank][:, q * P:(q + 1) * P], lhsT=lhsT, rhs=rhs, start=True, stop=(len(ex) == 0)),
                    reads=keys, writes=[pskey[sbank]])
                for n_, (l2, r2, k2) in enumerate(ex):
                    T.op("pe", lambda l2=l2, r2=r2, q=q, n_=n_, ex=ex: nc.tensor.matmul(
                        PS[sbank][:, q * P:(q + 1) * P], lhsT=l2, rhs=r2, start=False, stop=(n_ == len(ex) - 1)),
                        reads=k2, writes=[pskey[sbank]])
            w = len(grp) * P
            T.op("act", lambda w=w, ptslot=ptslot, sbank=sbank: nc.scalar.activation(
                out=PT[:, ptslot, 0:w], in_=PS[sbank][:, 0:w], func=AF.Exp, scale=0.125),
                reads=[pskey[sbank]], writes=[f"PT{ptslot}"])
            for q, j in enumerate(grp):
                first = (g0 == 0 and q == 0)
                last = (g0 + q == nk - 1)
                T.op("pe", lambda q=q, j=j, first=first, last=last, ptslot=ptslot: nc.tensor.matmul(
                    PS[abank][:, 0:65], lhsT=PT[:, ptslot, q * P:(q + 1) * P], rhs=vext(j),
                    start=first, stop=last),
                    reads=[f"PT{ptslot}", vkey], writes=[pskey[abank]])
            yield
        finish(PS[abank][:, 0:65], pskey[abank])
        yield

    xt_ctr = [0]
    for l in range(n_layers):
        if l == 0:
            for it in range(NT):
                make_uT_tile(xin[:, it, :], f"xin{it // 4}", 0, it)
            T.barrier()
            xin_cm.__exit__(None, None, None)
            _chk("uT")

        with ExitStack() as esl:
            wst = esl.enter_context(sbt("wst", [P, 2, KC, 128], F32))
            wbf = esl.enter_context(sbt("wbf", [P, 2, KC, 128], BF16))
            T.op("pool", lambda: nc.gpsimd.memset(wbf[:], 0.0), writes=["wbf0", "wbf1"])
            PT = esl.enter_context(sbt("PT", [P, 3, 512], BF16))
            rsm = esl.enter_context(sbt("rsm", [P, 32], F32))
            OTs = esl.enter_context(sbt("OTs", [P, 2, 512], F32))
            XB = esl.enter_context(sbt("XB", [P, 2, 512], BF16))
            ID4 = esl.enter_context(sbt("ID4", [P, 4, P], BF16))
            for q4 in range(4):
                T.op("pool", lambda: nc.gpsimd.tensor_copy(out=ID4[:, q4, :], in_=identbig), reads=["cstb"], writes=["ID4"])

            with ExitStack() as espf:
              FQp = espf.enter_context(sbt("FQp", [P, 6, S], BF16))
              FKp = espf.enter_context(sbt("FKp", [P, 6, S], BF16))
              if True:
                esp = espf
                CC = esp.enter_context(sbt("CC", [6, 2, S], F32))
                SP3 = esp.enter_context(sbt("SP3", [6, 3, S], BF16))
                bfn = esp.enter_context(sbt("bfn", [P, 8], F32))
                FV = esp.enter_context(sbt("FV", [P, NT, 6, 65], BF16))
                plan_w(["ff"] + [f"{p}{i}" for i in range(3) for p in ("fq", "fk", "fv")])
                T.op("pool", lambda: nc.gpsimd.memset(FKp[64:128, :, :], -1.0), writes=["FKpa"])
                T.op("pool", lambda: nc.gpsimd.memset(FQp[64:128, :, :], 0.0), writes=["FQpa"])
                T.op("pool", lambda: nc.gpsimd.memset(FQp[64:67, :, :], 1.0), writes=["FQpa"])
                _chk("f0")
                T.dma(lambda: nc.sync.dma_start(out=bfn[:], in_=bf_d[l, :, :]), "bfn", writes=["bfn"])
                _chk("f1")
                T.op("dve", lambda: nc.vector.tensor_scalar(out=bfn[:], in0=bfn[:], scalar1=-1.0, scalar2=None, op0=ALU.mult),
                     reads=["bfn"], writes=["bfn"])
                _chk("fa")
                slot, wkey, offs = load_w(l, wst, wbf, ["ff"])
                _chk("fb")
                for tc in range(4):
                    bank = pbank[0] % 4
                    pbank[0] += 1
                    projT(wbf, slot, wkey, offs["ff"][0], 6, tc, bank)
                    if tc == 0:
                        _chk("fc")
                    sl = slice(tc * 512, (tc + 1) * 512)
                    T.op("act", lambda: nc.scalar.activation(out=CC[:, 0, sl], in_=PS[bank][0:6, :], func=AF.Exp,
                                                            bias=bfn[0:6, 0:1], scale=-1.0),
                         reads=[pskey[bank], "bfn"], writes=["CC0"])
                    if tc == 0:
                        _chk("fd")
                    T.op("act", lambda: nc.scalar.activation(out=CC[:, 0, sl], in_=CC[:, 0, sl], func=AF.Ln,
                                                            bias=1.0, scale=1.0),
                         reads=["CC0"], writes=["CC0"])
                _chk("fp1")
                T.op("dve", lambda: nc.vector.tensor_tensor_scan(out=CC[:, 1, :], data0=CC[:, 0, :], data1=CC[:, 0, :],
                                                                 initial=0.0, op0=ALU.add, op1=ALU.max),
                     reads=["CC0"], writes=["CC1"])
                T.op("dve", lambda: nc.vector.tensor_scalar(out=CC[:, 1, :], in0=CC[:, 1, :], scalar1=8.0, scalar2=None, op0=ALU.mult),
                     reads=["CC1"], writes=["CC1"])
                cur = 1
                for r in range(3):
                    T.op("dve", lambda: nc.vector.tensor_copy(out=SP3[:, r, :], in_=CC[:, cur, :]),
                         reads=[f"CC{cur}"], writes=[f"SP{r}"])
                    if r < 2:
                        nxt = 1 - cur
                        T.op("dve", lambda: nc.vector.tensor_tensor(
                            out=CC[:, nxt, :], in0=CC[:, cur, :], in1=SP3[:, r, :], op=ALU.subtract),
                            reads=[f"CC{cur}", f"SP{r}"], writes=[f"CC{nxt}"])
                        cur = nxt
                _chk("fp2")
                _chk("fp3")
                _chk("foxpre")
                T.op("pool", lambda: nc.gpsimd.memset(FV[:, :, :, 64:65], 1.0), writes=["FV"])

                def proj_pair(name, dstp, dkey, i):
                    slot, wkey, offs = load_w(l, wst, wbf, [name])
                    off, ncols = offs[name]
                    banks = {}

                    def st1(tc):
                        b0 = pbank[0] % 4
                        b1 = (pbank[0] + 1) % 4
                        pbank[0] += 2
                        banks[tc] = (b0, b1)
                        projT(wbf, slot, wkey, off, ncols, tc, b0)
                        xs = tc % 2
                        T.op("act", lambda: nc.scalar.copy(out=XB[:, xs, :], in_=PS[b0][:, :]),
                             reads=[pskey[b0]], writes=[f"XB{xs}"])

                    def st2(tc):
                        b0, b1 = banks[tc]
                        sl = slice(tc * 512, (tc + 1) * 512)
                        xs = tc % 2
                        T.op("pe", lambda: nc.tensor.matmul(PS[b1][0:64, :], lhsT=shiftm[:, 0:64], rhs=XB[:, xs, :], start=True, stop=True),
                             reads=[f"XB{xs}", "cstb"], writes=[pskey[b1]])
                        T.op("dve", lambda: nc.vector.tensor_copy(out=dstp[0:64, 2 * i, sl], in_=XB[0:64, xs, :]),
                             reads=[f"XB{xs}"], writes=[dkey])
                        T.op("act", lambda: nc.scalar.copy(out=dstp[0:64, 2 * i + 1, sl], in_=PS[b1][0:64, :]),
                             reads=[pskey[b1]], writes=[dkey])

                    st1(0)
                    for tc in range(4):
                        if tc + 1 < 4:
                            st1(tc + 1)
                        st2(tc)

                for i in range(3):
                    proj_pair(f"fq{i}", FQp, "FQp", i)
                    proj_pair(f"fk{i}", FKp, "FKp", i)

                    def ev(g, view, pk, i=i):
                        T.op("act", lambda: nc.scalar.copy(
                            out=FV[:, 4 * g:4 * g + 4, 2 * i:2 * i + 2, 0:64],
                            in_=view.rearrange("p q (h d) -> p q h d", h=2)),
                            reads=[pk], writes=["FV"])
                    proj_tok(l, wst, wbf, f"fv{i}", ev)
                for h in range(6):
                    for r in range(3):
                        T.dma(lambda: nc.sync.dma_start(out=FKp[64 + r:65 + r, h, :], in_=SP3[h:h + 1, r, :]),
                              "FKpa", reads=[f"SP{r}"], writes=["FKpa"], nowaw=(h + r > 0))
                        T.dma(lambda: nc.sync.dma_start(out=FQp[67 + r:68 + r, h, :], in_=SP3[h:h + 1, r, :]),
                              "FQpa", reads=[f"SP{r}"], writes=["FQpa"], nowaw=(h + r > 0))
                for c4 in range(4):
                    for h in range(6):
                        hp = 64 * (h % 2)
                        pb = 32 * (h % 3)
                        ktl = [(j, 0) for j in range(4 * c4)] + [(4 * c4 + jl, P * jl) for jl in range(4)]
                        q0 = c4 * 512

                        def qk(j, c0, h=h, q0=q0):
                            return (FKp[:, h, j * P:(j + 1) * P], FQp[:, h, q0 + c0:q0 + 512],
                                    ["FKp", "FQp", "FKpa", "FQpa"])

                        def extra(j, c0, c4=c4):
                            if j >= 4 * c4:
                                return [(caus01, identbig, ["cstb"], (c0, c0 + P))]
                            return []

                        def fin(abank, h=h, c4=c4):
                            def blk(b, src, skey, rs, rkey, rcol=None):
                                T.op("dve", lambda: nc.vector.tensor_scalar(
                                    out=mix[:, 4 * c4 + b, 256 + 64 * h:256 + 64 * (h + 1)], in0=src,
                                    scalar1=rs, scalar2=None, op0=ALU.mult),
                                    reads=[skey, rkey], writes=[f"mix{4 * c4 + b}"])
                            finish_wide(abank, 512, 4, blk)
                        for _ in attn_wide_g(PT, 512, ktl, qk, extra, lambda j, h=h: FV[:, j, h, :], "FV", fin):
                            pass
                    if c4 == 0:
                        _chk("fox0")
                flush_fin()
                T.barrier()
            _chk("fox")

            with ExitStack() as esp:
                DQ = esp.enter_context(sbt("DQ", [P, 4, S], BF16))
                DK = esp.enter_context(sbt("DK", [P, S], BF16))
                IQ = esp.enter_context(sbt("IQ", [P, 3, S], BF16))
                IK = esp.enter_context(sbt("IK", [P, 3, S], BF16))
                DV = esp.enter_context(sbt("DV", [P, NT, 65], BF16))
                IW = esp.enter_context(sbt("IW", [P, NT, 3, 8], F32))
                rtmp = esp.enter_context(sbt("rtmp", [P, 2, 512], F32))
                ISC = esp.enter_context(sbt("ISC", [P, 2, S], F32))
                WKb = esp.enter_context(sbt("WKb", [P, S], BF16))
                MB = esp.enter_context(sbt("MB", [P, 2, S], BF16))
                RL = esp.enter_context(sbt("RL", [P, 2, 512], F32))
                bis = esp.enter_context(sbt("bis", [P, 2, 8], F32))
                Hp = esp.enter_context(sbt("Hp", [P, 2, 24], F32))
                Hn = esp.enter_context(sbt("Hn", [P, 2, 24], F32))
                WKa = rtmp[:].rearrange("p a b -> p (a b)").bitcast(BF16)
                T.op("pool", lambda: nc.gpsimd.memset(DV[:, :, 64:65], 1.0), writes=["DV"])
                _chk("d0")
                plan_w(["dq0", "dq1", "dq2", "dq3", "iq0", "iq1", "iq2", "dk", "ik", "dvw"])
                T.op("pool", lambda: nc.gpsimd.memset(DQ[64:128, :, :], 0.0), writes=["DQ"])
                T.op("pool", lambda: nc.gpsimd.memset(DK[64:128, :], 0.0), writes=["DK"])
                for i in range(4):
                    proj_rope(l, wst, wbf, f"dq{i}", DQ[:, i, :], "DQ", cos64, sin64, ("cos64", "sin64"), rtmp)
                _chk("d1")
                T.op("pool", lambda: nc.gpsimd.memset(IQ[:], 0.0), writes=["IQ0", "IQ1", "IQ2"])
                for i in range(3):
                    proj_rope(l, wst, wbf, f"iq{i}", IQ[:, i, :], f"IQ{i}", cos32, sin32, ("cos32", "sin32"), rtmp)
                _chk("d2")
                proj_rope(l, wst, wbf, "dk", DK, "DK", cos64, sin64, ("cos64", "sin64"), rtmp)
                proj_rope(l, wst, wbf, "ik", IK[:, 0, :], "IK", cos32, sin32, ("cos32", "sin32"), rtmp)
                T.op("pool", lambda: nc.gpsimd.memset(IK[:, 1, :], 0.0), writes=["IK"])
                T.op("pool", lambda: nc.gpsimd.memset(IK[:, 2, :], 0.0), writes=["IK"])
                T.op("pool", lambda: nc.gpsimd.tensor_copy(out=IK[32:64, 1, :], in_=IK[32:64, 0, :]), reads=["IK"], writes=["IK"])
                T.op("pool", lambda: nc.gpsimd.tensor_copy(out=IK[64:96, 2, :], in_=IK[64:96, 0, :]), reads=["IK"], writes=["IK"])
                T.op("pool", lambda: nc.gpsimd.memset(IK[32:64, 0, :], 0.0), reads=["IK"], writes=["IK"])
                T.op("pool", lambda: nc.gpsimd.memset(IK[64:128, 0, :], 0.0), reads=["IK"], writes=["IK"])
                _chk("d3")

                def ev(g, view, pk):
                    T.op("act", lambda: nc.scalar.copy(out=DV[:, 4 * g:4 * g + 4, 0:64], in_=view[:, :, 0:64]),
                         reads=[pk], writes=["DV"])
                    T.op("dve", lambda: nc.vector.tensor_copy(out=IW[:, 4 * g:4 * g + 4, 0, :], in_=view[:, :, 64:72]),
                         reads=[pk], writes=["IW"])
                proj_tok(l, wst, wbf, "dvw", ev)
                _chk("d4")
                T.op("dve", lambda: nc.vector.tensor_scalar(out=IW[:, :, 2, :], in0=IW[:, :, 0, :], scalar1=0.0, scalar2=2.0,
                                                            op0=ALU.is_ge, op1=ALU.mult),
                     reads=["IW"], writes=["IW"])
                T.op("dve", lambda: nc.vector.tensor_scalar(out=IW[:, :, 2, :], in0=IW[:, :, 2, :], scalar1=-1.0, scalar2=None,
                                                            op0=ALU.add),
                     reads=["IW"], writes=["IW"])
                T.op("dve", lambda: nc.vector.tensor_tensor(out=IW[:, :, 1, :], in0=IW[:, :, 0, :], in1=IW[:, :, 2, :], op=ALU.mult),
                     reads=["IW"], writes=["IW"])
                rl_ctr = [0]
                K_BIS = 16
                _chk("dsa_p")

                def indexer(i):
                    b_ = i % 2
                    L = (i + 1) * P
                    ik_ = f"ISC{b_}"
                    for c0 in range(0, L, 512):
                        w = min(512, L - c0)
                        for h in range(8):
                            bank = pbank[0] % 3
                            pbank[0] += 1
                            T.op("pe", lambda: nc.tensor.matmul(
                                PS[bank][:, 0:w], lhsT=IQ[:, h // 3, i * P:(i + 1) * P],
                                rhs=IK[:, h % 3, c0:c0 + w], start=True, stop=True),
                                reads=[f"IQ{h // 3}", "IK"], writes=[pskey[bank]])
                            rs = rl_ctr[0] % 2
                            rl_ctr[0] += 1
                            T.op("act", lambda: nc.scalar.activation(
                                out=RL[:, rs, 0:w], in_=PS[bank][:, 0:w], func=AF.Relu, scale=IW[:, i, 1, h:h + 1]),
                                reads=[pskey[bank], "IW"], writes=[f"RL{rs}"])
                            if h == 0:
                                T.op("dve", lambda: nc.vector.tensor_scalar(
                                    out=ISC[:, b_, c0:c0 + w], in0=RL[:, rs, 0:w], scalar1=IW[:, i, 2, h:h + 1], scalar2=None,
                                    op0=ALU.mult), reads=[f"RL{rs}", "IW"], writes=[ik_])
                            else:
                                T.op("dve", lambda: nc.vector.scalar_tensor_tensor(
                                    out=ISC[:, b_, c0:c0 + w], in0=RL[:, rs, 0:w], scalar=IW[:, i, 2, h:h + 1],
                                    in1=ISC[:, b_, c0:c0 + w], op0=ALU.mult, op1=ALU.add),
                                    reads=[f"RL{rs}", "IW", ik_], writes=[ik_])
                    T.op("dve", lambda: nc.vector.tensor_reduce(out=bis[:, b_, 0:1], in_=ISC[:, b_, 0:L], axis=AX.X, op=ALU.max,
                                                                apply_absolute_value=True),
                         reads=[ik_], writes=[f"bisM{b_}"])
                    T.op("dve", lambda: nc.vector.tensor_scalar(out=bis[:, b_, 0:1], in0=bis[:, b_, 0:1], scalar1=1.001, scalar2=1e-3,
                                                                op0=ALU.mult, op1=ALU.add),
                         reads=[f"bisM{b_}"], writes=[f"bisM{b_}"])
                    T.op("dve", lambda: nc.vector.tensor_scalar(out=Hp[:, b_, :], in0=cstf[:, 6, 0:24], scalar1=bis[:, b_, 0:1],
                                                                scalar2=None, op0=ALU.mult),
                         reads=[f"bisM{b_}", "cstf"], writes=[f"Hp{b_}"])
                    T.op("dve", lambda: nc.vector.memset(bis[:, b_, 1:2], 0.0), writes=[f"bismid{b_}"])
                    T.op("dve", lambda: nc.vector.tensor_scalar(out=Hn[:, b_, :], in0=Hp[:, b_, :], scalar1=-0.5, scalar2=None,
                                                                op0=ALU.mult),
                         reads=[f"Hp{b_}"], writes=[f"Hn{b_}"])
                    T.op("dve", lambda: nc.vector.memset(bis[:, b_, 5:6], 0.0), writes=[f"bisnm{b_}"])
                    T.op("dve", lambda: nc.vector.tensor_tensor(out=ISC[:, b_, i * P:L], in0=ISC[:, b_, i * P:L], in1=causneg, op=ALU.add),
                         reads=[ik_, "cstf"], writes=[ik_])

                def topk_g(i, eng="dve"):
                    b_ = i % 2
                    L = (i + 1) * P
                    ik_ = f"ISC{b_}"
                    if eng == "act":
                        for k in range(K_BIS):
                            T.op("act", lambda: nc.scalar.activation(
                                out=WKa[:, 0:L], in_=ISC[:, b_, 0:L], func=AF.Sign, bias=bis[:, b_, 5:6], scale=1.0,
                                accum_out=bis[:, b_, 2:3]),
                                reads=[ik_, f"bisnm{b_}"], writes=["ropetmp0", "ropetmp1", f"biscnt{b_}"])
                            T.op("act", lambda: nc.scalar.activation(
                                out=bis[:, b_, 6:7], in_=bis[:, b_, 2:3], func=AF.Sign, bias=float(L - 512) + 0.5, scale=1.0),
                                reads=[f"biscnt{b_}"], writes=[f"bissg{b_}"])
                            T.op("act", lambda: nc.scalar.activation(
                                out=bis[:, b_, 5:6], in_=bis[:, b_, 6:7], func=AF.Identity, scale=Hn[:, b_, k:k + 1],
                                bias=bis[:, b_, 5:6]),
                                reads=[f"bissg{b_}", f"Hn{b_}", f"bisnm{b_}"], writes=[f"bisnm{b_}"])
                            yield
                        T.op("act", lambda: nc.scalar.activation(
                            out=bis[:, b_, 4:5], in_=bis[:, b_, 5:6], func=AF.Identity, scale=-1.0,
                            bias=Hn[:, b_, K_BIS - 1:K_BIS]),
                            reads=[f"bisnm{b_}", f"Hn{b_}"], writes=[f"bisthr{b_}"])
                        T.op("dve", lambda: nc.vector.tensor_scalar(out=MB[:, b_, 0:L], in0=ISC[:, b_, 0:L], scalar1=bis[:, b_, 4:5],
                                                                    scalar2=-1.0, op0=ALU.is_ge, op1=ALU.add),
                             reads=[ik_, f"bisthr{b_}"], writes=[f"MB{b_}"])
                        yield
                        return
                    for k in range(K_BIS):
                        T.op("dve", lambda: nc.vector.tensor_scalar(
                            out=WKb[:, 0:L], in0=ISC[:, b_, 0:L], scalar1=bis[:, b_, 1:2], scalar2=None,
                            op0=ALU.is_ge, op1=ALU.add, accum_out=bis[:, b_, 2:3]),
                            reads=[ik_, f"bismid{b_}"], writes=["WKb", f"biscnt{b_}"])
                        T.op("dve", lambda: nc.vector.tensor_scalar(
                            out=bis[:, b_, 3:4], in0=bis[:, b_, 2:3], scalar1=256.0, scalar2=-0.5, op0=ALU.is_ge, op1=ALU.add),
                            reads=[f"biscnt{b_}"], writes=[f"biscm{b_}"])
                        T.op("dve", lambda: nc.vector.scalar_tensor_tensor(
                            out=bis[:, b_, 1:2], in0=bis[:, b_, 3:4], scalar=Hp[:, b_, k:k + 1], in1=bis[:, b_, 1:2],
                            op0=ALU.mult, op1=ALU.add),
                            reads=[f"biscm{b_}", f"Hp{b_}", f"bismid{b_}"], writes=[f"bismid{b_}"])
                        yield
                    T.op("dve", lambda: nc.vector.tensor_tensor(out=bis[:, b_, 4:5], in0=bis[:, b_, 1:2],
                                                                in1=Hp[:, b_, K_BIS:K_BIS + 1], op=ALU.subtract),
                         reads=[f"bismid{b_}", f"Hp{b_}"], writes=[f"bisthr{b_}"])
                    T.op("dve", lambda: nc.vector.tensor_scalar(out=MB[:, b_, 0:L], in0=ISC[:, b_, 0:L], scalar1=bis[:, b_, 4:5],
                                                                scalar2=-1.0, op0=ALU.is_ge, op1=ALU.add),
                         reads=[ik_, f"bisthr{b_}"], writes=[f"MB{b_}"])
                    yield

                def dsa_attn_g(i):
                    b_ = i % 2
                    ktl = [(j, 0) for j in range(i + 1)]

                    def qk(j, c0):
                        return (DK[:, j * P:(j + 1) * P], DQ[:, :, i * P:(i + 1) * P], ["DK", "DQ"])

                    def extra(j, c0):
                        if i >= 2:
                            return [(MB[:, b_, j * P:(j + 1) * P], ID4[:, :, :], [f"MB{b_}", "ID4"], (0, 512))]
                        if j == i:
                            return [(caus01, ID4[:, :, :], ["cstb", "ID4"], (0, 512))]
                        return []

                    def fin(abank):
                        def blk(b, src, skey, rs, rkey, rcol=None):
                            T.op("dve", lambda: nc.vector.tensor_scalar(
                                out=mix[:, i, 64 * b:64 * (b + 1)], in0=src, scalar1=rs, scalar2=None, op0=ALU.mult),
                                reads=[skey, rkey], writes=[f"mix{i}"])
                        finish_wide(abank, 512, 4, blk)
                    yield from attn_wide_g(PT, 512, ktl, qk, extra, lambda j: DV[:, j, :], "DV", fin)

                gens = {}

                def step(g_):
                    if g_ is None:
                        return True
                    try:
                        next(g_)
                        return False
                    except StopIteration:
                        return True

                for i in range(NT):
                    if i == 3:
                        _chk("dsa_i3")
                    if 2 <= i + 2 < NT:
                        indexer(i + 2)
                        gens[i + 2] = topk_g(i + 2, "dve" if (i % 2 == 0) else "act")
                    A_ = dsa_attn_g(i)
                    B1 = gens.pop(i + 1, None)
                    B2 = gens.get(i + 2)
                    dA = dB1 = False
                    c2 = 0
                    n2 = K_BIS // 2 if B2 is not None else 0
                    while not (dA and dB1 and c2 >= n2):
                        if not dA:
                            dA = step(A_)
                        if not dB1:
                            dB1 = step(B1)
                        if c2 < n2:
                            step(B2)
                            c2 += 1
                flush_fin()
                T.barrier()
            _chk("dsa")

            with ExitStack() as esp:
                NQ = esp.enter_context(sbt("NQ", [P, 3, S], BF16))
                NQR = esp.enter_context(sbt("NQR", [P, 3, S], BF16))
                KST = esp.enter_context(sbt("KST", [P, 2, S], BF16))
                KWT = esp.enter_context(sbt("KWT", [P, 2, S], BF16))
                VS = esp.enter_context(sbt("VS", [P, NT, 2, 65], BF16))
                VW = esp.enter_context(sbt("VW", [P, NT, 2, 65], BF16))
                NG = esp.enter_context(sbt("NG", [P, NT, 18], F32))
                KCC = esp.enter_context(sbt("KCC", [P, 2, P], BF16))
                VCC = esp.enter_context(sbt("VCC", [P, 2, 64], BF16))
                esq = ExitStack()
                KCT = esq.enter_context(sbt("KCT", [P, S], BF16))
                VCT = esq.enter_context(sbt("VCT", [P, S], BF16))
                rtmp = esq.enter_context(sbt("rtmp2", [P, 2, 512], F32))
                W1f = esq.enter_context(sbt("W1f", [P, 2048], F32))
                W1b = esq.enter_context(sbt("W1b", [P, 2048], BF16))
                W2f = esq.enter_context(sbt("W2f", [P, 2, 64], F32))
                W2b = esq.enter_context(sbt("W2b", [P, 2, 64], BF16))
                peTf = esq.enter_context(sbt("peTf", [P, 2, 32], F32))
                peTb = esq.enter_context(sbt("peTb", [P, 2, 32], BF16))
                cb = esq.enter_context(sbt("cmpbias", [P, 2], F32))
                HT = esq.enter_context(sbt("HT", [P, 2, P], BF16))
                T.op("pool", lambda: nc.gpsimd.memset(KCC[:], 0.0), writes=["KCC"])
                plan_w(["nq0", "nq1", "nq2", "nks", "nkw", "nkc", "nvc", "nvs", "nvw", "ng"])
                T.op("pool", lambda: nc.gpsimd.memset(VS[:, :, :, 64:65], 1.0), writes=["VS"])
                T.op("pool", lambda: nc.gpsimd.memset(VW[:, :, :, 64:65], 1.0), writes=["VW"])
                for i in range(3):
                    proj_rope(l, wst, wbf, f"nq{i}", NQR[:, i, :], f"NQR{i}", cos64, sin64, ("cos64", "sin64"), rtmp,
                              raw_dst=NQ[:, i, :], raw_key=f"NQ{i}")
                T.op("pool", lambda: nc.gpsimd.memset(KST[64:128, 0, :], 0.0), writes=["KST"])
                T.op("pool", lambda: nc.gpsimd.memset(KST[0:64, 1, :], 0.0), writes=["KST"])
                T.op("pool", lambda: nc.gpsimd.memset(KWT[64:128, 0, :], 0.0), writes=["KWT"])
                T.op("pool", lambda: nc.gpsimd.memset(KWT[0:64, 1, :], 0.0), writes=["KWT"])
                proj_rope(l, wst, wbf, "nks", KST, "KST", cos64, sin64, ("cos64", "sin64"), rtmp, halves=True)
                proj_rope(l, wst, wbf, "nkw", KWT, "KWT", cos64, sin64, ("cos64", "sin64"), rtmp, halves=True)
                proj_plain(l, wst, wbf, "nkc", KCT, "KCT")
                proj_plain(l, wst, wbf, "nvc", VCT, "VCT")
                for nm, dst, dk_ in (("nvs", VS, "VS"), ("nvw", VW, "VW")):
                    def ev(g, view, pk, dst=dst, dk_=dk_):
                        T.op("act", lambda: nc.scalar.copy(out=dst[:, 4 * g:4 * g + 4, :, 0:64],
                                                           in_=view.rearrange("p q (h d) -> p q h d", h=2)),
                             reads=[pk], writes=[dk_])
                    proj_tok(l, wst, wbf, nm, ev)

                def ev(g, view, pk):
                    T.op("act", lambda: nc.scalar.activation(out=NG[:, 4 * g:4 * g + 4, :], in_=view[:, :, 0:18], func=AF.Exp, scale=-1.0),
                         reads=[pk], writes=["NG"])
                proj_tok(l, wst, wbf, "ng", ev)
                T.op("dve", lambda: nc.vector.tensor_scalar(out=NG[:], in0=NG[:], scalar1=1.0, scalar2=None, op0=ALU.add),
                     reads=["NG"], writes=["NG"])
                T.op("dve", lambda: nc.vector.reciprocal(out=NG[:], in_=NG[:]), reads=["NG"], writes=["NG"])

                T.dma(lambda: nc.sync.dma_start(out=W2f[:], in_=w2_d[l, :, :, :].rearrange("a p m -> p a m")), "W2f", writes=["W2f"])
                T.op("pool", lambda: nc.gpsimd.tensor_copy(out=W2b[:], in_=W2f[:]), reads=["W2f"], writes=["W2b"])
                T.dma(lambda: nc.sync.dma_start(out=peTf[:], in_=peT_d[l, :, :, :].rearrange("a p m -> p a m")), "peTf", writes=["peTf"])
                T.op("pool", lambda: nc.gpsimd.tensor_copy(out=peTb[:], in_=peTf[:]), reads=["peTf"], writes=["peTb"])
                for a, (src, skey) in enumerate(((KCT, "KCT"), (VCT, "VCT"))):
                    T.dma(lambda a=a: nc.sync.dma_start(out=W1f[:], in_=w1_d[l, a, :, :]), "W1f", writes=["W1f"])
                    T.op("dve", lambda a=a: nc.vector.tensor_copy(out=W1b[:], in_=W1f[:]), reads=["W1f"], writes=["W1b"])
                    for g in range(2):
                        hp = 64 * g
                        for ll in range(32):
                            T.op("pe", lambda ll=ll: nc.tensor.matmul(
                                PS[0][hp:hp + 64, 0:1], lhsT=W1b[hp:hp + 64, ll * 64:(ll + 1) * 64],
                                rhs=peTb[hp:hp + 64, a, ll:ll + 1], start=(ll == 0), stop=(ll == 31)),
                                reads=["W1b", "peTb"], writes=["ps0"])
                        T.op("dve", lambda: nc.vector.tensor_copy(out=cb[hp:hp + 64, 0:1], in_=PS[0][hp:hp + 64, 0:1]),
                             reads=["ps0"], writes=["cb"])
                        for ll in range(32):
                            rhs = src[hp:hp + 64, ll:ll + 16 * (NC_CMP - 1) + 1:16]
                            T.op("pe", lambda ll=ll, rhs=rhs: nc.tensor.matmul(
                                PS[1][hp:hp + 64, 0:NC_CMP], lhsT=W1b[hp:hp + 64, ll * 64:(ll + 1) * 64],
                                rhs=rhs, start=(ll == 0), stop=(ll == 31)),
                                reads=["W1b", skey], writes=["ps1"])
                        T.op("act", lambda: nc.scalar.activation(out=HT[hp:hp + 64, a, 0:NC_CMP], in_=PS[1][hp:hp + 64, 0:NC_CMP],
                                                                func=AF.Silu, bias=cb[hp:hp + 64, 0:1], scale=1.0),
                             reads=["ps1", "cb"], writes=["HT"])
                        if a == 0:
                            T.op("pe", lambda: nc.tensor.matmul(PS[2][hp:hp + 64, 0:NC_CMP], lhsT=W2b[hp:hp + 64, 0, :],
                                                                rhs=HT[hp:hp + 64, 0, 0:NC_CMP], start=True, stop=True),
                                 reads=["W2b", "HT"], writes=["ps2"])
                            T.op("act", lambda: nc.scalar.copy(out=KCC[hp:hp + 64, g, 0:NC_CMP], in_=PS[2][hp:hp + 64, 0:NC_CMP]),
                                 reads=["ps2"], writes=["KCC"])
                        else:
                            T.op("pe", lambda: nc.tensor.matmul(PS[3][0:NC_CMP, 0:64], lhsT=HT[hp:hp + 64, 1, 0:NC_CMP],
                                                                rhs=W2b[hp:hp + 64, 1, :], start=True, stop=True),
                                 reads=["W2b", "HT"], writes=["ps3"])
                            T.op("act", lambda: nc.scalar.copy(out=VCC[0:NC_CMP, g, :], in_=PS[3][0:NC_CMP, 0:64]),
                                 reads=["ps3"], writes=["VCC"])
                T.barrier()
                esq.close()
                MBN = esp.enter_context(sbt("MBN", [P, 2, 2, S], BF16))
                cmk = esp.enter_context(sbt("cmk", [P, 2, P], F32))
                cmkb = esp.enter_context(sbt("cmkb", [P, 2, P], BF16))
                fbt = esp.enter_context(sbt("fbt", [P, 2, 32], F32))
                pn = esp.enter_context(sbt("pnorm", [P, 6, P], F32))
                pT = esp.enter_context(sbt("pT", [P, 6, P], BF16))
                PP = esp.enter_context(sbt("PP", [P, 2, 132], F32))
                sc = esp.enter_context(sbt("selsc", [P, 2, 2, 32], F32))
                sm = esp.enter_context(sbt("selm", [P, 16], F32))
                mbk = esp.enter_context(sbt("mbk", [P, 2, 32], BF16))
                ON = esp.enter_context(sbt("ON", [P, 2, 6, 64], F32))
                rs2 = esp.enter_context(sbt("rs2", [P, 16], F32))
                T.op("dve", lambda: nc.vector.memset(PP[:, 0, :], 0.0), writes=["PP0"])
                T.op("dve", lambda: nc.vector.memset(PP[:, 1, :], 0.0), writes=["PP1"])
                T.op("dve", lambda: nc.vector.memset(pn[:], 0.0), writes=[f"pn{h_}" for h_ in range(6)])

                def nsa_cmp_g(i):
                    par = i % 2
                    ms = i % 2
                    T.dma(lambda i=i, ms=ms: nc.sync.dma_start(out=cmk[:, ms, :], in_=cmpm_d[i, :, :]), f"cmk{ms}", writes=[f"cmk{ms}"])
                    T.op("pool", lambda ms=ms: nc.gpsimd.tensor_copy(out=cmkb[:, ms, :], in_=cmk[:, ms, :]),
                         reads=[f"cmk{ms}"], writes=[f"cmkb{ms}"])
                    if i >= 8:
                        T.dma(lambda i=i, ms=ms: nc.sync.dma_start(out=fbt[:, ms, :], in_=fb_d[i, :, :]), f"fbt{ms}", writes=[f"fbt{ms}"])
                    for hd in range(6):
                        g, jj = hd // 3, hd % 3
                        hp = 64 * g
                        bank = hd // 4
                        cs = (hd % 4) * P
                        T.op("pe", lambda: nc.tensor.matmul(
                            PS[bank][:, cs:cs + P], lhsT=NQ[:, jj, i * P:(i + 1) * P], rhs=KCC[:, g, :],
                            start=True, stop=False), reads=[f"NQ{jj}", "KCC"], writes=[pskey[bank]])
                        T.op("pe", lambda: nc.tensor.matmul(
                            PS[bank][:, cs:cs + P], lhsT=identbig, rhs=cmkb[:, ms, :], start=False, stop=True),
                            reads=["cstb", f"cmkb{ms}"], writes=[pskey[bank]])
                        if hd % 2 == 1:
                            yield
                    for hd in range(6):
                        bank = hd // 4
                        cs = (hd % 4) * P
                        T.op("act", lambda: nc.scalar.activation(
                            out=pn[:, hd, 0:NC_CMP], in_=PS[bank][:, cs:cs + NC_CMP], func=AF.Exp, scale=0.125,
                            accum_out=rs2[:, hd:hd + 1]),
                            reads=[pskey[bank]], writes=[f"pn{hd}", f"rs2_{hd}"])
                        if hd % 2 == 1:
                            yield
                    rk = [f"rs2_{h_}" for h_ in range(6)]
                    T.op("dve", lambda: nc.vector.tensor_scalar(out=rs2[:, 0:6], in0=rs2[:, 0:6], scalar1=1e-30, scalar2=None, op0=ALU.max),
                         reads=rk, writes=rk)
                    T.op("dve", lambda: nc.vector.reciprocal(out=rs2[:, 0:6], in_=rs2[:, 0:6]), reads=rk, writes=rk)
                    for hd in range(6):
                        T.op("dve", lambda: nc.vector.tensor_scalar(
                            out=pn[:, hd, 0:NC_CMP], in0=pn[:, hd, 0:NC_CMP], scalar1=rs2[:, hd:hd + 1], scalar2=None,
                            op0=ALU.mult), reads=[f"pn{hd}", f"rs2_{hd}"], writes=[f"pn{hd}"])
                    yield
                    if i >= 8:
                        for g in range(2):
                            T.op("dve", lambda: nc.vector.tensor_reduce(
                                out=PP[:, g, 1:1 + NC_CMP], in_=pn[:, 3 * g:3 * g + 3, 0:NC_CMP].rearrange("p j c -> p c j"),
                                axis=AX.X, op=ALU.add),
                                reads=[f"pn{3 * g}", f"pn{3 * g + 1}", f"pn{3 * g + 2}"], writes=[f"PP{g}"])
                        yield
                    for hd in range(6):
                        tb = hd // 4
                        cs = (hd % 4) * P
                        T.op("pe", lambda: nc.tensor.transpose(PS[tb][:, cs:cs + P], pn[:, hd, :], ident_f),
                             reads=[f"pn{hd}", "cstf"], writes=[pskey[tb]])
                    T.op("act", lambda: nc.scalar.copy(out=pT[:, 0:4, :], in_=PS[0][:, :].rearrange("p (h t) -> p h t", h=4)),
                         reads=["ps0"], writes=[f"pT{h_}" for h_ in range(4)])
                    T.op("act", lambda: nc.scalar.copy(out=pT[:, 4:6, :], in_=PS[1][:, 0:2 * P].rearrange("p (h t) -> p h t", h=2)),
                         reads=["ps1"], writes=["pT4", "pT5"])
                    yield
                    for hd in range(6):
                        g = hd // 3
                        T.op("pe", lambda: nc.tensor.matmul(
                            PS[0][:, hd * 64:(hd + 1) * 64], lhsT=pT[0:NC_CMP, hd, :], rhs=VCC[0:NC_CMP, g, :], start=True, stop=True),
                            reads=[f"pT{hd}", "VCC"], writes=["ps0"])
                    T.op("dve", lambda: nc.vector.tensor_tensor(
                        out=ON[:, par, :, :], in0=PS[0][:, 0:384].rearrange("p (h d) -> p h d", h=6),
                        in1=NG[:, i, 0:6].unsqueeze(2).to_broadcast([P, 6, 64]), op=ALU.mult),
                        reads=["ps0", "NG"], writes=[f"ON{par}_{h_}" for h_ in range(6)])
                    if 0 not in NSA_BR[0]:
                        T.op("dve", lambda: nc.vector.memset(ON[:, par, :, :], 0.0), writes=[f"ON{par}_{h_}" for h_ in range(6)])
                    yield
                    for g in range(2):
                        if i >= 8:
                            T.op("dve", lambda: nc.vector.tensor_reduce(
                                out=sc[:, g, 0, :], in_=PP[:, g, 0:128].rearrange("p (n k) -> p n k", k=4), axis=AX.X, op=ALU.add),
                                reads=[f"PP{g}"], writes=[f"sc{g}"])
                            T.op("dve", lambda: nc.vector.tensor_tensor(out=sc[:, g, 0, :], in0=sc[:, g, 0, :],
                                                                        in1=PP[:, g, 4:132:4], op=ALU.add),
                                 reads=[f"PP{g}", f"sc{g}"], writes=[f"sc{g}"])
                            T.op("dve", lambda ms=ms: nc.vector.tensor_tensor(out=sc[:, g, 0, :], in0=sc[:, g, 0, :],
                                                                              in1=fbt[:, ms, :], op=ALU.add),
                                 reads=[f"fbt{ms}", f"sc{g}"], writes=[f"sc{g}"])
                            T.op("dve", lambda: nc.vector.tensor_copy(out=sc[:, g, 1, :], in_=sc[:, g, 0, :]),
                                 reads=[f"sc{g}"], writes=[f"scw{g}"])
                            T.op("dve", lambda: nc.vector.max(out=sm[:, 0:8], in_=sc[:, g, 1, :]), reads=[f"scw{g}"], writes=["sm"])
                            T.op("dve", lambda: nc.vector.match_replace(out=sc[:, g, 1, :], in_to_replace=sm[:, 0:8],
                                                                        in_values=sc[:, g, 1, :], imm_value=-3.0e9),
                                 reads=[f"scw{g}", "sm"], writes=[f"scw{g}"])
                            T.op("dve", lambda: nc.vector.max(out=sm[:, 8:16], in_=sc[:, g, 1, :]), reads=[f"scw{g}"], writes=["sm"])
                            T.op("dve", lambda: nc.vector.tensor_scalar(out=mbk[:, g, :], in0=sc[:, g, 0, :], scalar1=sm[:, 15:16],
                                                                        scalar2=-1.0, op0=ALU.is_ge, op1=ALU.add),
                                 reads=[f"sc{g}", "sm"], writes=[f"mbk{g}"])
                            L = (i + 1) * P
                            nb = L // 64
                            T.op("act", lambda nb=nb, L=L: nc.scalar.copy(
                                out=MBN[:, par, g, 0:L].rearrange("p (n k) -> p n k", k=64),
                                in_=mbk[:, g, 0:nb].unsqueeze(2).to_broadcast([P, nb, 64])),
                                reads=[f"mbk{g}"], writes=[f"MBN{par}_{g}"])
                            T.op("pool", lambda L=L: nc.gpsimd.tensor_tensor(out=MBN[:, par, g, i * P:L], in0=MBN[:, par, g, i * P:L],
                                                                             in1=caus01, op=ALU.add),
                                 reads=["cstb", f"MBN{par}_{g}"], writes=[f"MBN{par}_{g}"])
                    yield

                def nsa_attn_g(i):
                    par = i % 2
                    for g in range(2):
                        hp = 64 * g
                        for br in (1, 2):
                            if br not in NSA_BR[0]:
                                continue
                            KT_, kkey = (KST, "KST") if br == 1 else (KWT, "KWT")
                            VX, vkey = (VS, "VS") if br == 1 else (VW, "VW")
                            kt = list(range(i + 1)) if br == 1 else list(range(max(0, i - 4), i + 1))
                            ktl = [(j, 0) for j in kt]

                            def qk(j, c0, KT_=KT_, kkey=kkey, g=g):
                                return (KT_[:, g, j * P:(j + 1) * P], NQR[:, :, i * P:(i + 1) * P],
                                        [kkey, "NQR0", "NQR1", "NQR2"])

                            def extra(j, c0, br=br, g=g):
                                id3 = ID4[:, 0:3, :]
                                if br == 1:
                                    if i >= 8:
                                        return [(MBN[:, par, g, j * P:(j + 1) * P], id3, [f"MBN{par}_{g}", "ID4"], (0, 384))]
                                    return [(caus01, id3, ["cstb", "ID4"], (0, 384))] if j == i else []
                                ex = []
                                if j == i:
                                    ex.append((caus01, id3, ["cstb", "ID4"], (0, 384)))
                                if j == i - 4:
                                    ex.append((band01, id3, ["cstb", "ID4"], (0, 384)))
                                return ex

                            def fin(abank, br=br, g=g):
                                def blk(b, src, skey, rs, rkey, rcol=None):
                                    hd = 3 * g + b
                                    gi = br * 6 + hd
                                    if b == 0:
                                        rs3 = rsm[:, rcol:rcol + 3]
                                        T.op("dve", lambda: nc.vector.tensor_tensor(out=rs3, in0=rs3, in1=NG[:, i, gi:gi + 3], op=ALU.mult),
                                             reads=[rkey, "NG"], writes=[rkey])
                                    T.op("dve", lambda: nc.vector.scalar_tensor_tensor(
                                        out=ON[:, par, hd, :], in0=src, scalar=rs, in1=ON[:, par, hd, :],
                                        op0=ALU.mult, op1=ALU.add),
                                        reads=[skey, rkey, f"ON{par}_{hd}"], writes=[f"ON{par}_{hd}"])
                                finish_wide(abank, 384, 3, blk)
                            yield from attn_wide_g(PT, 384, ktl, qk, extra, lambda j, VX=VX, g=g: VX[:, j, g, :], vkey, fin)
                    pending_fin.append(lambda: T.op(
                        "dve", lambda: nc.vector.tensor_copy(out=mix[:, i, 640:1024], in_=ON[:, par, :, :].rearrange("p h d -> p (h d)")),
                        reads=[f"ON{par}_{h_}" for h_ in range(6)], writes=[f"mix{i}"]))
                    yield

                fw_banks[0] = [2]
                interleave(nsa_cmp_g(0), iter(()))
                for i in range(NT):
                    interleave(nsa_attn_g(i), nsa_cmp_g(i + 1) if i + 1 < NT else iter(()))
                flush_fin()
                fw_banks[0] = [0, 1, 2]
                T.barrier()
            _chk("nsa")

            if "mix" in dbg and l == dbg_layer[0]:
                for i in range(NT):
                    T.dma(lambda i=i: nc.sync.dma_start(out=dbg_d["mix"][i * P:(i + 1) * P, :], in_=mix[:, i, :]),
                          "store", reads=[f"mix{i}"], writes=[f"dbgmix{i}"])
            with ExitStack() as esp:
                WG = esp.enter_context(sbt("WG", [P, KC, D], BF16))
                WO = esp.enter_context(sbt("WO", [P, KC, D], BF16))
                lng = esp.enter_context(sbt("lng", [P, D], F32))
                lnb = esp.enter_context(sbt("lnb", [P, D], F32))
                xres = esp.enter_context(sbt("xres", [P, 2, D], F32))
                Gt = esp.enter_context(sbt("Gt", [P, D], F32))
                mg = esp.enter_context(sbt("mg", [P, D], F32))
                mgT = esp.enter_context(sbt("mgT", [P, 2, KC, P], BF16))
                zt = esp.enter_context(sbt("zt", [P, D], F32))
                xo = esp.enter_context(sbt("xo", [P, 2, D], F32))
                st6 = esp.enter_context(sbt("st6", [P, 2, 6], F32))
                mv = esp.enter_context(sbt("mv", [P, 4], F32))
                mhalf = esp.enter_context(sbt("mhalf", [P, 2], F32))
                T.op("pool", lambda: nc.gpsimd.memset(mhalf[:], -0.5), writes=["mhalf"])
                T.dma(lambda: nc.sync.dma_start(out=lng[:], in_=lng_d[l, :, :]), "lng", writes=["lng"])
                T.dma(lambda: nc.sync.dma_start(out=lnb[:], in_=lnb_d[l, :, :]), "lnb", writes=["lnb"])
                go = UOFF["g0"][0]
                for kc in range(KC):
                    slot = wslot_ctr[0] % 2
                    wslot_ctr[0] += 1
                    src = wperm_d[l, kc * P:(kc + 1) * P, go:go + D]
                    T.dma(lambda src=src, slot=slot: nc.sync.dma_start(out=wst[:, slot, :, :].rearrange("p a b -> p (a b)"), in_=src),
                          f"wst{slot}", writes=[f"wst{slot}"])
                    T.op("dve", lambda kc=kc, slot=slot: nc.vector.tensor_copy(out=WG[:, kc, :], in_=wst[:, slot, :, :].rearrange("p a b -> p (a b)")),
                         reads=[f"wst{slot}"], writes=[f"WG{kc}"])
                for kc in range(KC):
                    slot = wslot_ctr[0] % 2
                    wslot_ctr[0] += 1
                    src = wout_d[l, kc * P:(kc + 1) * P, :]
                    T.dma(lambda src=src, slot=slot: nc.sync.dma_start(out=wst[:, slot, :, :].rearrange("p a b -> p (a b)"), in_=src),
                          f"wst{slot}", writes=[f"wst{slot}"])
                    T.op("act", lambda kc=kc, slot=slot: nc.scalar.copy(out=WO[:, kc, :], in_=wst[:, slot, :, :].rearrange("p a b -> p (a b)")),
                         reads=[f"wst{slot}"], writes=[f"WO{kc}"])
                xsrc = x_d if l == 0 else x1_d

                def epiA(i):
                    xs_ = i % 2
                    mp = i % 2
                    T.dma(lambda: nc.sync.dma_start(out=xres[:, xs_, :], in_=xsrc[i * P:(i + 1) * P, :]),
                          f"xres{xs_}", writes=[f"xres{xs_}"])
                    for hf in range(2):
                        for kc in range(KC):
                            T.op("pe", lambda: nc.tensor.matmul(
                                PS[hf][:, :], lhsT=uT[:, kc, i * P:(i + 1) * P], rhs=WG[:, kc, hf * 512:(hf + 1) * 512],
                                start=(kc == 0), stop=(kc == KC - 1)), reads=[f"uT{i}", f"WG{kc}"], writes=[pskey[hf]])
                        T.op("act", lambda: nc.scalar.activation(out=Gt[:, hf * 512:(hf + 1) * 512], in_=PS[hf][:, :], func=AF.Silu),
                             reads=[pskey[hf]], writes=["Gt"])
                    T.op("dve", lambda: nc.vector.tensor_tensor(out=mg[:], in0=Gt[:], in1=mix[:, i, :], op=ALU.mult),
                         reads=["Gt", f"mix{i}"], writes=["mg"])

                def epiA2(i):
                    mp = i % 2
                    for hf in range(2):
                        bank = 2 + hf
                        for q in range(4):
                            fc = hf * 4 + q
                            T.op("pe", lambda: nc.tensor.transpose(
                                PS[bank][:, q * P:(q + 1) * P], mg[:, fc * P:(fc + 1) * P], ident_f),
                                reads=["mg", "cstf"], writes=[pskey[bank]])
                        T.op("act", lambda: nc.scalar.copy(
                            out=mgT[:, mp, hf * 4:hf * 4 + 4, :], in_=PS[bank][:, :].rearrange("p (q t) -> p q t", q=4)),
                            reads=[pskey[bank]], writes=[f"mgT{mp}"])

                def epiB(i):
                    xs_ = i % 2
                    mp = i % 2
                    for hf in range(2):
                        bank = 4 + hf
                        for fc in range(KC):
                            T.op("pe", lambda: nc.tensor.matmul(
                                PS[bank][:, :], lhsT=mgT[:, mp, fc, :], rhs=WO[:, fc, hf * 512:(hf + 1) * 512],
                                start=(fc == 0), stop=(fc == KC - 1)), reads=[f"mgT{mp}", f"WO{fc}"], writes=[pskey[bank]])
                        sl = slice(hf * 512, (hf + 1) * 512)
                        T.op("dve", lambda: nc.vector.tensor_tensor(out=zt[:, sl], in0=PS[bank][:, :], in1=g1bc[:, l, sl], op=ALU.mult),
                             reads=[pskey[bank], "g1bc"], writes=["zt"])
                        T.op("dve", lambda: nc.vector.scalar_tensor_tensor(
                            out=zt[:, sl], in0=xres[:, xs_, sl], scalar=float(ALPHA), in1=zt[:, sl], op0=ALU.mult, op1=ALU.add),
                            reads=[f"xres{xs_}", "zt"], writes=["zt"])
                        T.op("dve", lambda: nc.vector.bn_stats(out=st6[:, hf, :], in_=zt[:, sl]), reads=["zt"], writes=["st6"])
                    T.op("dve", lambda: nc.vector.bn_aggr(out=mv[:, 0:2], in_=st6[:].rearrange("p a b -> p (a b)")), reads=["st6"], writes=["mv"])
                    T.op("dve", lambda: nc.vector.tensor_scalar(out=mv[:, 2:3], in0=mv[:, 1:2], scalar1=float(LN_EPS), scalar2=None, op0=ALU.add),
                         reads=["mv"], writes=["mv2"])
                    T.op("pool", lambda: nc.gpsimd.tensor_tensor(out=mv[:, 3:4], in0=mv[:, 2:3], in1=mhalf[:, 0:1], op=ALU.pow),
                         reads=["mv2", "mhalf"], writes=["mv3"])
                    os_ = i % 2
                    T.op("dve", lambda: nc.vector.tensor_scalar(out=xo[:, os_, :], in0=zt[:], scalar1=mv[:, 0:1], scalar2=mv[:, 3:4],
                                                                op0=ALU.subtract, op1=ALU.mult),
                         reads=["zt", "mv", "mv3"], writes=[f"xo{os_}"])
                    T.op("dve", lambda: nc.vector.tensor_tensor(out=xo[:, os_, :], in0=xo[:, os_, :], in1=lng[:], op=ALU.mult),
                         reads=[f"xo{os_}", "lng"], writes=[f"xo{os_}"])
                    T.op("pool", lambda: nc.gpsimd.tensor_tensor(out=xo[:, os_, :], in0=xo[:, os_, :], in1=lnb[:], op=ALU.add),
                         reads=[f"xo{os_}", "lnb"], writes=[f"xo{os_}"])
                    dst = out_d if l == n_layers - 1 else x1_d
                    T.dma(lambda: nc.sync.dma_start(out=dst[i * P:(i + 1) * P, :], in_=xo[:, os_, :]),
                          f"store{os_}", reads=[f"xo{os_}"], writes=[f"dst{l}_{i}"])

                epiA(0)
                epiA2(0)
                for i in range(NT):
                    if i + 1 < NT:
                        epiA(i + 1)
                    epiB(i)
                    if l < n_layers - 1 and i >= 1:
                        make_uT_tile(xo[:, (i - 1) % 2, :], f"xo{(i - 1) % 2}", l + 1, i - 1, bank0=6)
                    if i + 1 < NT:
                        epiA2(i + 1)
                if l < n_layers - 1:
                    make_uT_tile(xo[:, (NT - 1) % 2, :], f"xo{(NT - 1) % 2}", l + 1, NT - 1, bank0=6)
                T.barrier()


dbg_layer = [0]
NSA_BR = [(0, 1, 2)]


def _consts():
    t = np.arange(S, dtype=np.float32)
    out = {}
    for nm, dim in (("64", 64), ("32", 32)):
        half = dim // 2
        inv = (10000.0 ** (-np.arange(half, dtype=np.float32) / half)).astype(np.float32)
        ang = t[None, :] * inv[:, None]
        cos = np.cos(ang).astype(np.float32)
        sin = np.sin(ang).astype(np.float32)
        c_full = np.concatenate([cos, cos], 0)
        s_full = np.concatenate([-sin, sin], 0)
        out["cos" + nm] = np.ascontiguousarray(np.tile(c_full, (P // dim, 1))).astype(ml_dtypes.bfloat16)
        out["sin" + nm] = np.ascontiguousarray(np.tile(s_full, (P // dim, 1))).astype(ml_dtypes.bfloat16)
    a = np.arange(P)
    tt, ss = a[:, None], a[None, :]
    cst = np.zeros((P, 10, P), np.float32)
    for m in range(64):
        cst[64 + m, 9, m] = 1.0
    for m in range(P):
        cst[(m // 64) * 64 + ((m % 64) + 32) % 64, 7, m] = 1.0
        cst[(m // 32) * 32 + ((m % 32) + 16) % 32, 8, m] = 1.0
    cst[:, 6, :] = (2.0 ** (-np.arange(P, dtype=np.float64).clip(0, 60)))[None, :]
    cst[:, 0, :] = np.eye(P)
    cst[:, 1, :] = np.where(ss <= tt, 0.0, -1.0)
    cst[:, 2, :] = np.where(ss > tt, 0.0, -1.0)
    cst[:, 3, :] = np.where(ss <= tt, 0.0, NEG)
    cst[:, 4, :] = np.eye(P) * MASKV
    cst[:, 5, :] = 1.0
    out["cst"] = cst
    out["cst_b"] = cst.astype(ml_dtypes.bfloat16)
    cm = np.zeros((NT, P, P), np.float32)
    fb = np.zeros((NT, P, 32), np.float32)
    cidx = np.arange(P)
    nidx = np.arange(32)
    for i in range(NT):
        tpos = i * P + a
        valid = (16 * cidx[None, :] + 31 <= tpos[:, None]) & (cidx[None, :] < NC_CMP)
        cm[i] = np.where(valid, 0.0, -1.0)
        cur = tpos // 64
        forced = (nidx[None, :] == 0) | (nidx[None, :] == cur[:, None]) | (nidx[None, :] == cur[:, None] - 1)
        future = (64 * nidx[None, :]) > tpos[:, None]
        fb[i] = np.where(forced, 1.0e9, np.where(future, -1.0e9, 0.0))
    out["cmp_mask"] = cm
    out["sel_fb"] = fb
    return out


_NC_CACHE = {}


def _prep_shared(w_ada, b_ada, w_in, b_f, cmp_pe, cmp_w1, cmp_w2, w_out, ln_g, ln_b):
    f = lambda a: np.ascontiguousarray(np.asarray(a, dtype=np.float32))
    sh = dict(_consts())
    sh["w_ada"] = f(w_ada)
    sh["b_ada"] = f(b_ada)
    sh["b_gate_bc"] = f(np.broadcast_to(np.asarray(b_ada)[:, None, 2 * D:3 * D], (DEPTH, P, D)))
    sh["w_perm"] = f(np.asarray(w_in)[:, :, PERM])
    bfp = np.zeros((DEPTH, P, 8), np.float32)
    bfp[:, 0:6, 0] = np.asarray(b_f)
    sh["b_f"] = bfp
    peT = np.asarray(cmp_pe).transpose(0, 1, 3, 2)
    sh["cmp_peT"] = f(np.concatenate([peT, peT], axis=2))
    w1 = np.asarray(cmp_w1).reshape(DEPTH, 2, 32, 64, 64).transpose(0, 1, 3, 2, 4).reshape(DEPTH, 2, 64, 32 * 64)
    sh["cmp_w1r"] = f(np.concatenate([w1, w1], axis=2))
    w2 = np.asarray(cmp_w2)
    sh["cmp_w2r"] = f(np.concatenate([w2, w2], axis=2))
    sh["w_out"] = f(w_out)
    sh["ln_g_bc"] = f(np.broadcast_to(np.asarray(ln_g)[:, None, :], (DEPTH, P, D)))
    sh["ln_b_bc"] = f(np.broadcast_to(np.asarray(ln_b)[:, None, :], (DEPTH, P, D)))
    return sh


def kernel(x, c, w_ada, b_ada, w_in, b_f, cmp_pe, cmp_w1, cmp_w2, w_out, ln_g, ln_b):
    x = np.asarray(x, dtype=np.float32)
    c = np.asarray(c, dtype=np.float32)
    B = x.shape[0]
    if "nc" not in _NC_CACHE:
        _NC_CACHE["nc"] = build_program()
    nc = _NC_CACHE["nc"]
    sh = _prep_shared(w_ada, b_ada, w_in, b_f, cmp_pe, cmp_w1, cmp_w2, w_out, ln_g, ln_b)
    in_maps = []
    for b in range(B):
        m = dict(sh)
        m["x"] = np.ascontiguousarray(x[b])
        ccol = np.ascontiguousarray(c[b].reshape(KC, P).T)
        m["c_col"] = ccol
        m["c_bc"] = np.ascontiguousarray(np.broadcast_to(ccol[:, :, None], (P, KC, P)))
        in_maps.append(m)
    res = run_bass_kernel_spmd(nc, in_maps, core_ids=list(range(B)))
    return np.stack([np.asarray(r["out"], dtype=np.float32) for r in res.results], axis=0)
```

```python
import numpy as np
import ml_dtypes
from contextlib import ExitStack
import concourse.bass as bass
import concourse.mybir as mybir
from concourse.bass_utils import run_bass_kernel_spmd

F32 = mybir.dt.float32
BF16 = mybir.dt.bfloat16
AF = mybir.ActivationFunctionType
ALU = mybir.AluOpType
AX = mybir.AxisListType

P = 128
S = 2048
NT = 16
D = 1024
KC = 8
DEPTH = 2
HD = 64
ALPHA = (2.0 * DEPTH) ** 0.25
LN_EPS = 1e-5
NEG = -1.0e30
MASKV = 30000.0
NC_CMP = 127

_SPL = (("dsa_q", 256), ("dsa_k", 64), ("dsa_v", 64), ("idx_q", 256), ("idx_k", 32), ("idx_w", 8),
        ("fox_q", 384), ("fox_k", 384), ("fox_v", 384), ("fox_f", 6), ("nsa_q", 384),
        ("nsa_kc", 128), ("nsa_vc", 128), ("nsa_ks", 128), ("nsa_vs", 128), ("nsa_kw", 128),
        ("nsa_vw", 128), ("nsa_g", 18), ("gate", 1024))
OFF = {}
_o = 0
for _n, _w in _SPL:
    OFF[_n] = _o
    _o += _w
IN_WIDTH = _o


def _unit(name, h, dim=64):
    return np.arange(OFF[name] + h * dim, OFF[name] + (h + 1) * dim)


def _swap(cols):
    h = len(cols) // 2
    return np.concatenate([cols[h:], cols[:h]])


def _cat(*a):
    return np.concatenate(a)


def _build_units():
    U = {}
    for i in range(3):
        U[f"fq{i}"] = _cat(_unit("fox_q", 2 * i), _unit("fox_q", 2 * i + 1))
        U[f"fk{i}"] = _cat(_unit("fox_k", 2 * i), _unit("fox_k", 2 * i + 1))
        U[f"fv{i}"] = _cat(_unit("fox_v", 2 * i), _unit("fox_v", 2 * i + 1))
    U["ff"] = np.arange(OFF["fox_f"], OFF["fox_f"] + 6)
    for i in range(4):
        a = _unit("dsa_q", i)
        U[f"dq{i}"] = a
    for i in range(3):
        a = [_unit("idx_q", 3 * i + j, 32) for j in range(3) if 3 * i + j < 8]
        U[f"iq{i}"] = _cat(*a)
    k = _unit("dsa_k", 0)
    U["dk"] = k
    k = _unit("idx_k", 0, 32)
    U["ik"] = _cat(k, k, k)
    U["dvw"] = _cat(_unit("dsa_v", 0), np.arange(OFF["idx_w"], OFF["idx_w"] + 8), _unit("dsa_v", 0)[0:56])
    for i in range(3):
        a, b = _unit("nsa_q", i), _unit("nsa_q", i + 3)
        U[f"nq{i}"] = _cat(a, b)
    for nm in ("kc", "vc", "vs", "vw"):
        U["n" + nm] = np.arange(OFF["nsa_" + nm], OFF["nsa_" + nm] + 128)
    for nm in ("ks", "kw"):
        a, b = _unit("nsa_" + nm, 0), _unit("nsa_" + nm, 1)
        U["n" + nm] = _cat(a, b)
    U["ng"] = _cat(np.arange(OFF["nsa_g"], OFF["nsa_g"] + 18), np.arange(OFF["nsa_g"], OFF["nsa_g"] + 14))
    for i in range(8):
        U[f"g{i}"] = np.arange(OFF["gate"] + 128 * i, OFF["gate"] + 128 * (i + 1))
    return U


UNITS = _build_units()
UOFF = {}
_o = 0
for _k, _v in UNITS.items():
    UOFF[_k] = (_o, len(_v))
    _o += len(_v)
TOTC = _o
PERM = np.concatenate(list(UNITS.values()))


class Tr:
    def __init__(self, nc, es):
        self.nc = nc
        self.es = es
        self.sems = {}
        self.eng = {}
        for name, h in (("pe", nc.tensor), ("act", nc.scalar), ("dve", nc.vector),
                        ("pool", nc.gpsimd), ("sp", nc.sync)):
            sn = "e_" + name
            self.sems[sn] = es.enter_context(nc.semaphore(sn))
            self.eng[name] = dict(h=h, sn=sn, n=0, seen={})
        self.lw = {}
        self.rd = {}
        self.dcnt = {}
        self.hist = {}

    def _deps(self, reads, writes, nowaw=False):
        deps = []
        for k in reads:
            t = self.lw.get(k)
            if t:
                deps.append((t, True))
            if k.startswith("ps"):
                for t2 in self.rd.get(k, {}).items():
                    deps.append((t2, False))
        for k in writes:
            t = self.lw.get(k)
            if t and not nowaw:
                deps.append((t, False))
            for t2 in self.rd.get(k, {}).items():
                deps.append((t2, False))
        return deps

    def _emit_waits(self, e, deps, attach_ok=False):
        import itertools
        E = self.eng[e]
        need = {}
        for ((sn, v), raw) in deps:
            if sn == E["sn"] and e == "pe":
                continue
            if v > E["seen"].get(sn, 0) and v > need.get(sn, 0):
                need[sn] = v
        items = list(need.items())
        best = None
        perms = itertools.permutations(items) if len(items) <= 4 else [items]
        for order in perms:
            known = dict(E["seen"])
            res = []
            for sn, v in order:
                if known.get(sn, 0) >= v:
                    continue
                res.append((sn, v))
                known[sn] = v
                snap = self.hist.get((sn, v))
                if snap:
                    for k2, v2 in snap.items():
                        if v2 > known.get(k2, 0):
                            known[k2] = v2
            if best is None or len(res) < len(best[0]):
                best = (res, known)
        if best is None:
            return None
        items, known = best
        E["seen"] = known
        attach = None
        if attach_ok and items:
            attach = items.pop()
        for sn, v in items:
            E["h"].wait_ge(self.sems[sn], v)
        return attach

    def _record(self, tok, reads, writes):
        for k in reads:
            d = self.rd.setdefault(k, {})
            d[tok[0]] = max(d.get(tok[0], 0), tok[1])
        for k in writes:
            self.lw[k] = tok
            self.rd[k] = {}

    def op(self, e, fn, reads=(), writes=()):
        attach = self._emit_waits(e, self._deps(reads, writes), attach_ok=True)
        E = self.eng[e]
        ins = fn()
        if attach is not None:
            ins._wait_ge(self.sems[attach[0]], attach[1])
        E["n"] += 1
        ins.then_inc(self.sems[E["sn"]], 1)
        self.hist[(E["sn"], E["n"])] = dict(E["seen"])
        self._record((E["sn"], E["n"]), reads, writes)

    def dma(self, fn, semkey, reads=(), writes=(), nowaw=False):
        self._emit_waits("sp", self._deps(reads, writes, nowaw))
        sn = "d_" + semkey
        if sn not in self.sems:
            self.sems[sn] = self.es.enter_context(self.nc.semaphore(sn))
            self.dcnt[sn] = 0
        ins = fn()
        self.dcnt[sn] += 16
        ins.then_inc(self.sems[sn], 16)
        self._record((sn, self.dcnt[sn]), reads, writes)

    def barrier(self):
        toks = [(E["sn"], E["n"]) for E in self.eng.values() if E["n"] > 0]
        toks += [(sn, c) for sn, c in self.dcnt.items() if c > 0]
        for e in self.eng:
            self._emit_waits(e, [(t, True) for t in toks if t[0] != self.eng[e]["sn"]])

    def final(self):
        toks = [(sn, c) for sn, c in self.dcnt.items() if c > 0]
        toks += [(E["sn"], E["n"]) for n_, E in self.eng.items() if E["n"] > 0 and n_ != "sp"]
        self._emit_waits("sp", [(t, True) for t in toks])


class _Stop(Exception):
    pass


STOP = [None]


_DUMP = [None]


def _chk(name):
    if STOP[0] == name:
        if _DUMP[0] is not None:
            _DUMP[0]()
        raise _Stop()


def build_program(n_layers=DEPTH, dbg=()):
    nc = bass.Bass("TRN2", target_bir_lowering=False)
    es = ExitStack()
    T = Tr(nc, es)
    stopped = False
    try:
        _build_body(nc, es, T, n_layers, dbg)
    except _Stop:
        stopped = True
    T.final()
    print("instr counts:", {k: v["n"] for k, v in T.eng.items()}, "nsems", len(T.sems))
    if not stopped:
        es.close()
    return nc


def _build_body(nc, es, T, n_layers, dbg):

    _uid = [0]

    def sbt(name, shape, dt):
        _uid[0] += 1
        return nc.sbuf_tensor(f"{name}_{_uid[0]}", list(shape), dt)

    def dram(name, shape, dt=F32, kind="ExternalInput"):
        return nc.dram_tensor(name, list(shape), dt, kind=kind).ap()

    x_d = dram("x", [S, D])
    cbc_d = dram("c_bc", [P, KC, P])
    ccol_d = dram("c_col", [P, KC])
    wada_d = dram("w_ada", [DEPTH, D, 3 * D])
    bada_d = dram("b_ada", [DEPTH, 3 * D])
    bgate_d = dram("b_gate_bc", [DEPTH, P, D])
    wperm_d = dram("w_perm", [DEPTH, D, TOTC])
    bf_d = dram("b_f", [DEPTH, P, 8])
    peT_d = dram("cmp_peT", [DEPTH, 2, P, 32])
    w1_d = dram("cmp_w1r", [DEPTH, 2, P, 32 * 64])
    w2_d = dram("cmp_w2r", [DEPTH, 2, P, 64])
    wout_d = dram("w_out", [DEPTH, D, D])
    lng_d = dram("ln_g_bc", [DEPTH, P, D])
    lnb_d = dram("ln_b_bc", [DEPTH, P, D])
    cos64_d = dram("cos64", [P, S], BF16)
    sin64_d = dram("sin64", [P, S], BF16)
    cos32_d = dram("cos32", [P, S], BF16)
    sin32_d = dram("sin32", [P, S], BF16)
    cst_d = dram("cst", [P, 10, P])
    cstb_d = dram("cst_b", [P, 10, P], BF16)
    cmpm_d = dram("cmp_mask", [NT, P, P])
    fb_d = dram("sel_fb", [NT, P, 32])
    out_d = dram("out", [S, D], kind="ExternalOutput")
    x1_d = dram("x1_scratch", [S, D], kind="Internal")
    dbg_d = {}
    if "mix" in dbg:
        dbg_d["mix"] = dram("dbg_mix", [S, D], BF16, kind="ExternalOutput")

    def sb(name, shape, dt=F32):
        return es.enter_context(sbt(name, list(shape), dt))

    def pst(name):
        return es.enter_context(nc.psum_tensor(name, [P, 512], F32))

    PS = [pst(f"ps{i}") for i in range(8)]
    pskey = [f"ps{i}" for i in range(8)]

    uT = sb("uT", [P, KC, S], BF16)
    mix = sb("mix", [P, NT, D], BF16)
    cstf = sb("cstf", [P, 10, P], F32)
    ident_f = cstf[:, 0, :]
    causneg = cstf[:, 3, :]
    cstb = sb("cstb", [P, 10, P], BF16)
    ident_b = cstb[:, 0, :]
    caus01 = cstb[:, 1, :]
    band01 = cstb[:, 2, :]
    identbig = cstb[:, 4, :]
    ones_b = cstb[:, 5, :]
    perm64 = cstb[:, 7, :]
    perm32 = cstb[:, 8, :]
    shiftm = cstb[:, 9, :]
    cos64 = sb("cos64s", [P, S], BF16)
    sin64 = sb("sin64s", [P, S], BF16)
    cos32 = sb("cos32s", [P, S], BF16)
    sin32 = sb("sin32s", [P, S], BF16)
    modT = sb("modT", [P, DEPTH, 24], F32)
    g1bc = sb("g1bc", [P, DEPTH, D], F32)
    ccol = sb("ccol", [P, KC], F32)

    T.dma(lambda: nc.sync.dma_start(out=cstf[:], in_=cst_d[:, :, :]), "cstf", writes=["cstf"])
    T.dma(lambda: nc.sync.dma_start(out=cstb[:], in_=cstb_d[:, :, :]), "cstb", writes=["cstb"])
    T.dma(lambda: nc.sync.dma_start(out=ccol[:], in_=ccol_d[:, :]), "ccol", writes=["ccol"])

    xin_cm = sbt("xin", [P, NT, D], F32)
    xin = xin_cm.__enter__()
    for q4 in range(4):
        T.dma(lambda q4=q4: nc.sync.dma_start(out=xin[:, 4 * q4:4 * q4 + 4, :],
                                              in_=x_d[q4 * 512:(q4 + 1) * 512, :].rearrange("(t p) d -> p t d", p=P)),
              f"xin{q4}", writes=[f"xin{q4}"])

    for src, dst, nm in ((cos64_d, cos64, "cos64"), (sin64_d, sin64, "sin64"), (cos32_d, cos32, "cos32"), (sin32_d, sin32, "sin32")):
        T.dma(lambda src=src, dst=dst: nc.sync.dma_start(out=dst[:], in_=src[:, :]), nm, writes=[nm])

    with ExitStack() as es2:
        cbc = es2.enter_context(sbt("cbc", [P, KC, P], F32))
        wst = es2.enter_context(sbt("wadast", [P, 2, KC, 512], F32))
        brow = es2.enter_context(sbt("brow", [1, 512], F32))
        mrow = es2.enter_context(sbt("mrow", [1, 512], F32))
        bg = es2.enter_context(sbt("bgate", [P, D], F32))
        T.dma(lambda: nc.sync.dma_start(out=cbc[:], in_=cbc_d[:, :, :]), "cbc", writes=["cbc"])
        blk = 0
        for l in range(n_layers):
            T.dma(lambda l=l: nc.sync.dma_start(out=bg[:], in_=bgate_d[l, :, :]), "bg", writes=["bg"])
            for b6 in range(6):
                slot = blk % 2
                blk += 1
                src = wada_d[l, :, b6 * 512:(b6 + 1) * 512].rearrange("(kc k) n -> k kc n", k=P)
                T.dma(lambda src=src, slot=slot: nc.sync.dma_start(out=wst[:, slot, :, :], in_=src),
                      f"wst{slot}", writes=[f"wst{slot}"])
                if b6 < 4:
                    for kc in range(KC):
                        T.op("pe", lambda slot=slot, kc=kc: nc.tensor.matmul(
                            PS[3][0:1, :], lhsT=ccol[:, kc:kc + 1], rhs=wst[:, slot, kc, :],
                            start=(kc == 0), stop=(kc == KC - 1)),
                            reads=[f"wst{slot}", "ccol"], writes=["ps3"])
                    T.dma(lambda l=l, b6=b6: nc.sync.dma_start(out=brow[0:1, :], in_=bada_d[l:l + 1, b6 * 512:(b6 + 1) * 512]),
                          "brow", writes=["brow"])
                    T.op("dve", lambda: nc.vector.tensor_tensor(out=mrow[0:1, :], in0=PS[3][0:1, :], in1=brow[0:1, :], op=ALU.add),
                         reads=["ps3", "brow"], writes=["mrow"])
                    for f in range(4):
                        j = b6 * 4 + f
                        T.op("pe", lambda f=f, j=j: nc.tensor.matmul(
                            PS[0][:, j:j + 1], lhsT=mrow[0:1, f * P:(f + 1) * P], rhs=cstf[0:1, 0, 0:1],
                            start=True, stop=True),
                            reads=["mrow", "cstf"], writes=["ps0"])
                else:
                    g = b6 - 4
                    for kc in range(KC):
                        T.op("pe", lambda slot=slot, kc=kc, g=g: nc.tensor.matmul(
                            PS[1 + g][:, :], lhsT=cbc[:, kc, :], rhs=wst[:, slot, kc, :],
                            start=(kc == 0), stop=(kc == KC - 1)),
                            reads=[f"wst{slot}", "cbc"], writes=[pskey[1 + g]])
                    T.op("dve", lambda l=l, g=g: nc.vector.tensor_tensor(
                        out=g1bc[:, l, g * 512:(g + 1) * 512], in0=PS[1 + g][:, :],
                        in1=bg[:, g * 512:(g + 1) * 512], op=ALU.add),
                        reads=[pskey[1 + g], "bg"], writes=["g1bc"])
            T.op("dve", lambda l=l: nc.vector.tensor_copy(out=modT[:, l, 0:16], in_=PS[0][:, 0:16]),
                 reads=["ps0"], writes=["modT"])
            T.op("dve", lambda l=l: nc.vector.tensor_scalar(out=modT[:, l, 8:16], in0=modT[:, l, 8:16],
                                                            scalar1=1.0, scalar2=None, op0=ALU.add),
                 reads=["modT"], writes=["modT"])
            T.op("dve", lambda l=l: nc.vector.tensor_scalar(out=g1bc[:, l, :], in0=g1bc[:, l, :],
                                                            scalar1=1.0, scalar2=None, op0=ALU.add),
                 reads=["g1bc"], writes=["g1bc"])
        T.barrier()
    _chk("mod")

    def _dump():
        if "mix" in dbg:
            T.barrier()
            for i in range(NT):
                T.dma(lambda i=i: nc.sync.dma_start(out=dbg_d["mix"][i * P:(i + 1) * P, :], in_=mix[:, i, :]),
                      "store", reads=[f"mix{i}"], writes=[f"dbgmix{i}"])
    _DUMP[0] = _dump

    def make_uT_tile(src_tile, src_key, l, it, bank0=2):
        for half in range(2):
            bank = bank0 + half
            for q in range(4):
                kc = half * 4 + q
                T.op("pe", lambda kc=kc, q=q, bank=bank: nc.tensor.transpose(
                    PS[bank][:, q * P:(q + 1) * P], src_tile[:, kc * P:(kc + 1) * P], ident_f),
                    reads=[src_key, "cstf"], writes=[pskey[bank]])
            for q in range(4):
                kc = half * 4 + q
                if q % 2 == 0 or bank0 != 2:
                    T.op("act", lambda kc=kc, q=q, bank=bank: nc.scalar.activation(
                        out=uT[:, kc, it * P:(it + 1) * P], in_=PS[bank][:, q * P:(q + 1) * P],
                        func=AF.Identity, bias=modT[:, l, kc:kc + 1], scale=modT[:, l, 8 + kc:9 + kc]),
                        reads=[pskey[bank], "modT"], writes=[f"uT{it}"])
                else:
                    T.op("dve", lambda kc=kc, q=q, bank=bank: nc.vector.tensor_scalar(
                        out=uT[:, kc, it * P:(it + 1) * P], in0=PS[bank][:, q * P:(q + 1) * P],
                        scalar1=modT[:, l, 8 + kc:9 + kc], scalar2=modT[:, l, kc:kc + 1], op0=ALU.mult, op1=ALU.add),
                        reads=[pskey[bank], "modT"], writes=[f"uT{it}"])

    wslot_ctr = [0]

    wplan = []
    wready = {}

    def plan_w(names):
        wplan[:] = list(names)

    NSLOT = 2
    wdma = {}
    wcast_ctr = [0]

    def _dma_w(l, wst, name):
        slot = wslot_ctr[0] % NSLOT
        wslot_ctr[0] += 1
        o0, tot = UOFF[name]
        assert tot <= 128
        src = wperm_d[l, :, o0:o0 + tot].rearrange("(kc k) n -> k kc n", k=P)
        T.dma(lambda: nc.sync.dma_start(out=wst[:, slot, :, 0:tot], in_=src), f"wst{slot}", writes=[f"wst{slot}"])
        wdma[name] = (slot, tot)

    def _cast_w(wst, wbf, name):
        slot, tot = wdma.pop(name)
        bs = wcast_ctr[0] % 2
        wcast_ctr[0] += 1
        T.op("act", lambda: nc.scalar.copy(out=wbf[:, bs, :, 0:tot], in_=wst[:, slot, :, 0:tot]),
             reads=[f"wst{slot}"], writes=[f"wbf{bs}"])
        wready[name] = (bs, f"wbf{bs}", {name: (0, tot)})

    def load_w(l, wst, wbf, names):
        name = names[0]
        assert len(names) == 1
        if name not in wready:
            if name not in wdma:
                _dma_w(l, wst, name)
            _cast_w(wst, wbf, name)
        res = wready.pop(name)
        if wplan and wplan[0] == name:
            wplan.pop(0)
        for k_, nm in enumerate(wplan[:2]):
            if nm not in wready and nm not in wdma:
                _dma_w(l, wst, nm)
        if wplan and wplan[0] in wdma:
            _cast_w(wst, wbf, wplan[0])
        return res

    pbank = [0]

    def projT(wbf, slot, wkey, off, ncols, tc, bank):
        assert off == 0
        for kc in range(KC):
            T.op("pe", lambda kc=kc: nc.tensor.matmul(
                PS[bank][:, :], lhsT=wbf[:, slot, kc, :],
                rhs=uT[:, kc, tc * 512:(tc + 1) * 512], start=(kc == 0), stop=(kc == KC - 1)),
                reads=[wkey] + [f"uT{4 * tc + i}" for i in range(4)], writes=[pskey[bank]])

    def proj_plain(l, wst, wbf, name, dst, dkey):
        slot, wkey, offs = load_w(l, wst, wbf, [name])
        off, ncols = offs[name]
        for tc in range(4):
            bank = pbank[0] % 4
            pbank[0] += 1
            projT(wbf, slot, wkey, off, ncols, tc, bank)
            T.op("act", lambda tc=tc, bank=bank: nc.scalar.copy(
                out=dst[0:ncols, tc * 512:(tc + 1) * 512], in_=PS[bank][0:ncols, :]),
                reads=[pskey[bank]], writes=[dkey])

    def proj_rope(l, wst, wbf, name, dst, dkey, cosT, sinT, tkeys, tmp, raw_dst=None, raw_key=None, halves=False):
        slot, wkey, offs = load_w(l, wst, wbf, [name])
        perm = perm64 if tkeys[0] == "cos64" else perm32
        n_ = offs[name][1]
        banks = {}

        def stage1(tc):
            b0 = pbank[0] % 4
            b1 = (pbank[0] + 1) % 4
            pbank[0] += 2
            banks[tc] = (b0, b1)
            projT(wbf, slot, wkey, offs[name][0], n_, tc, b0)
            xs = tc % 2
            T.op("act", lambda: nc.scalar.copy(out=XB[:, xs, :], in_=PS[b0][:, :]),
                 reads=[pskey[b0]], writes=[f"XB{xs}"])

        def stage2(tc):
            b0, b1 = banks[tc]
            xs = tc % 2
            sl = slice(tc * 512, (tc + 1) * 512)
            T.op("pe", lambda: nc.tensor.matmul(PS[b1][:, :], lhsT=perm, rhs=XB[:, xs, :], start=True, stop=True),
                 reads=[f"XB{xs}", "cstb"], writes=[pskey[b1]])
            if raw_dst is not None:
                T.op("act", lambda: nc.scalar.copy(out=raw_dst[0:n_, sl], in_=XB[0:n_, xs, :]),
                     reads=[f"XB{xs}"], writes=[raw_key])
            T.op("dve", lambda: nc.vector.tensor_tensor(out=tmp[0:n_, 0, :], in0=PS[b0][0:n_, :], in1=cosT[0:n_, sl], op=ALU.mult),
                 reads=[pskey[b0], tkeys[0]], writes=["ropetmp0"])
            T.op("dve", lambda: nc.vector.tensor_tensor(out=tmp[0:n_, 1, :], in0=PS[b1][0:n_, :], in1=sinT[0:n_, sl], op=ALU.mult),
                 reads=[pskey[b1], tkeys[1]], writes=["ropetmp1"])
            if halves:
                T.op("dve", lambda: nc.vector.tensor_tensor(out=dst[0:64, 0, sl], in0=tmp[0:64, 0, :], in1=tmp[0:64, 1, :], op=ALU.add),
                     reads=["ropetmp0", "ropetmp1"], writes=[dkey])
                T.op("dve", lambda: nc.vector.tensor_tensor(out=dst[64:128, 1, sl], in0=tmp[64:128, 0, :], in1=tmp[64:128, 1, :], op=ALU.add),
                     reads=["ropetmp0", "ropetmp1"], writes=[dkey])
            else:
                T.op("dve", lambda: nc.vector.tensor_tensor(out=dst[0:n_, sl], in0=tmp[0:n_, 0, :], in1=tmp[0:n_, 1, :], op=ALU.add),
                     reads=["ropetmp0", "ropetmp1"], writes=[dkey])

        stage1(0)
        for tc in range(4):
            if tc + 1 < 4:
                stage1(tc + 1)
            stage2(tc)

    def proj_tok(l, wst, wbf, name, evac):
        slot, wkey, offs = load_w(l, wst, wbf, [name])
        off, ncols = offs[name]
        for g in range(4):
            bank = pbank[0] % 4
            pbank[0] += 1
            for q in range(4):
                it = 4 * g + q
                for kc in range(KC):
                    T.op("pe", lambda kc=kc, q=q, it=it: nc.tensor.matmul(
                        PS[bank][:, q * P:q * P + ncols], lhsT=uT[:, kc, it * P:(it + 1) * P],
                        rhs=wbf[:, slot, kc, off:off + ncols], start=(kc == 0), stop=(kc == KC - 1)),
                        reads=[wkey, f"uT{it}"], writes=[pskey[bank]])
            view = PS[bank][:, :].rearrange("p (q c) -> p q c", q=4)[:, :, 0:ncols]
            evac(g, view, pskey[bank])

    acc_ctr = [0]
    sb_ctr = [0]

    def attn_core(PT, i, ktiles, qk, extra, vext, vkey, finish):
        for _ in attn_core_g(PT, i, ktiles, qk, extra, vext, vkey, finish):
            pass

    def interleave(a, b):
        a = iter(a)
        b = iter(b)
        da = db = False
        while not (da and db):
            if not da:
                try:
                    next(a)
                except StopIteration:
                    da = True
            if not db:
                try:
                    next(b)
                except StopIteration:
                    db = True

    def _ov(out_ap, rhs):
        if len(rhs.shape) == 3:
            return out_ap.rearrange("p (h t) -> p h t", h=rhs.shape[1])
        return out_ap

    pending_fin = []

    def flush_fin():
        while pending_fin:
            pending_fin.pop(0)()

    def attn_wide_g(PT, ncol, ktl, qk, extra, vext, vkey, finish):
        abank = 6 + (acc_ctr[0] % 2)
        acc_ctr[0] += 1
        nk = len(ktl)
        slots = []

        def scores(n):
            j, c0 = ktl[n]
            sbank = 3 + (sb_ctr[0] % 3)
            ptslot = sb_ctr[0] % 3
            sb_ctr[0] += 1
            slots.append(ptslot)
            lhsT, rhs, keys = qk(j, c0)
            ex = extra(j, c0)
            T.op("pe", lambda: nc.tensor.matmul(_ov(PS[sbank][:, c0:ncol], rhs), lhsT=lhsT, rhs=rhs, start=True,
                                                stop=(len(ex) == 0)),
                 reads=keys, writes=[pskey[sbank]])
            for n_, (l2, r2, k2, (e0, e1)) in enumerate(ex):
                T.op("pe", lambda: nc.tensor.matmul(_ov(PS[sbank][:, e0:e1], r2), lhsT=l2, rhs=r2, start=False,
                                                    stop=(n_ == len(ex) - 1)),
                     reads=k2, writes=[pskey[sbank]])
            T.op("act", lambda: nc.scalar.activation(out=PT[:, ptslot, c0:ncol], in_=PS[sbank][:, c0:ncol], func=AF.Exp, scale=0.125),
                 reads=[pskey[sbank]], writes=[f"PT{ptslot}"])

        def pv(n):
            j, c0 = ktl[n]
            ptslot = slots[n]
            T.op("pe", lambda: nc.tensor.matmul(PS[abank][0:65, c0:ncol], lhsT=vext(j), rhs=PT[:, ptslot, c0:ncol],
                                                start=(n == 0), stop=(n == nk - 1)),
                 reads=[f"PT{ptslot}", vkey], writes=[pskey[abank]])

        scores(0)
        if nk > 1:
            scores(1)
        for n in range(nk):
            if n + 2 < nk:
                scores(n + 2)
            pv(n)
            if n == 0:
                flush_fin()
            yield
        finish(abank)
        yield

    ot_ctr = [0]
    fw_banks = [[0, 1, 2]]

    def finish_wide(abank, ncol, nb, emit_block):
        os_ = ot_ctr[0] % 2
        c = ot_ctr[0] % 8
        ot_ctr[0] += 1
        T.op("dve", lambda: nc.vector.tensor_copy(out=OTs[0:65, os_, 0:ncol], in_=PS[abank][0:65, 0:ncol]),
             reads=[pskey[abank]], writes=[f"OTs{os_}"])

        def part2():
            tb = fw_banks[0][pbank[0] % len(fw_banks[0])]
            pbank[0] += 1
            for b in range(nb):
                T.op("pe", lambda: nc.tensor.transpose(PS[tb][:, b * 65:(b + 1) * 65], OTs[0:65, os_, b * P:(b + 1) * P],
                                                       ident_f[0:65, 0:65]),
                     reads=[f"OTs{os_}", "cstf"], writes=[pskey[tb]])
            T.op("dve", lambda: nc.vector.reciprocal(out=rsm[:, 4 * c:4 * c + nb], in_=PS[tb][:, 64:64 + 65 * (nb - 1) + 1:65]),
                 reads=[pskey[tb]], writes=[f"rsm{c}"])
            for b in range(nb):
                emit_block(b, PS[tb][:, b * 65:b * 65 + 64], pskey[tb], rsm[:, 4 * c + b:4 * c + b + 1], f"rsm{c}", 4 * c + b)
        pending_fin.append(part2)

    def attn_core_g(PT, i, ktiles, qk, extra, vext, vkey, finish):
        abank = 6 + (acc_ctr[0] % 2)
        acc_ctr[0] += 1
        nk = len(ktiles)
        for g0 in range(0, nk, 4):
            grp = ktiles[g0:g0 + 4]
            sbank = 4 + (sb_ctr[0] % 2)
            ptslot = sb_ctr[0] % 2
            sb_ctr[0] += 1
            for q, j in enumerate(grp):
                lhsT, rhs, keys = qk(j)
                ex = extra(j)
                T.op("pe", lambda lhsT=lhsT, rhs=rhs, q=q, ex=ex: nc.tensor.matmul(
                    PS[sbank][:, q * P:(q + 1) * P], lhsT=lhsT, rhs=rhs, start=True, stop=(len(ex) == 0)),
                    reads=keys, writes=[pskey[sbank]])
                for n_, (l2, r2, k2) in enumerate(ex):
                    T.op("pe", lambda l2=l2, r2=r2, q=q, n_=n_, ex=ex: nc.tensor.matmul(
                        PS[sbank][:, q * P:(q + 1) * P], lhsT=l2, rhs=r2, start=False, stop=(n_ == len(ex) - 1)),
                        reads=k2, writes=[pskey[sbank]])
            w = len(grp) * P
            T.op("act", lambda w=w, ptslot=ptslot, sbank=sbank: nc.scalar.activation(
                out=PT[:, ptslot, 0:w], in_=PS[sbank][:, 0:w], func=AF.Exp, scale=0.125),
                reads=[pskey[sbank]], writes=[f"PT{ptslot}"])
            for q, j in enumerate(grp):
                first = (g0 == 0 and q == 0)
                last = (g0 + q == nk - 1)
                T.op("pe", lambda q=q, j=j, first=first, last=last, ptslot=ptslot: nc.tensor.matmul(
                    PS[abank][:, 0:65], lhsT=PT[:, ptslot, q * P:(q + 1) * P], rhs=vext(j),
                    start=first, stop=last),
                    reads=[f"PT{ptslot}", vkey], writes=[pskey[abank]])
            yield
        finish(PS[abank][:, 0:65], pskey[abank])
        yield

    xt_ctr = [0]
    for l in range(n_layers):
        if l == 0:
            for it in range(NT):
                make_uT_tile(xin[:, it, :], f"xin{it // 4}", 0, it)
            T.barrier()
            xin_cm.__exit__(None, None, None)
            _chk("uT")

        with ExitStack() as esl:
            wst = esl.enter_context(sbt("wst", [P, 2, KC, 128], F32))
            wbf = esl.enter_context(sbt("wbf", [P, 2, KC, 128], BF16))
            T.op("pool", lambda: nc.gpsimd.memset(wbf[:], 0.0), writes=["wbf0", "wbf1"])
            PT = esl.enter_context(sbt("PT", [P, 3, 512], BF16))
            rsm = esl.enter_context(sbt("rsm", [P, 32], F32))
            OTs = esl.enter_context(sbt("OTs", [P, 2, 512], F32))
            XB = esl.enter_context(sbt("XB", [P, 2, 512], BF16))
            ID4 = esl.enter_context(sbt("ID4", [P, 4, P], BF16))
            for q4 in range(4):
                T.op("pool", lambda: nc.gpsimd.tensor_copy(out=ID4[:, q4, :], in_=identbig), reads=["cstb"], writes=["ID4"])

            with ExitStack() as espf:
              FQp = espf.enter_context(sbt("FQp", [P, 6, S], BF16))
              FKp = espf.enter_context(sbt("FKp", [P, 6, S], BF16))
              if True:
                esp = espf
                CC = esp.enter_context(sbt("CC", [6, 2, S], F32))
                SP3 = esp.enter_context(sbt("SP3", [6, 3, S], BF16))
                bfn = esp.enter_context(sbt("bfn", [P, 8], F32))
                FV = esp.enter_context(sbt("FV", [P, NT, 6, 65], BF16))
                plan_w(["ff"] + [f"{p}{i}" for i in range(3) for p in ("fq", "fk", "fv")])
                T.op("pool", lambda: nc.gpsimd.memset(FKp[64:128, :, :], -1.0), writes=["FKpa"])
                T.op("pool", lambda: nc.gpsimd.memset(FQp[64:128, :, :], 0.0), writes=["FQpa"])
                T.op("pool", lambda: nc.gpsimd.memset(FQp[64:67, :, :], 1.0), writes=["FQpa"])
                _chk("f0")
                T.dma(lambda: nc.sync.dma_start(out=bfn[:], in_=bf_d[l, :, :]), "bfn", writes=["bfn"])
                _chk("f1")
                T.op("dve", lambda: nc.vector.tensor_scalar(out=bfn[:], in0=bfn[:], scalar1=-1.0, scalar2=None, op0=ALU.mult),
                     reads=["bfn"], writes=["bfn"])
                _chk("fa")
                slot, wkey, offs = load_w(l, wst, wbf, ["ff"])
                _chk("fb")
                for tc in range(4):
                    bank = pbank[0] % 4
                    pbank[0] += 1
                    projT(wbf, slot, wkey, offs["ff"][0], 6, tc, bank)
                    if tc == 0:
                        _chk("fc")
                    sl = slice(tc * 512, (tc + 1) * 512)
                    T.op("act", lambda: nc.scalar.activation(out=CC[:, 0, sl], in_=PS[bank][0:6, :], func=AF.Exp,
                                                            bias=bfn[0:6, 0:1], scale=-1.0),
                         reads=[pskey[bank], "bfn"], writes=["CC0"])
                    if tc == 0:
                        _chk("fd")
                    T.op("act", lambda: nc.scalar.activation(out=CC[:, 0, sl], in_=CC[:, 0, sl], func=AF.Ln,
                                                            bias=1.0, scale=1.0),
                         reads=["CC0"], writes=["CC0"])
                _chk("fp1")
                T.op("dve", lambda: nc.vector.tensor_tensor_scan(out=CC[:, 1, :], data0=CC[:, 0, :], data1=CC[:, 0, :],
                                                                 initial=0.0, op0=ALU.add, op1=ALU.max),
                     reads=["CC0"], writes=["CC1"])
                T.op("dve", lambda: nc.vector.tensor_scalar(out=CC[:, 1, :], in0=CC[:, 1, :], scalar1=8.0, scalar2=None, op0=ALU.mult),
                     reads=["CC1"], writes=["CC1"])
                cur = 1
                for r in range(3):
                    T.op("dve", lambda: nc.vector.tensor_copy(out=SP3[:, r, :], in_=CC[:, cur, :]),
                         reads=[f"CC{cur}"], writes=[f"SP{r}"])
                    if r < 2:
                        nxt = 1 - cur
                        T.op("dve", lambda: nc.vector.tensor_tensor(
                            out=CC[:, nxt, :], in0=CC[:, cur, :], in1=SP3[:, r, :], op=ALU.subtract),
                            reads=[f"CC{cur}", f"SP{r}"], writes=[f"CC{nxt}"])
                        cur = nxt
                _chk("fp2")
                _chk("fp3")
                _chk("foxpre")
                T.op("pool", lambda: nc.gpsimd.memset(FV[:, :, :, 64:65], 1.0), writes=["FV"])

                def proj_pair(name, dstp, dkey, i):
                    slot, wkey, offs = load_w(l, wst, wbf, [name])
                    off, ncols = offs[name]
                    banks = {}

                    def st1(tc):
                        b0 = pbank[0] % 4
                        b1 = (pbank[0] + 1) % 4
                        pbank[0] += 2
                        banks[tc] = (b0, b1)
                        projT(wbf, slot, wkey, off, ncols, tc, b0)
                        xs = tc % 2
                        T.op("act", lambda: nc.scalar.copy(out=XB[:, xs, :], in_=PS[b0][:, :]),
                             reads=[pskey[b0]], writes=[f"XB{xs}"])

                    def st2(tc):
                        b0, b1 = banks[tc]
                        sl = slice(tc * 512, (tc + 1) * 512)
                        xs = tc % 2
                        T.op("pe", lambda: nc.tensor.matmul(PS[b1][0:64, :], lhsT=shiftm[:, 0:64], rhs=XB[:, xs, :], start=True, stop=True),
                             reads=[f"XB{xs}", "cstb"], writes=[pskey[b1]])
                        T.op("dve", lambda: nc.vector.tensor_copy(out=dstp[0:64, 2 * i, sl], in_=XB[0:64, xs, :]),
                             reads=[f"XB{xs}"], writes=[dkey])
                        T.op("act", lambda: nc.scalar.copy(out=dstp[0:64, 2 * i + 1, sl], in_=PS[b1][0:64, :]),
                             reads=[pskey[b1]], writes=[dkey])

                    st1(0)
                    for tc in range(4):
                        if tc + 1 < 4:
                            st1(tc + 1)
                        st2(tc)

                for i in range(3):
                    proj_pair(f"fq{i}", FQp, "FQp", i)
                    proj_pair(f"fk{i}", FKp, "FKp", i)

                    def ev(g, view, pk, i=i):
                        T.op("act", lambda: nc.scalar.copy(
                            out=FV[:, 4 * g:4 * g + 4, 2 * i:2 * i + 2, 0:64],
                            in_=view.rearrange("p q (h d) -> p q h d", h=2)),
                            reads=[pk], writes=["FV"])
                    proj_tok(l, wst, wbf, f"fv{i}", ev)
                for h in range(6):
                    for r in range(3):
                        T.dma(lambda: nc.sync.dma_start(out=FKp[64 + r:65 + r, h, :], in_=SP3[h:h + 1, r, :]),
                              "FKpa", reads=[f"SP{r}"], writes=["FKpa"], nowaw=(h + r > 0))
                        T.dma(lambda: nc.sync.dma_start(out=FQp[67 + r:68 + r, h, :], in_=SP3[h:h + 1, r, :]),
                              "FQpa", reads=[f"SP{r}"], writes=["FQpa"], nowaw=(h + r > 0))
                for c4 in range(4):
                    for h in range(6):
                        hp = 64 * (h % 2)
                        pb = 32 * (h % 3)
                        ktl = [(j, 0) for j in range(4 * c4)] + [(4 * c4 + jl, P * jl) for jl in range(4)]
                        q0 = c4 * 512

                        def qk(j, c0, h=h, q0=q0):
                            return (FKp[:, h, j * P:(j + 1) * P], FQp[:, h, q0 + c0:q0 + 512],
                                    ["FKp", "FQp", "FKpa", "FQpa"])

                        def extra(j, c0, c4=c4):
                            if j >= 4 * c4:
                                return [(caus01, identbig, ["cstb"], (c0, c0 + P))]
                            return []

                        def fin(abank, h=h, c4=c4):
                            def blk(b, src, skey, rs, rkey, rcol=None):
                                T.op("dve", lambda: nc.vector.tensor_scalar(
                                    out=mix[:, 4 * c4 + b, 256 + 64 * h:256 + 64 * (h + 1)], in0=src,
                                    scalar1=rs, scalar2=None, op0=ALU.mult),
                                    reads=[skey, rkey], writes=[f"mix{4 * c4 + b}"])
                            finish_wide(abank, 512, 4, blk)
                        for _ in attn_wide_g(PT, 512, ktl, qk, extra, lambda j, h=h: FV[:, j, h, :], "FV", fin):
                            pass
                    if c4 == 0:
                        _chk("fox0")
                flush_fin()
                T.barrier()
            _chk("fox")

            with ExitStack() as esp:
                DQ = esp.enter_context(sbt("DQ", [P, 4, S], BF16))
                DK = esp.enter_context(sbt("DK", [P, S], BF16))
                IQ = esp.enter_context(sbt("IQ", [P, 3, S], BF16))
                IK = esp.enter_context(sbt("IK", [P, 3, S], BF16))
                DV = esp.enter_context(sbt("DV", [P, NT, 65], BF16))
                IW = esp.enter_context(sbt("IW", [P, NT, 3, 8], F32))
                rtmp = esp.enter_context(sbt("rtmp", [P, 2, 512], F32))
                ISC = esp.enter_context(sbt("ISC", [P, 2, S], F32))
                WKb = esp.enter_context(sbt("WKb", [P, S], BF16))
                MB = esp.enter_context(sbt("MB", [P, 2, S], BF16))
                RL = esp.enter_context(sbt("RL", [P, 2, 512], F32))
                bis = esp.enter_context(sbt("bis", [P, 2, 8], F32))
                Hp = esp.enter_context(sbt("Hp", [P, 2, 24], F32))
                Hn = esp.enter_context(sbt("Hn", [P, 2, 24], F32))
                WKa = rtmp[:].rearrange("p a b -> p (a b)").bitcast(BF16)
                T.op("pool", lambda: nc.gpsimd.memset(DV[:, :, 64:65], 1.0), writes=["DV"])
                _chk("d0")
                plan_w(["dq0", "dq1", "dq2", "dq3", "iq0", "iq1", "iq2", "dk", "ik", "dvw"])
                T.op("pool", lambda: nc.gpsimd.memset(DQ[64:128, :, :], 0.0), writes=["DQ"])
                T.op("pool", lambda: nc.gpsimd.memset(DK[64:128, :], 0.0), writes=["DK"])
                for i in range(4):
                    proj_rope(l, wst, wbf, f"dq{i}", DQ[:, i, :], "DQ", cos64, sin64, ("cos64", "sin64"), rtmp)
                _chk("d1")
                T.op("pool", lambda: nc.gpsimd.memset(IQ[:], 0.0), writes=["IQ0", "IQ1", "IQ2"])
                for i in range(3):
                    proj_rope(l, wst, wbf, f"iq{i}", IQ[:, i, :], f"IQ{i}", cos32, sin32, ("cos32", "sin32"), rtmp)
                _chk("d2")
                proj_rope(l, wst, wbf, "dk", DK, "DK", cos64, sin64, ("cos64", "sin64"), rtmp)
                proj_rope(l, wst, wbf, "ik", IK[:, 0, :], "IK", cos32, sin32, ("cos32", "sin32"), rtmp)
                T.op("pool", lambda: nc.gpsimd.memset(IK[:, 1, :], 0.0), writes=["IK"])
                T.op("pool", lambda: nc.gpsimd.memset(IK[:, 2, :], 0.0), writes=["IK"])
                T.op("pool", lambda: nc.gpsimd.tensor_copy(out=IK[32:64, 1, :], in_=IK[32:64, 0, :]), reads=["IK"], writes=["IK"])
                T.op("pool", lambda: nc.gpsimd.tensor_copy(out=IK[64:96, 2, :], in_=IK[64:96, 0, :]), reads=["IK"], writes=["IK"])
                T.op("pool", lambda: nc.gpsimd.memset(IK[32:64, 0, :], 0.0), reads=["IK"], writes=["IK"])
                T.op("pool", lambda: nc.gpsimd.memset(IK[64:128, 0, :], 0.0), reads=["IK"], writes=["IK"])
                _chk("d3")

                def ev(g, view, pk):
                    T.op("act", lambda: nc.scalar.copy(out=DV[:, 4 * g:4 * g + 4, 0:64], in_=view[:, :, 0:64]),
                         reads=[pk], writes=["DV"])
                    T.op("dve", lambda: nc.vector.tensor_copy(out=IW[:, 4 * g:4 * g + 4, 0, :], in_=view[:, :, 64:72]),
                         reads=[pk], writes=["IW"])
                proj_tok(l, wst, wbf, "dvw", ev)
                _chk("d4")
                T.op("dve", lambda: nc.vector.tensor_scalar(out=IW[:, :, 2, :], in0=IW[:, :, 0, :], scalar1=0.0, scalar2=2.0,
                                                            op0=ALU.is_ge, op1=ALU.mult),
                     reads=["IW"], writes=["IW"])
                T.op("dve", lambda: nc.vector.tensor_scalar(out=IW[:, :, 2, :], in0=IW[:, :, 2, :], scalar1=-1.0, scalar2=None,
                                                            op0=ALU.add),
                     reads=["IW"], writes=["IW"])
                T.op("dve", lambda: nc.vector.tensor_tensor(out=IW[:, :, 1, :], in0=IW[:, :, 0, :], in1=IW[:, :, 2, :], op=ALU.mult),
                     reads=["IW"], writes=["IW"])
                rl_ctr = [0]
                K_BIS = 16
                _chk("dsa_p")

                def indexer(i):
                    b_ = i % 2
                    L = (i + 1) * P
                    ik_ = f"ISC{b_}"
                    for c0 in range(0, L, 512):
                        w = min(512, L - c0)
                        for h in range(8):
                            bank = pbank[0] % 3
                            pbank[0] += 1
                            T.op("pe", lambda: nc.tensor.matmul(
                                PS[bank][:, 0:w], lhsT=IQ[:, h // 3, i * P:(i + 1) * P],
                                rhs=IK[:, h % 3, c0:c0 + w], start=True, stop=True),
                                reads=[f"IQ{h // 3}", "IK"], writes=[pskey[bank]])
                            rs = rl_ctr[0] % 2
                            rl_ctr[0] += 1
                            T.op("act", lambda: nc.scalar.activation(
                                out=RL[:, rs, 0:w], in_=PS[bank][:, 0:w], func=AF.Relu, scale=IW[:, i, 1, h:h + 1]),
                                reads=[pskey[bank], "IW"], writes=[f"RL{rs}"])
                            if h == 0:
                                T.op("dve", lambda: nc.vector.tensor_scalar(
                                    out=ISC[:, b_, c0:c0 + w], in0=RL[:, rs, 0:w], scalar1=IW[:, i, 2, h:h + 1], scalar2=None,
                                    op0=ALU.mult), reads=[f"RL{rs}", "IW"], writes=[ik_])
                            else:
                                T.op("dve", lambda: nc.vector.scalar_tensor_tensor(
                                    out=ISC[:, b_, c0:c0 + w], in0=RL[:, rs, 0:w], scalar=IW[:, i, 2, h:h + 1],
                                    in1=ISC[:, b_, c0:c0 + w], op0=ALU.mult, op1=ALU.add),
                                    reads=[f"RL{rs}", "IW", ik_], writes=[ik_])
                    T.op("dve", lambda: nc.vector.tensor_reduce(out=bis[:, b_, 0:1], in_=ISC[:, b_, 0:L], axis=AX.X, op=ALU.max,
                                                                apply_absolute_value=True),
                         reads=[ik_], writes=[f"bisM{b_}"])
                    T.op("dve", lambda: nc.vector.tensor_scalar(out=bis[:, b_, 0:1], in0=bis[:, b_, 0:1], scalar1=1.001, scalar2=1e-3,
                                                                op0=ALU.mult, op1=ALU.add),
                         reads=[f"bisM{b_}"], writes=[f"bisM{b_}"])
                    T.op("dve", lambda: nc.vector.tensor_scalar(out=Hp[:, b_, :], in0=cstf[:, 6, 0:24], scalar1=bis[:, b_, 0:1],
                                                                scalar2=None, op0=ALU.mult),
                         reads=[f"bisM{b_}", "cstf"], writes=[f"Hp{b_}"])
                    T.op("dve", lambda: nc.vector.memset(bis[:, b_, 1:2], 0.0), writes=[f"bismid{b_}"])
                    T.op("dve", lambda: nc.vector.tensor_scalar(out=Hn[:, b_, :], in0=Hp[:, b_, :], scalar1=-0.5, scalar2=None,
                                                                op0=ALU.mult),
                         reads=[f"Hp{b_}"], writes=[f"Hn{b_}"])
                    T.op("dve", lambda: nc.vector.memset(bis[:, b_, 5:6], 0.0), writes=[f"bisnm{b_}"])
                    T.op("dve", lambda: nc.vector.tensor_tensor(out=ISC[:, b_, i * P:L], in0=ISC[:, b_, i * P:L], in1=causneg, op=ALU.add),
                         reads=[ik_, "cstf"], writes=[ik_])

                def topk_g(i, eng="dve"):
                    b_ = i % 2
                    L = (i + 1) * P
                    ik_ = f"ISC{b_}"
                    if eng == "act":
                        for k in range(K_BIS):
                            T.op("act", lambda: nc.scalar.activation(
                                out=WKa[:, 0:L], in_=ISC[:, b_, 0:L], func=AF.Sign, bias=bis[:, b_, 5:6], scale=1.0,
                                accum_out=bis[:, b_, 2:3]),
                                reads=[ik_, f"bisnm{b_}"], writes=["ropetmp0", "ropetmp1", f"biscnt{b_}"])
                            T.op("act", lambda: nc.scalar.activation(
                                out=bis[:, b_, 6:7], in_=bis[:, b_, 2:3], func=AF.Sign, bias=float(L - 512) + 0.5, scale=1.0),
                                reads=[f"biscnt{b_}"], writes=[f"bissg{b_}"])
                            T.op("act", lambda: nc.scalar.activation(
                                out=bis[:, b_, 5:6], in_=bis[:, b_, 6:7], func=AF.Identity, scale=Hn[:, b_, k:k + 1],
                                bias=bis[:, b_, 5:6]),
                                reads=[f"bissg{b_}", f"Hn{b_}", f"bisnm{b_}"], writes=[f"bisnm{b_}"])
                            yield
                        T.op("act", lambda: nc.scalar.activation(
                            out=bis[:, b_, 4:5], in_=bis[:, b_, 5:6], func=AF.Identity, scale=-1.0,
                            bias=Hn[:, b_, K_BIS - 1:K_BIS]),
                            reads=[f"bisnm{b_}", f"Hn{b_}"], writes=[f"bisthr{b_}"])
                        T.op("dve", lambda: nc.vector.tensor_scalar(out=MB[:, b_, 0:L], in0=ISC[:, b_, 0:L], scalar1=bis[:, b_, 4:5],
                                                                    scalar2=-1.0, op0=ALU.is_ge, op1=ALU.add),
                             reads=[ik_, f"bisthr{b_}"], writes=[f"MB{b_}"])
                        yield
                        return
                    for k in range(K_BIS):
                        T.op("dve", lambda: nc.vector.tensor_scalar(
                            out=WKb[:, 0:L], in0=ISC[:, b_, 0:L], scalar1=bis[:, b_, 1:2], scalar2=None,
                            op0=ALU.is_ge, op1=ALU.add, accum_out=bis[:, b_, 2:3]),
                            reads=[ik_, f"bismid{b_}"], writes=["WKb", f"biscnt{b_}"])
                        T.op("dve", lambda: nc.vector.tensor_scalar(
                            out=bis[:, b_, 3:4], in0=bis[:, b_, 2:3], scalar1=256.0, scalar2=-0.5, op0=ALU.is_ge, op1=ALU.add),
                            reads=[f"biscnt{b_}"], writes=[f"biscm{b_}"])
                        T.op("dve", lambda: nc.vector.scalar_tensor_tensor(
                            out=bis[:, b_, 1:2], in0=bis[:, b_, 3:4], scalar=Hp[:, b_, k:k + 1], in1=bis[:, b_, 1:2],
                            op0=ALU.mult, op1=ALU.add),
                            reads=[f"biscm{b_}", f"Hp{b_}", f"bismid{b_}"], writes=[f"bismid{b_}"])
                        yield
                    T.op("dve", lambda: nc.vector.tensor_tensor(out=bis[:, b_, 4:5], in0=bis[:, b_, 1:2],
                                                                in1=Hp[:, b_, K_BIS:K_BIS + 1], op=ALU.subtract),
                         reads=[f"bismid{b_}", f"Hp{b_}"], writes=[f"bisthr{b_}"])
                    T.op("dve", lambda: nc.vector.tensor_scalar(out=MB[:, b_, 0:L], in0=ISC[:, b_, 0:L], scalar1=bis[:, b_, 4:5],
                                                                scalar2=-1.0, op0=ALU.is_ge, op1=ALU.add),
                         reads=[ik_, f"bisthr{b_}"], writes=[f"MB{b_}"])
                    yield

                def dsa_attn_g(i):
                    b_ = i % 2
                    ktl = [(j, 0) for j in range(i + 1)]

                    def qk(j, c0):
                        return (DK[:, j * P:(j + 1) * P], DQ[:, :, i * P:(i + 1) * P], ["DK", "DQ"])

                    def extra(j, c0):
                        if i >= 2:
                            return [(MB[:, b_, j * P:(j + 1) * P], ID4[:, :, :], [f"MB{b_}", "ID4"], (0, 512))]
                        if j == i:
                            return [(caus01, ID4[:, :, :], ["cstb", "ID4"], (0, 512))]
                        return []

                    def fin(abank):
                        def blk(b, src, skey, rs, rkey, rcol=None):
                            T.op("dve", lambda: nc.vector.tensor_scalar(
                                out=mix[:, i, 64 * b:64 * (b + 1)], in0=src, scalar1=rs, scalar2=None, op0=ALU.mult),
                                reads=[skey, rkey], writes=[f"mix{i}"])
                        finish_wide(abank, 512, 4, blk)
                    yield from attn_wide_g(PT, 512, ktl, qk, extra, lambda j: DV[:, j, :], "DV", fin)

                gens = {}

                def step(g_):
                    if g_ is None:
                        return True
                    try:
                        next(g_)
                        return False
                    except StopIteration:
                        return True

                for i in range(NT):
                    if i == 3:
                        _chk("dsa_i3")
                    if 2 <= i + 2 < NT:
                        indexer(i + 2)
                        gens[i + 2] = topk_g(i + 2, "dve" if (i % 2 == 0) else "act")
                    A_ = dsa_attn_g(i)
                    B1 = gens.pop(i + 1, None)
                    B2 = gens.get(i + 2)
                    dA = dB1 = False
                    c2 = 0
                    n2 = K_BIS // 2 if B2 is not None else 0
                    while not (dA and dB1 and c2 >= n2):
                        if not dA:
                            dA = step(A_)
                        if not dB1:
                            dB1 = step(B1)
                        if c2 < n2:
                            step(B2)
                            c2 += 1
                flush_fin()
                T.barrier()
            _chk("dsa")

            with ExitStack() as esp:
                NQ = esp.enter_context(sbt("NQ", [P, 3, S], BF16))
                NQR = esp.enter_context(sbt("NQR", [P, 3, S], BF16))
                KST = esp.enter_context(sbt("KST", [P, 2, S], BF16))
                KWT = esp.enter_context(sbt("KWT", [P, 2, S], BF16))
                VS = esp.enter_context(sbt("VS", [P, NT, 2, 65], BF16))
                VW = esp.enter_context(sbt("VW", [P, NT, 2, 65], BF16))
                NG = esp.enter_context(sbt("NG", [P, NT, 18], F32))
                KCC = esp.enter_context(sbt("KCC", [P, 2, P], BF16))
                VCC = esp.enter_context(sbt("VCC", [P, 2, 64], BF16))
                esq = ExitStack()
                KCT = esq.enter_context(sbt("KCT", [P, S], BF16))
                VCT = esq.enter_context(sbt("VCT", [P, S], BF16))
                rtmp = esq.enter_context(sbt("rtmp2", [P, 2, 512], F32))
                W1f = esq.enter_context(sbt("W1f", [P, 2048], F32))
                W1b = esq.enter_context(sbt("W1b", [P, 2048], BF16))
                W2f = esq.enter_context(sbt("W2f", [P, 2, 64], F32))
                W2b = esq.enter_context(sbt("W2b", [P, 2, 64], BF16))
                peTf = esq.enter_context(sbt("peTf", [P, 2, 32], F32))
                peTb = esq.enter_context(sbt("peTb", [P, 2, 32], BF16))
                cb = esq.enter_context(sbt("cmpbias", [P, 2], F32))
                HT = esq.enter_context(sbt("HT", [P, 2, P], BF16))
                T.op("pool", lambda: nc.gpsimd.memset(KCC[:], 0.0), writes=["KCC"])
                plan_w(["nq0", "nq1", "nq2", "nks", "nkw", "nkc", "nvc", "nvs", "nvw", "ng"])
                T.op("pool", lambda: nc.gpsimd.memset(VS[:, :, :, 64:65], 1.0), writes=["VS"])
                T.op("pool", lambda: nc.gpsimd.memset(VW[:, :, :, 64:65], 1.0), writes=["VW"])
                for i in range(3):
                    proj_rope(l, wst, wbf, f"nq{i}", NQR[:, i, :], f"NQR{i}", cos64, sin64, ("cos64", "sin64"), rtmp,
                              raw_dst=NQ[:, i, :], raw_key=f"NQ{i}")
                T.op("pool", lambda: nc.gpsimd.memset(KST[64:128, 0, :], 0.0), writes=["KST"])
                T.op("pool", lambda: nc.gpsimd.memset(KST[0:64, 1, :], 0.0), writes=["KST"])
                T.op("pool", lambda: nc.gpsimd.memset(KWT[64:128, 0, :], 0.0), writes=["KWT"])
                T.op("pool", lambda: nc.gpsimd.memset(KWT[0:64, 1, :], 0.0), writes=["KWT"])
                proj_rope(l, wst, wbf, "nks", KST, "KST", cos64, sin64, ("cos64", "sin64"), rtmp, halves=True)
                proj_rope(l, wst, wbf, "nkw", KWT, "KWT", cos64, sin64, ("cos64", "sin64"), rtmp, halves=True)
                proj_plain(l, wst, wbf, "nkc", KCT, "KCT")
                proj_plain(l, wst, wbf, "nvc", VCT, "VCT")
                for nm, dst, dk_ in (("nvs", VS, "VS"), ("nvw", VW, "VW")):
                    def ev(g, view, pk, dst=dst, dk_=dk_):
                        T.op("act", lambda: nc.scalar.copy(out=dst[:, 4 * g:4 * g + 4, :, 0:64],
                                                           in_=view.rearrange("p q (h d) -> p q h d", h=2)),
                             reads=[pk], writes=[dk_])
                    proj_tok(l, wst, wbf, nm, ev)

                def ev(g, view, pk):
                    T.op("act", lambda: nc.scalar.activation(out=NG[:, 4 * g:4 * g + 4, :], in_=view[:, :, 0:18], func=AF.Exp, scale=-1.0),
                         reads=[pk], writes=["NG"])
                proj_tok(l, wst, wbf, "ng", ev)
                T.op("dve", lambda: nc.vector.tensor_scalar(out=NG[:], in0=NG[:], scalar1=1.0, scalar2=None, op0=ALU.add),
                     reads=["NG"], writes=["NG"])
                T.op("dve", lambda: nc.vector.reciprocal(out=NG[:], in_=NG[:]), reads=["NG"], writes=["NG"])

                T.dma(lambda: nc.sync.dma_start(out=W2f[:], in_=w2_d[l, :, :, :].rearrange("a p m -> p a m")), "W2f", writes=["W2f"])
                T.op("pool", lambda: nc.gpsimd.tensor_copy(out=W2b[:], in_=W2f[:]), reads=["W2f"], writes=["W2b"])
                T.dma(lambda: nc.sync.dma_start(out=peTf[:], in_=peT_d[l, :, :, :].rearrange("a p m -> p a m")), "peTf", writes=["peTf"])
                T.op("pool", lambda: nc.gpsimd.tensor_copy(out=peTb[:], in_=peTf[:]), reads=["peTf"], writes=["peTb"])
                for a, (src, skey) in enumerate(((KCT, "KCT"), (VCT, "VCT"))):
                    T.dma(lambda a=a: nc.sync.dma_start(out=W1f[:], in_=w1_d[l, a, :, :]), "W1f", writes=["W1f"])
                    T.op("dve", lambda a=a: nc.vector.tensor_copy(out=W1b[:], in_=W1f[:]), reads=["W1f"], writes=["W1b"])
                    for g in range(2):
                        hp = 64 * g
                        for ll in range(32):
                            T.op("pe", lambda ll=ll: nc.tensor.matmul(
                                PS[0][hp:hp + 64, 0:1], lhsT=W1b[hp:hp + 64, ll * 64:(ll + 1) * 64],
                                rhs=peTb[hp:hp + 64, a, ll:ll + 1], start=(ll == 0), stop=(ll == 31)),
                                reads=["W1b", "peTb"], writes=["ps0"])
                        T.op("dve", lambda: nc.vector.tensor_copy(out=cb[hp:hp + 64, 0:1], in_=PS[0][hp:hp + 64, 0:1]),
                             reads=["ps0"], writes=["cb"])
                        for ll in range(32):
                            rhs = src[hp:hp + 64, ll:ll + 16 * (NC_CMP - 1) + 1:16]
                            T.op("pe", lambda ll=ll, rhs=rhs: nc.tensor.matmul(
                                PS[1][hp:hp + 64, 0:NC_CMP], lhsT=W1b[hp:hp + 64, ll * 64:(ll + 1) * 64],
                                rhs=rhs, start=(ll == 0), stop=(ll == 31)),
                                reads=["W1b", skey], writes=["ps1"])
                        T.op("act", lambda: nc.scalar.activation(out=HT[hp:hp + 64, a, 0:NC_CMP], in_=PS[1][hp:hp + 64, 0:NC_CMP],
                                                                func=AF.Silu, bias=cb[hp:hp + 64, 0:1], scale=1.0),
                             reads=["ps1", "cb"], writes=["HT"])
                        if a == 0:
                            T.op("pe", lambda: nc.tensor.matmul(PS[2][hp:hp + 64, 0:NC_CMP], lhsT=W2b[hp:hp + 64, 0, :],
                                                                rhs=HT[hp:hp + 64, 0, 0:NC_CMP], start=True, stop=True),
                                 reads=["W2b", "HT"], writes=["ps2"])
                            T.op("act", lambda: nc.scalar.copy(out=KCC[hp:hp + 64, g, 0:NC_CMP], in_=PS[2][hp:hp + 64, 0:NC_CMP]),
                                 reads=["ps2"], writes=["KCC"])
                        else:
                            T.op("pe", lambda: nc.tensor.matmul(PS[3][0:NC_CMP, 0:64], lhsT=HT[hp:hp + 64, 1, 0:NC_CMP],
                                                                rhs=W2b[hp:hp + 64, 1, :], start=True, stop=True),
                                 reads=["W2b", "HT"], writes=["ps3"])
                            T.op("act", lambda: nc.scalar.copy(out=VCC[0:NC_CMP, g, :], in_=PS[3][0:NC_CMP, 0:64]),
                                 reads=["ps3"], writes=["VCC"])
                T.barrier()
                esq.close()
                MBN = esp.enter_context(sbt("MBN", [P, 2, 2, S], BF16))
                cmk = esp.enter_context(sbt("cmk", [P, 2, P], F32))
                cmkb = esp.enter_context(sbt("cmkb", [P, 2, P], BF16))
                fbt = esp.enter_context(sbt("fbt", [P, 2, 32], F32))
                pn = esp.enter_context(sbt("pnorm", [P, 6, P], F32))
                pT = esp.enter_context(sbt("pT", [P, 6, P], BF16))
                PP = esp.enter_context(sbt("PP", [P, 2, 132], F32))
                sc = esp.enter_context(sbt("selsc", [P, 2, 2, 32], F32))
                sm = esp.enter_context(sbt("selm", [P, 16], F32))
                mbk = esp.enter_context(sbt("mbk", [P, 2, 32], BF16))
                ON = esp.enter_context(sbt("ON", [P, 2, 6, 64], F32))
                rs2 = esp.enter_context(sbt("rs2", [P, 16], F32))
                T.op("dve", lambda: nc.vector.memset(PP[:, 0, :], 0.0), writes=["PP0"])
                T.op("dve", lambda: nc.vector.memset(PP[:, 1, :], 0.0), writes=["PP1"])
                T.op("dve", lambda: nc.vector.memset(pn[:], 0.0), writes=[f"pn{h_}" for h_ in range(6)])

                def nsa_cmp_g(i):
                    par = i % 2
                    ms = i % 2
                    T.dma(lambda i=i, ms=ms: nc.sync.dma_start(out=cmk[:, ms, :], in_=cmpm_d[i, :, :]), f"cmk{ms}", writes=[f"cmk{ms}"])
                    T.op("pool", lambda ms=ms: nc.gpsimd.tensor_copy(out=cmkb[:, ms, :], in_=cmk[:, ms, :]),
                         reads=[f"cmk{ms}"], writes=[f"cmkb{ms}"])
                    if i >= 8:
                        T.dma(lambda i=i, ms=ms: nc.sync.dma_start(out=fbt[:, ms, :], in_=fb_d[i, :, :]), f"fbt{ms}", writes=[f"fbt{ms}"])
                    for hd in range(6):
                        g, jj = hd // 3, hd % 3
                        hp = 64 * g
                        bank = hd // 4
                        cs = (hd % 4) * P
                        T.op("pe", lambda: nc.tensor.matmul(
                            PS[bank][:, cs:cs + P], lhsT=NQ[:, jj, i * P:(i + 1) * P], rhs=KCC[:, g, :],
                            start=True, stop=False), reads=[f"NQ{jj}", "KCC"], writes=[pskey[bank]])
                        T.op("pe", lambda: nc.tensor.matmul(
                            PS[bank][:, cs:cs + P], lhsT=identbig, rhs=cmkb[:, ms, :], start=False, stop=True),
                            reads=["cstb", f"cmkb{ms}"], writes=[pskey[bank]])
                        if hd % 2 == 1:
                            yield
                    for hd in range(6):
                        bank = hd // 4
                        cs = (hd % 4) * P
                        T.op("act", lambda: nc.scalar.activation(
                            out=pn[:, hd, 0:NC_CMP], in_=PS[bank][:, cs:cs + NC_CMP], func=AF.Exp, scale=0.125,
                            accum_out=rs2[:, hd:hd + 1]),
                            reads=[pskey[bank]], writes=[f"pn{hd}", f"rs2_{hd}"])
                        if hd % 2 == 1:
                            yield
                    rk = [f"rs2_{h_}" for h_ in range(6)]
                    T.op("dve", lambda: nc.vector.tensor_scalar(out=rs2[:, 0:6], in0=rs2[:, 0:6], scalar1=1e-30, scalar2=None, op0=ALU.max),
                         reads=rk, writes=rk)
                    T.op("dve", lambda: nc.vector.reciprocal(out=rs2[:, 0:6], in_=rs2[:, 0:6]), reads=rk, writes=rk)
                    for hd in range(6):
                        T.op("dve", lambda: nc.vector.tensor_scalar(
                            out=pn[:, hd, 0:NC_CMP], in0=pn[:, hd, 0:NC_CMP], scalar1=rs2[:, hd:hd + 1], scalar2=None,
                            op0=ALU.mult), reads=[f"pn{hd}", f"rs2_{hd}"], writes=[f"pn{hd}"])
                    yield
                    if i >= 8:
                        for g in range(2):
                            T.op("dve", lambda: nc.vector.tensor_reduce(
                                out=PP[:, g, 1:1 + NC_CMP], in_=pn[:, 3 * g:3 * g + 3, 0:NC_CMP].rearrange("p j c -> p c j"),
                                axis=AX.X, op=ALU.add),
                                reads=[f"pn{3 * g}", f"pn{3 * g + 1}", f"pn{3 * g + 2}"], writes=[f"PP{g}"])
                        yield
                    for hd in range(6):
                        tb = hd // 4
                        cs = (hd % 4) * P
                        T.op("pe", lambda: nc.tensor.transpose(PS[tb][:, cs:cs + P], pn[:, hd, :], ident_f),
                             reads=[f"pn{hd}", "cstf"], writes=[pskey[tb]])
                    T.op("act", lambda: nc.scalar.copy(out=pT[:, 0:4, :], in_=PS[0][:, :].rearrange("p (h t) -> p h t", h=4)),
                         reads=["ps0"], writes=[f"pT{h_}" for h_ in range(4)])
                    T.op("act", lambda: nc.scalar.copy(out=pT[:, 4:6, :], in_=PS[1][:, 0:2 * P].rearrange("p (h t) -> p h t", h=2)),
                         reads=["ps1"], writes=["pT4", "pT5"])
                    yield
                    for hd in range(6):
                        g = hd // 3
                        T.op("pe", lambda: nc.tensor.matmul(
                            PS[0][:, hd * 64:(hd + 1) * 64], lhsT=pT[0:NC_CMP, hd, :], rhs=VCC[0:NC_CMP, g, :], start=True, stop=True),
                            reads=[f"pT{hd}", "VCC"], writes=["ps0"])
                    T.op("dve", lambda: nc.vector.tensor_tensor(
                        out=ON[:, par, :, :], in0=PS[0][:, 0:384].rearrange("p (h d) -> p h d", h=6),
                        in1=NG[:, i, 0:6].unsqueeze(2).to_broadcast([P, 6, 64]), op=ALU.mult),
                        reads=["ps0", "NG"], writes=[f"ON{par}_{h_}" for h_ in range(6)])
                    if 0 not in NSA_BR[0]:
                        T.op("dve", lambda: nc.vector.memset(ON[:, par, :, :], 0.0), writes=[f"ON{par}_{h_}" for h_ in range(6)])
                    yield
                    for g in range(2):
                        if i >= 8:
                            T.op("dve", lambda: nc.vector.tensor_reduce(
                                out=sc[:, g, 0, :], in_=PP[:, g, 0:128].rearrange("p (n k) -> p n k", k=4), axis=AX.X, op=ALU.add),
                                reads=[f"PP{g}"], writes=[f"sc{g}"])
                            T.op("dve", lambda: nc.vector.tensor_tensor(out=sc[:, g, 0, :], in0=sc[:, g, 0, :],
                                                                        in1=PP[:, g, 4:132:4], op=ALU.add),
                                 reads=[f"PP{g}", f"sc{g}"], writes=[f"sc{g}"])
                            T.op("dve", lambda ms=ms: nc.vector.tensor_tensor(out=sc[:, g, 0, :], in0=sc[:, g, 0, :],
                                                                              in1=fbt[:, ms, :], op=ALU.add),
                                 reads=[f"fbt{ms}", f"sc{g}"], writes=[f"sc{g}"])
                            T.op("dve", lambda: nc.vector.tensor_copy(out=sc[:, g, 1, :], in_=sc[:, g, 0, :]),
                                 reads=[f"sc{g}"], writes=[f"scw{g}"])
                            T.op("dve", lambda: nc.vector.max(out=sm[:, 0:8], in_=sc[:, g, 1, :]), reads=[f"scw{g}"], writes=["sm"])
                            T.op("dve", lambda: nc.vector.match_replace(out=sc[:, g, 1, :], in_to_replace=sm[:, 0:8],
                                                                        in_values=sc[:, g, 1, :], imm_value=-3.0e9),
                                 reads=[f"scw{g}", "sm"], writes=[f"scw{g}"])
                            T.op("dve", lambda: nc.vector.max(out=sm[:, 8:16], in_=sc[:, g, 1, :]), reads=[f"scw{g}"], writes=["sm"])
                            T.op("dve", lambda: nc.vector.tensor_scalar(out=mbk[:, g, :], in0=sc[:, g, 0, :], scalar1=sm[:, 15:16],
                                                                        scalar2=-1.0, op0=ALU.is_ge, op1=ALU.add),
                                 reads=[f"sc{g}", "sm"], writes=[f"mbk{g}"])
                            L = (i + 1) * P
                            nb = L // 64
                            T.op("act", lambda nb=nb, L=L: nc.scalar.copy(
                                out=MBN[:, par, g, 0:L].rearrange("p (n k) -> p n k", k=64),
                                in_=mbk[:, g, 0:nb].unsqueeze(2).to_broadcast([P, nb, 64])),
                                reads=[f"mbk{g}"], writes=[f"MBN{par}_{g}"])
                            T.op("pool", lambda L=L: nc.gpsimd.tensor_tensor(out=MBN[:, par, g, i * P:L], in0=MBN[:, par, g, i * P:L],
                                                                             in1=caus01, op=ALU.add),
                                 reads=["cstb", f"MBN{par}_{g}"], writes=[f"MBN{par}_{g}"])
                    yield

                def nsa_attn_g(i):
                    par = i % 2
                    for g in range(2):
                        hp = 64 * g
                        for br in (1, 2):
                            if br not in NSA_BR[0]:
                                continue
                            KT_, kkey = (KST, "KST") if br == 1 else (KWT, "KWT")
                            VX, vkey = (VS, "VS") if br == 1 else (VW, "VW")
                            kt = list(range(i + 1)) if br == 1 else list(range(max(0, i - 4), i + 1))
                            ktl = [(j, 0) for j in kt]

                            def qk(j, c0, KT_=KT_, kkey=kkey, g=g):
                                return (KT_[:, g, j * P:(j + 1) * P], NQR[:, :, i * P:(i + 1) * P],
                                        [kkey, "NQR0", "NQR1", "NQR2"])

                            def extra(j, c0, br=br, g=g):
                                id3 = ID4[:, 0:3, :]
                                if br == 1:
                                    if i >= 8:
                                        return [(MBN[:, par, g, j * P:(j + 1) * P], id3, [f"MBN{par}_{g}", "ID4"], (0, 384))]
                                    return [(caus01, id3, ["cstb", "ID4"], (0, 384))] if j == i else []
                                ex = []
                                if j == i:
                                    ex.append((caus01, id3, ["cstb", "ID4"], (0, 384)))
                                if j == i - 4:
                                    ex.append((band01, id3, ["cstb", "ID4"], (0, 384)))
                                return ex

                            def fin(abank, br=br, g=g):
                                def blk(b, src, skey, rs, rkey, rcol=None):
                                    hd = 3 * g + b
                                    gi = br * 6 + hd
                                    if b == 0:
                                        rs3 = rsm[:, rcol:rcol + 3]
                                        T.op("dve", lambda: nc.vector.tensor_tensor(out=rs3, in0=rs3, in1=NG[:, i, gi:gi + 3], op=ALU.mult),
                                             reads=[rkey, "NG"], writes=[rkey])
                                    T.op("dve", lambda: nc.vector.scalar_tensor_tensor(
                                        out=ON[:, par, hd, :], in0=src, scalar=rs, in1=ON[:, par, hd, :],
                                        op0=ALU.mult, op1=ALU.add),
                                        reads=[skey, rkey, f"ON{par}_{hd}"], writes=[f"ON{par}_{hd}"])
                                finish_wide(abank, 384, 3, blk)
                            yield from attn_wide_g(PT, 384, ktl, qk, extra, lambda j, VX=VX, g=g: VX[:, j, g, :], vkey, fin)
                    pending_fin.append(lambda: T.op(
                        "dve", lambda: nc.vector.tensor_copy(out=mix[:, i, 640:1024], in_=ON[:, par, :, :].rearrange("p h d -> p (h d)")),
                        reads=[f"ON{par}_{h_}" for h_ in range(6)], writes=[f"mix{i}"]))
                    yield

                fw_banks[0] = [2]
                interleave(nsa_cmp_g(0), iter(()))
                for i in range(NT):
                    interleave(nsa_attn_g(i), nsa_cmp_g(i + 1) if i + 1 < NT else iter(()))
                flush_fin()
                fw_banks[0] = [0, 1, 2]
                T.barrier()
            _chk("nsa")

            if "mix" in dbg and l == dbg_layer[0]:
                for i in range(NT):
                    T.dma(lambda i=i: nc.sync.dma_start(out=dbg_d["mix"][i * P:(i + 1) * P, :], in_=mix[:, i, :]),
                          "store", reads=[f"mix{i}"], writes=[f"dbgmix{i}"])
            with ExitStack() as esp:
                WG = esp.enter_context(sbt("WG", [P, KC, D], BF16))
                WO = esp.enter_context(sbt("WO", [P, KC, D], BF16))
                lng = esp.enter_context(sbt("lng", [P, D], F32))
                lnb = esp.enter_context(sbt("lnb", [P, D], F32))
                xres = esp.enter_context(sbt("xres", [P, 2, D], F32))
                Gt = esp.enter_context(sbt("Gt", [P, D], F32))
                mg = esp.enter_context(sbt("mg", [P, D], F32))
                mgT = esp.enter_context(sbt("mgT", [P, 2, KC, P], BF16))
                zt = esp.enter_context(sbt("zt", [P, D], F32))
                xo = esp.enter_context(sbt("xo", [P, 2, D], F32))
                st6 = esp.enter_context(sbt("st6", [P, 2, 6], F32))
                mv = esp.enter_context(sbt("mv", [P, 4], F32))
                mhalf = esp.enter_context(sbt("mhalf", [P, 2], F32))
                T.op("pool", lambda: nc.gpsimd.memset(mhalf[:], -0.5), writes=["mhalf"])
                T.dma(lambda: nc.sync.dma_start(out=lng[:], in_=lng_d[l, :, :]), "lng", writes=["lng"])
                T.dma(lambda: nc.sync.dma_start(out=lnb[:], in_=lnb_d[l, :, :]), "lnb", writes=["lnb"])
                go = UOFF["g0"][0]
                for kc in range(KC):
                    slot = wslot_ctr[0] % 2
                    wslot_ctr[0] += 1
                    src = wperm_d[l, kc * P:(kc + 1) * P, go:go + D]
                    T.dma(lambda src=src, slot=slot: nc.sync.dma_start(out=wst[:, slot, :, :].rearrange("p a b -> p (a b)"), in_=src),
                          f"wst{slot}", writes=[f"wst{slot}"])
                    T.op("dve", lambda kc=kc, slot=slot: nc.vector.tensor_copy(out=WG[:, kc, :], in_=wst[:, slot, :, :].rearrange("p a b -> p (a b)")),
                         reads=[f"wst{slot}"], writes=[f"WG{kc}"])
                for kc in range(KC):
                    slot = wslot_ctr[0] % 2
                    wslot_ctr[0] += 1
                    src = wout_d[l, kc * P:(kc + 1) * P, :]
                    T.dma(lambda src=src, slot=slot: nc.sync.dma_start(out=wst[:, slot, :, :].rearrange("p a b -> p (a b)"), in_=src),
                          f"wst{slot}", writes=[f"wst{slot}"])
                    T.op("act", lambda kc=kc, slot=slot: nc.scalar.copy(out=WO[:, kc, :], in_=wst[:, slot, :, :].rearrange("p a b -> p (a b)")),
                         reads=[f"wst{slot}"], writes=[f"WO{kc}"])
                xsrc = x_d if l == 0 else x1_d

                def epiA(i):
                    xs_ = i % 2
                    mp = i % 2
                    T.dma(lambda: nc.sync.dma_start(out=xres[:, xs_, :], in_=xsrc[i * P:(i + 1) * P, :]),
                          f"xres{xs_}", writes=[f"xres{xs_}"])
                    for hf in range(2):
                        for kc in range(KC):
                            T.op("pe", lambda: nc.tensor.matmul(
                                PS[hf][:, :], lhsT=uT[:, kc, i * P:(i + 1) * P], rhs=WG[:, kc, hf * 512:(hf + 1) * 512],
                                start=(kc == 0), stop=(kc == KC - 1)), reads=[f"uT{i}", f"WG{kc}"], writes=[pskey[hf]])
                        T.op("act", lambda: nc.scalar.activation(out=Gt[:, hf * 512:(hf + 1) * 512], in_=PS[hf][:, :], func=AF.Silu),
                             reads=[pskey[hf]], writes=["Gt"])
                    T.op("dve", lambda: nc.vector.tensor_tensor(out=mg[:], in0=Gt[:], in1=mix[:, i, :], op=ALU.mult),
                         reads=["Gt", f"mix{i}"], writes=["mg"])

                def epiA2(i):
                    mp = i % 2
                    for hf in range(2):
                        bank = 2 + hf
                        for q in range(4):
                            fc = hf * 4 + q
                            T.op("pe", lambda: nc.tensor.transpose(
                                PS[bank][:, q * P:(q + 1) * P], mg[:, fc * P:(fc + 1) * P], ident_f),
                                reads=["mg", "cstf"], writes=[pskey[bank]])
                        T.op("act", lambda: nc.scalar.copy(
                            out=mgT[:, mp, hf * 4:hf * 4 + 4, :], in_=PS[bank][:, :].rearrange("p (q t) -> p q t", q=4)),
                            reads=[pskey[bank]], writes=[f"mgT{mp}"])

                def epiB(i):
                    xs_ = i % 2
                    mp = i % 2
                    for hf in range(2):
                        bank = 4 + hf
                        for fc in range(KC):
                            T.op("pe", lambda: nc.tensor.matmul(
                                PS[bank][:, :], lhsT=mgT[:, mp, fc, :], rhs=WO[:, fc, hf * 512:(hf + 1) * 512],
                                start=(fc == 0), stop=(fc == KC - 1)), reads=[f"mgT{mp}", f"WO{fc}"], writes=[pskey[bank]])
                        sl = slice(hf * 512, (hf + 1) * 512)
                        T.op("dve", lambda: nc.vector.tensor_tensor(out=zt[:, sl], in0=PS[bank][:, :], in1=g1bc[:, l, sl], op=ALU.mult),
                             reads=[pskey[bank], "g1bc"], writes=["zt"])
                        T.op("dve", lambda: nc.vector.scalar_tensor_tensor(
                            out=zt[:, sl], in0=xres[:, xs_, sl], scalar=float(ALPHA), in1=zt[:, sl], op0=ALU.mult, op1=ALU.add),
                            reads=[f"xres{xs_}", "zt"], writes=["zt"])
                        T.op("dve", lambda: nc.vector.bn_stats(out=st6[:, hf, :], in_=zt[:, sl]), reads=["zt"], writes=["st6"])
                    T.op("dve", lambda: nc.vector.bn_aggr(out=mv[:, 0:2], in_=st6[:].rearrange("p a b -> p (a b)")), reads=["st6"], writes=["mv"])
                    T.op("dve", lambda: nc.vector.tensor_scalar(out=mv[:, 2:3], in0=mv[:, 1:2], scalar1=float(LN_EPS), scalar2=None, op0=ALU.add),
                         reads=["mv"], writes=["mv2"])
                    T.op("pool", lambda: nc.gpsimd.tensor_tensor(out=mv[:, 3:4], in0=mv[:, 2:3], in1=mhalf[:, 0:1], op=ALU.pow),
                         reads=["mv2", "mhalf"], writes=["mv3"])
                    os_ = i % 2
                    T.op("dve", lambda: nc.vector.scalar_tensor_tensor(out=xo[:, os_, :], in0=zt[:], scalar=mv[:, 0:1], in1=lng[:],
                                                                       op0=ALU.subtract, op1=ALU.mult),
                         reads=["zt", "mv", "lng"], writes=[f"xo{os_}"])
                    T.op("dve", lambda: nc.vector.scalar_tensor_tensor(out=xo[:, os_, :], in0=xo[:, os_, :], scalar=mv[:, 3:4], in1=lnb[:],
                                                                       op0=ALU.mult, op1=ALU.add),
                         reads=[f"xo{os_}", "mv3", "lnb"], writes=[f"xo{os_}"])
                    dst = out_d if l == n_layers - 1 else x1_d
                    T.dma(lambda: nc.sync.dma_start(out=dst[i * P:(i + 1) * P, :], in_=xo[:, os_, :]),
                          f"store{os_}", reads=[f"xo{os_}"], writes=[f"dst{l}_{i}"])

                epiA(0)
                epiA2(0)
                for i in range(NT):
                    if i + 1 < NT:
                        epiA(i + 1)
                    epiB(i)
                    if l < n_layers - 1 and i >= 1:
                        make_uT_tile(xo[:, (i - 1) % 2, :], f"xo{(i - 1) % 2}", l + 1, i - 1, bank0=6)
                    if i + 1 < NT:
                        epiA2(i + 1)
                if l < n_layers - 1:
                    make_uT_tile(xo[:, (NT - 1) % 2, :], f"xo{(NT - 1) % 2}", l + 1, NT - 1, bank0=6)
                T.barrier()


dbg_layer = [0]
NSA_BR = [(0, 1, 2)]


def _consts():
    t = np.arange(S, dtype=np.float32)
    out = {}
    for nm, dim in (("64", 64), ("32", 32)):
        half = dim // 2
        inv = (10000.0 ** (-np.arange(half, dtype=np.float32) / half)).astype(np.float32)
        ang = t[None, :] * inv[:, None]
        cos = np.cos(ang).astype(np.float32)
        sin = np.sin(ang).astype(np.float32)
        c_full = np.concatenate([cos, cos], 0)
        s_full = np.concatenate([-sin, sin], 0)
        out["cos" + nm] = np.ascontiguousarray(np.tile(c_full, (P // dim, 1))).astype(ml_dtypes.bfloat16)
        out["sin" + nm] = np.ascontiguousarray(np.tile(s_full, (P // dim, 1))).astype(ml_dtypes.bfloat16)
    a = np.arange(P)
    tt, ss = a[:, None], a[None, :]
    cst = np.zeros((P, 10, P), np.float32)
    for m in range(64):
        cst[64 + m, 9, m] = 1.0
    for m in range(P):
        cst[(m // 64) * 64 + ((m % 64) + 32) % 64, 7, m] = 1.0
        cst[(m // 32) * 32 + ((m % 32) + 16) % 32, 8, m] = 1.0
    cst[:, 6, :] = (2.0 ** (-np.arange(P, dtype=np.float64).clip(0, 60)))[None, :]
    cst[:, 0, :] = np.eye(P)
    cst[:, 1, :] = np.where(ss <= tt, 0.0, -1.0)
    cst[:, 2, :] = np.where(ss > tt, 0.0, -1.0)
    cst[:, 3, :] = np.where(ss <= tt, 0.0, NEG)
    cst[:, 4, :] = np.eye(P) * MASKV
    cst[:, 5, :] = 1.0
    out["cst"] = cst
    out["cst_b"] = cst.astype(ml_dtypes.bfloat16)
    cm = np.zeros((NT, P, P), np.float32)
    fb = np.zeros((NT, P, 32), np.float32)
    cidx = np.arange(P)
    nidx = np.arange(32)
    for i in range(NT):
        tpos = i * P + a
        valid = (16 * cidx[None, :] + 31 <= tpos[:, None]) & (cidx[None, :] < NC_CMP)
        cm[i] = np.where(valid, 0.0, -1.0)
        cur = tpos // 64
        forced = (nidx[None, :] == 0) | (nidx[None, :] == cur[:, None]) | (nidx[None, :] == cur[:, None] - 1)
        future = (64 * nidx[None, :]) > tpos[:, None]
        fb[i] = np.where(forced, 1.0e9, np.where(future, -1.0e9, 0.0))
    out["cmp_mask"] = cm
    out["sel_fb"] = fb
    return out


_NC_CACHE = {}


def _prep_shared(w_ada, b_ada, w_in, b_f, cmp_pe, cmp_w1, cmp_w2, w_out, ln_g, ln_b):
    f = lambda a: np.ascontiguousarray(np.asarray(a, dtype=np.float32))
    sh = dict(_consts())
    sh["w_ada"] = f(w_ada)
    sh["b_ada"] = f(b_ada)
    sh["b_gate_bc"] = f(np.broadcast_to(np.asarray(b_ada)[:, None, 2 * D:3 * D], (DEPTH, P, D)))
    sh["w_perm"] = f(np.asarray(w_in)[:, :, PERM])
    bfp = np.zeros((DEPTH, P, 8), np.float32)
    bfp[:, 0:6, 0] = np.asarray(b_f)
    sh["b_f"] = bfp
    peT = np.asarray(cmp_pe).transpose(0, 1, 3, 2)
    sh["cmp_peT"] = f(np.concatenate([peT, peT], axis=2))
    w1 = np.asarray(cmp_w1).reshape(DEPTH, 2, 32, 64, 64).transpose(0, 1, 3, 2, 4).reshape(DEPTH, 2, 64, 32 * 64)
    sh["cmp_w1r"] = f(np.concatenate([w1, w1], axis=2))
    w2 = np.asarray(cmp_w2)
    sh["cmp_w2r"] = f(np.concatenate([w2, w2], axis=2))
    sh["w_out"] = f(w_out)
    sh["ln_g_bc"] = f(np.broadcast_to(np.asarray(ln_g)[:, None, :], (DEPTH, P, D)))
    sh["ln_b_bc"] = f(np.broadcast_to(np.asarray(ln_b)[:, None, :], (DEPTH, P, D)))
    return sh


def kernel(x, c, w_ada, b_ada, w_in, b_f, cmp_pe, cmp_w1, cmp_w2, w_out, ln_g, ln_b):
    x = np.asarray(x, dtype=np.float32)
    c = np.asarray(c, dtype=np.float32)
    B = x.shape[0]
    if "nc" not in _NC_CACHE:
        _NC_CACHE["nc"] = build_program()
    nc = _NC_CACHE["nc"]
    sh = _prep_shared(w_ada, b_ada, w_in, b_f, cmp_pe, cmp_w1, cmp_w2, w_out, ln_g, ln_b)
    in_maps = []
    for b in range(B):
        m = dict(sh)
        m["x"] = np.ascontiguousarray(x[b])
        ccol = np.ascontiguousarray(c[b].reshape(KC, P).T)
        m["c_col"] = ccol
        m["c_bc"] = np.ascontiguousarray(np.broadcast_to(ccol[:, :, None], (P, KC, P)))
        in_maps.append(m)
    res = run_bass_kernel_spmd(nc, in_maps, core_ids=list(range(B)))
    return np.stack([np.asarray(r["out"], dtype=np.float32) for r in res.results], axis=0)
```

```python
import numpy as np
import ml_dtypes
from contextlib import ExitStack
import concourse.bass as bass
import concourse.mybir as mybir
from concourse.bass_utils import run_bass_kernel_spmd

F32 = mybir.dt.float32
BF16 = mybir.dt.bfloat16
AF = mybir.ActivationFunctionType
ALU = mybir.AluOpType
AX = mybir.AxisListType

P = 128
S = 2048
NT = 16
D = 1024
KC = 8
DEPTH = 2
HD = 64
ALPHA = (2.0 * DEPTH) ** 0.25
LN_EPS = 1e-5
NEG = -1.0e30
MASKV = 30000.0
NC_CMP = 127

_SPL = (("dsa_q", 256), ("dsa_k", 64), ("dsa_v", 64), ("idx_q", 256), ("idx_k", 32), ("idx_w", 8),
        ("fox_q", 384), ("fox_k", 384), ("fox_v", 384), ("fox_f", 6), ("nsa_q", 384),
        ("nsa_kc", 128), ("nsa_vc", 128), ("nsa_ks", 128), ("nsa_vs", 128), ("nsa_kw", 128),
        ("nsa_vw", 128), ("nsa_g", 18), ("gate", 1024))
OFF = {}
_o = 0
for _n, _w in _SPL:
    OFF[_n] = _o
    _o += _w
IN_WIDTH = _o


def _unit(name, h, dim=64):
    return np.arange(OFF[name] + h * dim, OFF[name] + (h + 1) * dim)


def _swap(cols):
    h = len(cols) // 2
    return np.concatenate([cols[h:], cols[:h]])


def _cat(*a):
    return np.concatenate(a)


def _build_units():
    U = {}
    for i in range(3):
        U[f"fq{i}"] = _cat(_unit("fox_q", 2 * i), _unit("fox_q", 2 * i + 1))
        U[f"fk{i}"] = _cat(_unit("fox_k", 2 * i), _unit("fox_k", 2 * i + 1))
        U[f"fv{i}"] = _cat(_unit("fox_v", 2 * i), _unit("fox_v", 2 * i + 1))
    U["ff"] = np.arange(OFF["fox_f"], OFF["fox_f"] + 6)
    for i in range(4):
        a = _unit("dsa_q", i)
        U[f"dq{i}"] = a
    for i in range(3):
        a = [_unit("idx_q", 3 * i + j, 32) for j in range(3) if 3 * i + j < 8]
        U[f"iq{i}"] = _cat(*a)
    k = _unit("dsa_k", 0)
    U["dk"] = k
    k = _unit("idx_k", 0, 32)
    U["ik"] = _cat(k, k, k)
    U["dvw"] = _cat(_unit("dsa_v", 0), np.arange(OFF["idx_w"], OFF["idx_w"] + 8), _unit("dsa_v", 0)[0:56])
    for i in range(3):
        a, b = _unit("nsa_q", i), _unit("nsa_q", i + 3)
        U[f"nq{i}"] = _cat(a, b)
    for nm in ("kc", "vc", "vs", "vw"):
        U["n" + nm] = np.arange(OFF["nsa_" + nm], OFF["nsa_" + nm] + 128)
    for nm in ("ks", "kw"):
        a, b = _unit("nsa_" + nm, 0), _unit("nsa_" + nm, 1)
        U["n" + nm] = _cat(a, b)
    U["ng"] = _cat(np.arange(OFF["nsa_g"], OFF["nsa_g"] + 18), np.arange(OFF["nsa_g"], OFF["nsa_g"] + 14))
    for i in range(8):
        U[f"g{i}"] = np.arange(OFF["gate"] + 128 * i, OFF["gate"] + 128 * (i + 1))
    return U


UNITS = _build_units()
UOFF = {}
_o = 0
for _k, _v in UNITS.items():
    UOFF[_k] = (_o, len(_v))
    _o += len(_v)
TOTC = _o
PERM = np.concatenate(list(UNITS.values()))


class Tr:
    def __init__(self, nc, es):
        self.nc = nc
        self.es = es
        self.sems = {}
        self.eng = {}
        for name, h in (("pe", nc.tensor), ("act", nc.scalar), ("dve", nc.vector),
                        ("pool", nc.gpsimd), ("sp", nc.sync)):
            sn = "e_" + name
            self.sems[sn] = es.enter_context(nc.semaphore(sn))
            self.eng[name] = dict(h=h, sn=sn, n=0, seen={})
        self.lw = {}
        self.rd = {}
        self.dcnt = {}
        self.hist = {}

    def _deps(self, reads, writes, nowaw=False):
        deps = []
        for k in reads:
            t = self.lw.get(k)
            if t:
                deps.append((t, True))
            if k.startswith("ps"):
                for t2 in self.rd.get(k, {}).items():
                    deps.append((t2, False))
        for k in writes:
            t = self.lw.get(k)
            if t and not nowaw:
                deps.append((t, False))
            for t2 in self.rd.get(k, {}).items():
                deps.append((t2, False))
        return deps

    def _emit_waits(self, e, deps, attach_ok=False):
        import itertools
        E = self.eng[e]
        need = {}
        for ((sn, v), raw) in deps:
            if sn == E["sn"] and e == "pe":
                continue
            if v > E["seen"].get(sn, 0) and v > need.get(sn, 0):
                need[sn] = v
        items = list(need.items())
        best = None
        perms = itertools.permutations(items) if len(items) <= 4 else [items]
        for order in perms:
            known = dict(E["seen"])
            res = []
            for sn, v in order:
                if known.get(sn, 0) >= v:
                    continue
                res.append((sn, v))
                known[sn] = v
                snap = self.hist.get((sn, v))
                if snap:
                    for k2, v2 in snap.items():
                        if v2 > known.get(k2, 0):
                            known[k2] = v2
            if best is None or len(res) < len(best[0]):
                best = (res, known)
        if best is None:
            return None
        items, known = best
        E["seen"] = known
        attach = None
        if attach_ok and items:
            attach = items.pop()
        for sn, v in items:
            E["h"].wait_ge(self.sems[sn], v)
        return attach

    def _record(self, tok, reads, writes):
        for k in reads:
            d = self.rd.setdefault(k, {})
            d[tok[0]] = max(d.get(tok[0], 0), tok[1])
        for k in writes:
            self.lw[k] = tok
            self.rd[k] = {}

    def op(self, e, fn, reads=(), writes=()):
        attach = self._emit_waits(e, self._deps(reads, writes), attach_ok=True)
        E = self.eng[e]
        ins = fn()
        if attach is not None:
            ins._wait_ge(self.sems[attach[0]], attach[1])
        E["n"] += 1
        ins.then_inc(self.sems[E["sn"]], 1)
        self.hist[(E["sn"], E["n"])] = dict(E["seen"])
        self._record((E["sn"], E["n"]), reads, writes)

    def dma(self, fn, semkey, reads=(), writes=(), nowaw=False):
        self._emit_waits("sp", self._deps(reads, writes, nowaw))
        sn = "d_" + semkey
        if sn not in self.sems:
            self.sems[sn] = self.es.enter_context(self.nc.semaphore(sn))
            self.dcnt[sn] = 0
        ins = fn()
        self.dcnt[sn] += 16
        ins.then_inc(self.sems[sn], 16)
        self._record((sn, self.dcnt[sn]), reads, writes)

    def barrier(self):
        toks = [(E["sn"], E["n"]) for E in self.eng.values() if E["n"] > 0]
        toks += [(sn, c) for sn, c in self.dcnt.items() if c > 0]
        for e in self.eng:
            self._emit_waits(e, [(t, True) for t in toks if t[0] != self.eng[e]["sn"]])

    def final(self):
        toks = [(sn, c) for sn, c in self.dcnt.items() if c > 0]
        toks += [(E["sn"], E["n"]) for n_, E in self.eng.items() if E["n"] > 0 and n_ != "sp"]
        self._emit_waits("sp", [(t, True) for t in toks])


class _Stop(Exception):
    pass


STOP = [None]


_DUMP = [None]


def _chk(name):
    if STOP[0] == name:
        if _DUMP[0] is not None:
            _DUMP[0]()
        raise _Stop()


def build_program(n_layers=DEPTH, dbg=()):
    nc = bass.Bass("TRN2", target_bir_lowering=False)
    es = ExitStack()
    T = Tr(nc, es)
    stopped = False
    try:
        _build_body(nc, es, T, n_layers, dbg)
    except _Stop:
        stopped = True
    T.final()
    print("instr counts:", {k: v["n"] for k, v in T.eng.items()}, "nsems", len(T.sems))
    if not stopped:
        es.close()
    return nc


def _build_body(nc, es, T, n_layers, dbg):

    _uid = [0]

    def sbt(name, shape, dt):
        _uid[0] += 1
        return nc.sbuf_tensor(f"{name}_{_uid[0]}", list(shape), dt)

    def dram(name, shape, dt=F32, kind="ExternalInput"):
        return nc.dram_tensor(name, list(shape), dt, kind=kind).ap()

    x_d = dram("x", [S, D])
    cbc_d = dram("c_bc", [P, KC, P])
    ccol_d = dram("c_col", [P, KC])
    wada_d = dram("w_ada", [DEPTH, D, 3 * D])
    bada_d = dram("b_ada", [DEPTH, 3 * D])
    bgate_d = dram("b_gate_bc", [DEPTH, P, D])
    wperm_d = dram("w_perm", [DEPTH, D, TOTC])
    bf_d = dram("b_f", [DEPTH, P, 8])
    peT_d = dram("cmp_peT", [DEPTH, 2, P, 32])
    w1_d = dram("cmp_w1r", [DEPTH, 2, P, 32 * 64])
    w2_d = dram("cmp_w2r", [DEPTH, 2, P, 64])
    wout_d = dram("w_out", [DEPTH, D, D])
    lng_d = dram("ln_g_bc", [DEPTH, P, D])
    lnb_d = dram("ln_b_bc", [DEPTH, P, D])
    cos64_d = dram("cos64", [P, S], BF16)
    sin64_d = dram("sin64", [P, S], BF16)
    cos32_d = dram("cos32", [P, S], BF16)
    sin32_d = dram("sin32", [P, S], BF16)
    cst_d = dram("cst", [P, 10, P])
    cstb_d = dram("cst_b", [P, 10, P], BF16)
    cmpm_d = dram("cmp_mask", [NT, P, P])
    fb_d = dram("sel_fb", [NT, P, 32])
    out_d = dram("out", [S, D], kind="ExternalOutput")
    x1_d = dram("x1_scratch", [S, D], kind="Internal")
    dbg_d = {}
    if "mix" in dbg:
        dbg_d["mix"] = dram("dbg_mix", [S, D], BF16, kind="ExternalOutput")

    def sb(name, shape, dt=F32):
        return es.enter_context(sbt(name, list(shape), dt))

    def pst(name):
        return es.enter_context(nc.psum_tensor(name, [P, 512], F32))

    PS = [pst(f"ps{i}") for i in range(8)]
    pskey = [f"ps{i}" for i in range(8)]

    uT = sb("uT", [P, KC, S], BF16)
    mix = sb("mix", [P, NT, D], BF16)
    cstf = sb("cstf", [P, 10, P], F32)
    ident_f = cstf[:, 0, :]
    causneg = cstf[:, 3, :]
    cstb = sb("cstb", [P, 10, P], BF16)
    ident_b = cstb[:, 0, :]
    caus01 = cstb[:, 1, :]
    band01 = cstb[:, 2, :]
    identbig = cstb[:, 4, :]
    ones_b = cstb[:, 5, :]
    perm64 = cstb[:, 7, :]
    perm32 = cstb[:, 8, :]
    shiftm = cstb[:, 9, :]
    cos64 = sb("cos64s", [P, S], BF16)
    sin64 = sb("sin64s", [P, S], BF16)
    cos32 = sb("cos32s", [P, S], BF16)
    sin32 = sb("sin32s", [P, S], BF16)
    modT = sb("modT", [P, DEPTH, 24], F32)
    g1bc = sb("g1bc", [P, DEPTH, D], F32)
    ccol = sb("ccol", [P, KC], F32)

    T.dma(lambda: nc.sync.dma_start(out=cstf[:], in_=cst_d[:, :, :]), "cstf", writes=["cstf"])
    T.dma(lambda: nc.sync.dma_start(out=cstb[:], in_=cstb_d[:, :, :]), "cstb", writes=["cstb"])
    T.dma(lambda: nc.sync.dma_start(out=ccol[:], in_=ccol_d[:, :]), "ccol", writes=["ccol"])

    xin_cm = sbt("xin", [P, NT, D], F32)
    xin = xin_cm.__enter__()
    for q4 in range(4):
        T.dma(lambda q4=q4: nc.sync.dma_start(out=xin[:, 4 * q4:4 * q4 + 4, :],
                                              in_=x_d[q4 * 512:(q4 + 1) * 512, :].rearrange("(t p) d -> p t d", p=P)),
              f"xin{q4}", writes=[f"xin{q4}"])

    for src, dst, nm in ((cos64_d, cos64, "cos64"), (sin64_d, sin64, "sin64"), (cos32_d, cos32, "cos32"), (sin32_d, sin32, "sin32")):
        T.dma(lambda src=src, dst=dst: nc.sync.dma_start(out=dst[:], in_=src[:, :]), nm, writes=[nm])

    with ExitStack() as es2:
        cbc = es2.enter_context(sbt("cbc", [P, KC, P], F32))
        wst = es2.enter_context(sbt("wadast", [P, 2, KC, 512], F32))
        brow = es2.enter_context(sbt("brow", [1, 512], F32))
        mrow = es2.enter_context(sbt("mrow", [1, 512], F32))
        bg = es2.enter_context(sbt("bgate", [P, D], F32))
        T.dma(lambda: nc.sync.dma_start(out=cbc[:], in_=cbc_d[:, :, :]), "cbc", writes=["cbc"])
        blk = 0
        for l in range(n_layers):
            T.dma(lambda l=l: nc.sync.dma_start(out=bg[:], in_=bgate_d[l, :, :]), "bg", writes=["bg"])
            for b6 in range(6):
                slot = blk % 2
                blk += 1
                src = wada_d[l, :, b6 * 512:(b6 + 1) * 512].rearrange("(kc k) n -> k kc n", k=P)
                T.dma(lambda src=src, slot=slot: nc.sync.dma_start(out=wst[:, slot, :, :], in_=src),
                      f"wst{slot}", writes=[f"wst{slot}"])
                if b6 < 4:
                    for kc in range(KC):
                        T.op("pe", lambda slot=slot, kc=kc: nc.tensor.matmul(
                            PS[3][0:1, :], lhsT=ccol[:, kc:kc + 1], rhs=wst[:, slot, kc, :],
                            start=(kc == 0), stop=(kc == KC - 1)),
                            reads=[f"wst{slot}", "ccol"], writes=["ps3"])
                    T.dma(lambda l=l, b6=b6: nc.sync.dma_start(out=brow[0:1, :], in_=bada_d[l:l + 1, b6 * 512:(b6 + 1) * 512]),
                          "brow", writes=["brow"])
                    T.op("dve", lambda: nc.vector.tensor_tensor(out=mrow[0:1, :], in0=PS[3][0:1, :], in1=brow[0:1, :], op=ALU.add),
                         reads=["ps3", "brow"], writes=["mrow"])
                    for f in range(4):
                        j = b6 * 4 + f
                        T.op("pe", lambda f=f, j=j: nc.tensor.matmul(
                            PS[0][:, j:j + 1], lhsT=mrow[0:1, f * P:(f + 1) * P], rhs=cstf[0:1, 0, 0:1],
                            start=True, stop=True),
                            reads=["mrow", "cstf"], writes=["ps0"])
                else:
                    g = b6 - 4
                    for kc in range(KC):
                        T.op("pe", lambda slot=slot, kc=kc, g=g: nc.tensor.matmul(
                            PS[1 + g][:, :], lhsT=cbc[:, kc, :], rhs=wst[:, slot, kc, :],
                            start=(kc == 0), stop=(kc == KC - 1)),
                            reads=[f"wst{slot}", "cbc"], writes=[pskey[1 + g]])
                    T.op("dve", lambda l=l, g=g: nc.vector.tensor_tensor(
                        out=g1bc[:, l, g * 512:(g + 1) * 512], in0=PS[1 + g][:, :],
                        in1=bg[:, g * 512:(g + 1) * 512], op=ALU.add),
                        reads=[pskey[1 + g], "bg"], writes=["g1bc"])
            T.op("dve", lambda l=l: nc.vector.tensor_copy(out=modT[:, l, 0:16], in_=PS[0][:, 0:16]),
                 reads=["ps0"], writes=["modT"])
            T.op("dve", lambda l=l: nc.vector.tensor_scalar(out=modT[:, l, 8:16], in0=modT[:, l, 8:16],
                                                            scalar1=1.0, scalar2=None, op0=ALU.add),
                 reads=["modT"], writes=["modT"])
            T.op("dve", lambda l=l: nc.vector.tensor_scalar(out=g1bc[:, l, :], in0=g1bc[:, l, :],
                                                            scalar1=1.0, scalar2=None, op0=ALU.add),
                 reads=["g1bc"], writes=["g1bc"])
        T.barrier()
    _chk("mod")

    def _dump():
        if "mix" in dbg:
            T.barrier()
            for i in range(NT):
                T.dma(lambda i=i: nc.sync.dma_start(out=dbg_d["mix"][i * P:(i + 1) * P, :], in_=mix[:, i, :]),
                      "store", reads=[f"mix{i}"], writes=[f"dbgmix{i}"])
    _DUMP[0] = _dump

    def make_uT_tile(src_tile, src_key, l, it, bank0=2):
        for half in range(2):
            bank = bank0 + half
            for q in range(4):
                kc = half * 4 + q
                T.op("pe", lambda kc=kc, q=q, bank=bank: nc.tensor.transpose(
                    PS[bank][:, q * P:(q + 1) * P], src_tile[:, kc * P:(kc + 1) * P], ident_f),
                    reads=[src_key, "cstf"], writes=[pskey[bank]])
            for q in range(4):
                kc = half * 4 + q
                if q % 2 == 0 or bank0 != 2:
                    T.op("act", lambda kc=kc, q=q, bank=bank: nc.scalar.activation(
                        out=uT[:, kc, it * P:(it + 1) * P], in_=PS[bank][:, q * P:(q + 1) * P],
                        func=AF.Identity, bias=modT[:, l, kc:kc + 1], scale=modT[:, l, 8 + kc:9 + kc]),
                        reads=[pskey[bank], "modT"], writes=[f"uT{it}"])
                else:
                    T.op("dve", lambda kc=kc, q=q, bank=bank: nc.vector.tensor_scalar(
                        out=uT[:, kc, it * P:(it + 1) * P], in0=PS[bank][:, q * P:(q + 1) * P],
                        scalar1=modT[:, l, 8 + kc:9 + kc], scalar2=modT[:, l, kc:kc + 1], op0=ALU.mult, op1=ALU.add),
                        reads=[pskey[bank], "modT"], writes=[f"uT{it}"])

    wslot_ctr = [0]

    wplan = []
    wready = {}

    def plan_w(names):
        wplan[:] = list(names)

    NSLOT = 2
    wdma = {}
    wcast_ctr = [0]

    def _dma_w(l, wst, name):
        slot = wslot_ctr[0] % NSLOT
        wslot_ctr[0] += 1
        o0, tot = UOFF[name]
        assert tot <= 128
        src = wperm_d[l, :, o0:o0 + tot].rearrange("(kc k) n -> k kc n", k=P)
        T.dma(lambda: nc.sync.dma_start(out=wst[:, slot, :, 0:tot], in_=src), f"wst{slot}", writes=[f"wst{slot}"])
        wdma[name] = (slot, tot)

    def _cast_w(wst, wbf, name):
        slot, tot = wdma.pop(name)
        bs = wcast_ctr[0] % 2
        wcast_ctr[0] += 1
        T.op("act", lambda: nc.scalar.copy(out=wbf[:, bs, :, 0:tot], in_=wst[:, slot, :, 0:tot]),
             reads=[f"wst{slot}"], writes=[f"wbf{bs}"])
        wready[name] = (bs, f"wbf{bs}", {name: (0, tot)})

    def load_w(l, wst, wbf, names):
        name = names[0]
        assert len(names) == 1
        if name not in wready:
            if name not in wdma:
                _dma_w(l, wst, name)
            _cast_w(wst, wbf, name)
        res = wready.pop(name)
        if wplan and wplan[0] == name:
            wplan.pop(0)
        for k_, nm in enumerate(wplan[:2]):
            if nm not in wready and nm not in wdma:
                _dma_w(l, wst, nm)
        if wplan and wplan[0] in wdma:
            _cast_w(wst, wbf, wplan[0])
        return res

    pbank = [0]

    def projT(wbf, slot, wkey, off, ncols, tc, bank):
        assert off == 0
        for kc in range(KC):
            T.op("pe", lambda kc=kc: nc.tensor.matmul(
                PS[bank][:, :], lhsT=wbf[:, slot, kc, :],
                rhs=uT[:, kc, tc * 512:(tc + 1) * 512], start=(kc == 0), stop=(kc == KC - 1)),
                reads=[wkey] + [f"uT{4 * tc + i}" for i in range(4)], writes=[pskey[bank]])

    def proj_plain(l, wst, wbf, name, dst, dkey):
        slot, wkey, offs = load_w(l, wst, wbf, [name])
        off, ncols = offs[name]
        for tc in range(4):
            bank = pbank[0] % 4
            pbank[0] += 1
            projT(wbf, slot, wkey, off, ncols, tc, bank)
            T.op("act", lambda tc=tc, bank=bank: nc.scalar.copy(
                out=dst[0:ncols, tc * 512:(tc + 1) * 512], in_=PS[bank][0:ncols, :]),
                reads=[pskey[bank]], writes=[dkey])

    def proj_rope(l, wst, wbf, name, dst, dkey, cosT, sinT, tkeys, tmp, raw_dst=None, raw_key=None, halves=False):
        slot, wkey, offs = load_w(l, wst, wbf, [name])
        perm = perm64 if tkeys[0] == "cos64" else perm32
        n_ = offs[name][1]
        banks = {}

        def stage1(tc):
            b0 = pbank[0] % 4
            b1 = (pbank[0] + 1) % 4
            pbank[0] += 2
            banks[tc] = (b0, b1)
            projT(wbf, slot, wkey, offs[name][0], n_, tc, b0)
            xs = tc % 2
            T.op("act", lambda: nc.scalar.copy(out=XB[:, xs, :], in_=PS[b0][:, :]),
                 reads=[pskey[b0]], writes=[f"XB{xs}"])

        def stage2(tc):
            b0, b1 = banks[tc]
            xs = tc % 2
            sl = slice(tc * 512, (tc + 1) * 512)
            T.op("pe", lambda: nc.tensor.matmul(PS[b1][:, :], lhsT=perm, rhs=XB[:, xs, :], start=True, stop=True),
                 reads=[f"XB{xs}", "cstb"], writes=[pskey[b1]])
            if raw_dst is not None:
                T.op("act", lambda: nc.scalar.copy(out=raw_dst[0:n_, sl], in_=XB[0:n_, xs, :]),
                     reads=[f"XB{xs}"], writes=[raw_key])
            T.op("dve", lambda: nc.vector.tensor_tensor(out=tmp[0:n_, 0, :], in0=PS[b0][0:n_, :], in1=cosT[0:n_, sl], op=ALU.mult),
                 reads=[pskey[b0], tkeys[0]], writes=["ropetmp0"])
            T.op("dve", lambda: nc.vector.tensor_tensor(out=tmp[0:n_, 1, :], in0=PS[b1][0:n_, :], in1=sinT[0:n_, sl], op=ALU.mult),
                 reads=[pskey[b1], tkeys[1]], writes=["ropetmp1"])
            if halves:
                T.op("dve", lambda: nc.vector.tensor_tensor(out=dst[0:64, 0, sl], in0=tmp[0:64, 0, :], in1=tmp[0:64, 1, :], op=ALU.add),
                     reads=["ropetmp0", "ropetmp1"], writes=[dkey])
                T.op("dve", lambda: nc.vector.tensor_tensor(out=dst[64:128, 1, sl], in0=tmp[64:128, 0, :], in1=tmp[64:128, 1, :], op=ALU.add),
                     reads=["ropetmp0", "ropetmp1"], writes=[dkey])
            else:
                T.op("dve", lambda: nc.vector.tensor_tensor(out=dst[0:n_, sl], in0=tmp[0:n_, 0, :], in1=tmp[0:n_, 1, :], op=ALU.add),
                     reads=["ropetmp0", "ropetmp1"], writes=[dkey])

        stage1(0)
        for tc in range(4):
            if tc + 1 < 4:
                stage1(tc + 1)
            stage2(tc)

    def proj_tok(l, wst, wbf, name, evac):
        slot, wkey, offs = load_w(l, wst, wbf, [name])
        off, ncols = offs[name]
        for g in range(4):
            bank = pbank[0] % 4
            pbank[0] += 1
            for q in range(4):
                it = 4 * g + q
                for kc in range(KC):
                    T.op("pe", lambda kc=kc, q=q, it=it: nc.tensor.matmul(
                        PS[bank][:, q * P:q * P + ncols], lhsT=uT[:, kc, it * P:(it + 1) * P],
                        rhs=wbf[:, slot, kc, off:off + ncols], start=(kc == 0), stop=(kc == KC - 1)),
                        reads=[wkey, f"uT{it}"], writes=[pskey[bank]])
            view = PS[bank][:, :].rearrange("p (q c) -> p q c", q=4)[:, :, 0:ncols]
            evac(g, view, pskey[bank])

    acc_ctr = [0]
    sb_ctr = [0]

    def attn_core(PT, i, ktiles, qk, extra, vext, vkey, finish):
        for _ in attn_core_g(PT, i, ktiles, qk, extra, vext, vkey, finish):
            pass

    def interleave(a, b):
        a = iter(a)
        b = iter(b)
        da = db = False
        while not (da and db):
            if not da:
                try:
                    next(a)
                except StopIteration:
                    da = True
            if not db:
                try:
                    next(b)
                except StopIteration:
                    db = True

    def _ov(out_ap, rhs):
        if len(rhs.shape) == 3:
            return out_ap.rearrange("p (h t) -> p h t", h=rhs.shape[1])
        return out_ap

    pending_fin = []

    def flush_fin():
        while pending_fin:
            pending_fin.pop(0)()

    def attn_wide_g(PT, ncol, ktl, qk, extra, vext, vkey, finish):
        abank = 6 + (acc_ctr[0] % 2)
        acc_ctr[0] += 1
        nk = len(ktl)
        slots = []

        def scores(n):
            j, c0 = ktl[n]
            sbank = 3 + (sb_ctr[0] % 3)
            ptslot = sb_ctr[0] % 3
            sb_ctr[0] += 1
            slots.append(ptslot)
            lhsT, rhs, keys = qk(j, c0)
            ex = extra(j, c0)
            T.op("pe", lambda: nc.tensor.matmul(_ov(PS[sbank][:, c0:ncol], rhs), lhsT=lhsT, rhs=rhs, start=True,
                                                stop=(len(ex) == 0)),
                 reads=keys, writes=[pskey[sbank]])
            for n_, (l2, r2, k2, (e0, e1)) in enumerate(ex):
                T.op("pe", lambda: nc.tensor.matmul(_ov(PS[sbank][:, e0:e1], r2), lhsT=l2, rhs=r2, start=False,
                                                    stop=(n_ == len(ex) - 1)),
                     reads=k2, writes=[pskey[sbank]])
            T.op("act", lambda: nc.scalar.activation(out=PT[:, ptslot, c0:ncol], in_=PS[sbank][:, c0:ncol], func=AF.Exp, scale=0.125),
                 reads=[pskey[sbank]], writes=[f"PT{ptslot}"])

        def pv(n):
            j, c0 = ktl[n]
            ptslot = slots[n]
            T.op("pe", lambda: nc.tensor.matmul(PS[abank][0:65, c0:ncol], lhsT=vext(j), rhs=PT[:, ptslot, c0:ncol],
                                                start=(n == 0), stop=(n == nk - 1)),
                 reads=[f"PT{ptslot}", vkey], writes=[pskey[abank]])

        scores(0)
        if nk > 1:
            scores(1)
        for n in range(nk):
            if n + 2 < nk:
                scores(n + 2)
            pv(n)
            if n == 0:
                flush_fin()
            yield
        finish(abank)
        yield

    ot_ctr = [0]
    fw_banks = [[0, 1, 2]]

    def finish_wide(abank, ncol, nb, emit_block):
        os_ = ot_ctr[0] % 2
        c = ot_ctr[0] % 8
        ot_ctr[0] += 1
        T.op("dve", lambda: nc.vector.tensor_copy(out=OTs[0:65, os_, 0:ncol], in_=PS[abank][0:65, 0:ncol]),
             reads=[pskey[abank]], writes=[f"OTs{os_}"])

        def part2():
            tb = fw_banks[0][pbank[0] % len(fw_banks[0])]
            pbank[0] += 1
            for b in range(nb):
                T.op("pe", lambda: nc.tensor.transpose(PS[tb][:, b * 65:(b + 1) * 65], OTs[0:65, os_, b * P:(b + 1) * P],
                                                       ident_f[0:65, 0:65]),
                     reads=[f"OTs{os_}", "cstf"], writes=[pskey[tb]])
            T.op("dve", lambda: nc.vector.reciprocal(out=rsm[:, 4 * c:4 * c + nb], in_=PS[tb][:, 64:64 + 65 * (nb - 1) + 1:65]),
                 reads=[pskey[tb]], writes=[f"rsm{c}"])
            for b in range(nb):
                emit_block(b, PS[tb][:, b * 65:b * 65 + 64], pskey[tb], rsm[:, 4 * c + b:4 * c + b + 1], f"rsm{c}", 4 * c + b)
        pending_fin.append(part2)

    def attn_core_g(PT, i, ktiles, qk, extra, vext, vkey, finish):
        abank = 6 + (acc_ctr[0] % 2)
        acc_ctr[0] += 1
        nk = len(ktiles)
        for g0 in range(0, nk, 4):
            grp = ktiles[g0:g0 + 4]
            sbank = 4 + (sb_ctr[0] % 2)
            ptslot = sb_ctr[0] % 2
            sb_ctr[0] += 1
            for q, j in enumerate(grp):
                lhsT, rhs, keys = qk(j)
                ex = extra(j)
                T.op("pe", lambda lhsT=lhsT, rhs=rhs, q=q, ex=ex: nc.tensor.matmul(
                    PS[sbank][:, q * P:(q + 1) * P], lhsT=lhsT, rhs=rhs, start=True, stop=(len(ex) == 0)),
                    reads=keys, writes=[pskey[sbank]])
                for n_, (l2, r2, k2) in enumerate(ex):
                    T.op("pe", lambda l2=l2, r2=r2, q=q, n_=n_, ex=ex: nc.tensor.matmul(
                        PS[sbank][:, q * P:(q + 1) * P], lhsT=l2, rhs=r2, start=False, stop=(n_ == len(ex) - 1)),
                        reads=k2, writes=[pskey[sbank]])
            w = len(grp) * P
            T.op("act", lambda w=w, ptslot=ptslot, sbank=sbank: nc.scalar.activation(
                out=PT[:, ptslot, 0:w], in_=PS[sbank][:, 0:w], func=AF.Exp, scale=0.125),
                reads=[pskey[sbank]], writes=[f"PT{ptslot}"])
            for q, j in enumerate(grp):
                first = (g0 == 0 and q == 0)
                last = (g0 + q == nk - 1)
                T.op("pe", lambda q=q, j=j, first=first, last=last, ptslot=ptslot: nc.tensor.matmul(
                    PS[abank][:, 0:65], lhsT=PT[:, ptslot, q * P:(q + 1) * P], rhs=vext(j),
                    start=first, stop=last),
                    reads=[f"PT{ptslot}", vkey], writes=[pskey[abank]])
            yield
        finish(PS[abank][:, 0:65], pskey[abank])
        yield

    xt_ctr = [0]
    for l in range(n_layers):
        if l == 0:
            for it in range(NT):
                make_uT_tile(xin[:, it, :], f"xin{it // 4}", 0, it)
            T.barrier()
            xin_cm.__exit__(None, None, None)
            _chk("uT")

        with ExitStack() as esl:
            wst = esl.enter_context(sbt("wst", [P, 2, KC, 128], F32))
            wbf = esl.enter_context(sbt("wbf", [P, 2, KC, 128], BF16))
            T.op("pool", lambda: nc.gpsimd.memset(wbf[:], 0.0), writes=["wbf0", "wbf1"])
            PT = esl.enter_context(sbt("PT", [P, 3, 512], BF16))
            rsm = esl.enter_context(sbt("rsm", [P, 32], F32))
            OTs = esl.enter_context(sbt("OTs", [P, 2, 512], F32))
            XB = esl.enter_context(sbt("XB", [P, 2, 512], BF16))
            ID4 = esl.enter_context(sbt("ID4", [P, 4, P], BF16))
            for q4 in range(4):
                T.op("pool", lambda: nc.gpsimd.tensor_copy(out=ID4[:, q4, :], in_=identbig), reads=["cstb"], writes=["ID4"])

            with ExitStack() as espf:
              FQp = espf.enter_context(sbt("FQp", [P, 6, S], BF16))
              FKp = espf.enter_context(sbt("FKp", [P, 6, S], BF16))
              if True:
                esp = espf
                CC = esp.enter_context(sbt("CC", [6, 2, S], F32))
                SP3 = esp.enter_context(sbt("SP3", [6, 3, S], BF16))
                bfn = esp.enter_context(sbt("bfn", [P, 8], F32))
                FV = esp.enter_context(sbt("FV", [P, NT, 6, 65], BF16))
                plan_w(["ff"] + [f"{p}{i}" for i in range(3) for p in ("fq", "fk", "fv")])
                T.op("pool", lambda: nc.gpsimd.memset(FKp[64:128, :, :], -1.0), writes=["FKpa"])
                T.op("pool", lambda: nc.gpsimd.memset(FQp[64:128, :, :], 0.0), writes=["FQpa"])
                T.op("pool", lambda: nc.gpsimd.memset(FQp[64:67, :, :], 1.0), writes=["FQpa"])
                _chk("f0")
                T.dma(lambda: nc.sync.dma_start(out=bfn[:], in_=bf_d[l, :, :]), "bfn", writes=["bfn"])
                _chk("f1")
                T.op("dve", lambda: nc.vector.tensor_scalar(out=bfn[:], in0=bfn[:], scalar1=-1.0, scalar2=None, op0=ALU.mult),
                     reads=["bfn"], writes=["bfn"])
                _chk("fa")
                slot, wkey, offs = load_w(l, wst, wbf, ["ff"])
                _chk("fb")
                for tc in range(4):
                    bank = pbank[0] % 4
                    pbank[0] += 1
                    projT(wbf, slot, wkey, offs["ff"][0], 6, tc, bank)
                    if tc == 0:
                        _chk("fc")
                    sl = slice(tc * 512, (tc + 1) * 512)
                    T.op("act", lambda: nc.scalar.activation(out=CC[:, 0, sl], in_=PS[bank][0:6, :], func=AF.Exp,
                                                            bias=bfn[0:6, 0:1], scale=-1.0),
                         reads=[pskey[bank], "bfn"], writes=["CC0"])
                    if tc == 0:
                        _chk("fd")
                    T.op("act", lambda: nc.scalar.activation(out=CC[:, 0, sl], in_=CC[:, 0, sl], func=AF.Ln,
                                                            bias=1.0, scale=1.0),
                         reads=["CC0"], writes=["CC0"])
                _chk("fp1")
                T.op("dve", lambda: nc.vector.tensor_tensor_scan(out=CC[:, 1, :], data0=CC[:, 0, :], data1=CC[:, 0, :],
                                                                 initial=0.0, op0=ALU.add, op1=ALU.max),
                     reads=["CC0"], writes=["CC1"])
                T.op("dve", lambda: nc.vector.tensor_scalar(out=CC[:, 1, :], in0=CC[:, 1, :], scalar1=8.0, scalar2=None, op0=ALU.mult),
                     reads=["CC1"], writes=["CC1"])
                cur = 1
                for r in range(3):
                    T.op("dve", lambda: nc.vector.tensor_copy(out=SP3[:, r, :], in_=CC[:, cur, :]),
                         reads=[f"CC{cur}"], writes=[f"SP{r}"])
                    if r < 2:
                        nxt = 1 - cur
                        T.op("dve", lambda: nc.vector.tensor_tensor(
                            out=CC[:, nxt, :], in0=CC[:, cur, :], in1=SP3[:, r, :], op=ALU.subtract),
                            reads=[f"CC{cur}", f"SP{r}"], writes=[f"CC{nxt}"])
                        cur = nxt
                _chk("fp2")
                _chk("fp3")
                _chk("foxpre")
                T.op("pool", lambda: nc.gpsimd.memset(FV[:, :, :, 64:65], 1.0), writes=["FV"])

                def proj_pair(name, dstp, dkey, i):
                    slot, wkey, offs = load_w(l, wst, wbf, [name])
                    off, ncols = offs[name]
                    banks = {}

                    def st1(tc):
                        b0 = pbank[0] % 4
                        b1 = (pbank[0] + 1) % 4
                        pbank[0] += 2
                        banks[tc] = (b0, b1)
                        projT(wbf, slot, wkey, off, ncols, tc, b0)
                        xs = tc % 2
                        T.op("act", lambda: nc.scalar.copy(out=XB[:, xs, :], in_=PS[b0][:, :]),
                             reads=[pskey[b0]], writes=[f"XB{xs}"])

                    def st2(tc):
                        b0, b1 = banks[tc]
                        sl = slice(tc * 512, (tc + 1) * 512)
                        xs = tc % 2
                        T.op("pe", lambda: nc.tensor.matmul(PS[b1][0:64, :], lhsT=shiftm[:, 0:64], rhs=XB[:, xs, :], start=True, stop=True),
                             reads=[f"XB{xs}", "cstb"], writes=[pskey[b1]])
                        T.op("dve", lambda: nc.vector.tensor_copy(out=dstp[0:64, 2 * i, sl], in_=XB[0:64, xs, :]),
                             reads=[f"XB{xs}"], writes=[dkey])
                        T.op("act", lambda: nc.scalar.copy(out=dstp[0:64, 2 * i + 1, sl], in_=PS[b1][0:64, :]),
                             reads=[pskey[b1]], writes=[dkey])

                    st1(0)
                    for tc in range(4):
                        if tc + 1 < 4:
                            st1(tc + 1)
                        st2(tc)

                for i in range(3):
                    proj_pair(f"fq{i}", FQp, "FQp", i)
                    proj_pair(f"fk{i}", FKp, "FKp", i)

                    def ev(g, view, pk, i=i):
                        T.op("act", lambda: nc.scalar.copy(
                            out=FV[:, 4 * g:4 * g + 4, 2 * i:2 * i + 2, 0:64],
                            in_=view.rearrange("p q (h d) -> p q h d", h=2)),
                            reads=[pk], writes=["FV"])
                    proj_tok(l, wst, wbf, f"fv{i}", ev)
                for h in range(6):
                    for r in range(3):
                        T.dma(lambda: nc.sync.dma_start(out=FKp[64 + r:65 + r, h, :], in_=SP3[h:h + 1, r, :]),
                              "FKpa", reads=[f"SP{r}"], writes=["FKpa"], nowaw=(h + r > 0))
                        T.dma(lambda: nc.sync.dma_start(out=FQp[67 + r:68 + r, h, :], in_=SP3[h:h + 1, r, :]),
                              "FQpa", reads=[f"SP{r}"], writes=["FQpa"], nowaw=(h + r > 0))
                for c4 in range(4):
                    for h in range(6):
                        hp = 64 * (h % 2)
                        pb = 32 * (h % 3)
                        ktl = [(j, 0) for j in range(4 * c4)] + [(4 * c4 + jl, P * jl) for jl in range(4)]
                        q0 = c4 * 512

                        def qk(j, c0, h=h, q0=q0):
                            return (FKp[:, h, j * P:(j + 1) * P], FQp[:, h, q0 + c0:q0 + 512],
                                    ["FKp", "FQp", "FKpa", "FQpa"])

                        def extra(j, c0, c4=c4):
                            if j >= 4 * c4:
                                return [(caus01, identbig, ["cstb"], (c0, c0 + P))]
                            return []

                        def fin(abank, h=h, c4=c4):
                            def blk(b, src, skey, rs, rkey, rcol=None):
                                T.op("dve", lambda: nc.vector.tensor_scalar(
                                    out=mix[:, 4 * c4 + b, 256 + 64 * h:256 + 64 * (h + 1)], in0=src,
                                    scalar1=rs, scalar2=None, op0=ALU.mult),
                                    reads=[skey, rkey], writes=[f"mix{4 * c4 + b}"])
                            finish_wide(abank, 512, 4, blk)
                        for _ in attn_wide_g(PT, 512, ktl, qk, extra, lambda j, h=h: FV[:, j, h, :], "FV", fin):
                            pass
                    if c4 == 0:
                        _chk("fox0")
                flush_fin()
                T.barrier()
            _chk("fox")

            with ExitStack() as esp:
                DQ = esp.enter_context(sbt("DQ", [P, 4, S], BF16))
                DK = esp.enter_context(sbt("DK", [P, S], BF16))
                IQ = esp.enter_context(sbt("IQ", [P, 3, S], BF16))
                IK = esp.enter_context(sbt("IK", [P, 3, S], BF16))
                DV = esp.enter_context(sbt("DV", [P, NT, 65], BF16))
                IW = esp.enter_context(sbt("IW", [P, NT, 3, 8], F32))
                rtmp = esp.enter_context(sbt("rtmp", [P, 2, 512], F32))
                ISC = esp.enter_context(sbt("ISC", [P, 2, S], F32))
                WKb = esp.enter_context(sbt("WKb", [P, S], BF16))
                MB = esp.enter_context(sbt("MB", [P, 2, S], BF16))
                RL = esp.enter_context(sbt("RL", [P, 2, 512], F32))
                bis = esp.enter_context(sbt("bis", [P, 2, 8], F32))
                Hp = esp.enter_context(sbt("Hp", [P, 2, 24], F32))
                Hn = esp.enter_context(sbt("Hn", [P, 2, 24], F32))
                WKa = rtmp[:].rearrange("p a b -> p (a b)").bitcast(BF16)
                T.op("pool", lambda: nc.gpsimd.memset(DV[:, :, 64:65], 1.0), writes=["DV"])
                _chk("d0")
                plan_w(["dq0", "dq1", "dq2", "dq3", "iq0", "iq1", "iq2", "dk", "ik", "dvw"])
                T.op("pool", lambda: nc.gpsimd.memset(DQ[64:128, :, :], 0.0), writes=["DQ"])
                T.op("pool", lambda: nc.gpsimd.memset(DK[64:128, :], 0.0), writes=["DK"])
                for i in range(4):
                    proj_rope(l, wst, wbf, f"dq{i}", DQ[:, i, :], "DQ", cos64, sin64, ("cos64", "sin64"), rtmp)
                _chk("d1")
                T.op("pool", lambda: nc.gpsimd.memset(IQ[:], 0.0), writes=["IQ0", "IQ1", "IQ2"])
                for i in range(3):
                    proj_rope(l, wst, wbf, f"iq{i}", IQ[:, i, :], f"IQ{i}", cos32, sin32, ("cos32", "sin32"), rtmp)
                _chk("d2")
                proj_rope(l, wst, wbf, "dk", DK, "DK", cos64, sin64, ("cos64", "sin64"), rtmp)
                proj_rope(l, wst, wbf, "ik", IK[:, 0, :], "IK", cos32, sin32, ("cos32", "sin32"), rtmp)
                T.op("pool", lambda: nc.gpsimd.memset(IK[:, 1, :], 0.0), writes=["IK"])
                T.op("pool", lambda: nc.gpsimd.memset(IK[:, 2, :], 0.0), writes=["IK"])
                T.op("pool", lambda: nc.gpsimd.tensor_copy(out=IK[32:64, 1, :], in_=IK[32:64, 0, :]), reads=["IK"], writes=["IK"])
                T.op("pool", lambda: nc.gpsimd.tensor_copy(out=IK[64:96, 2, :], in_=IK[64:96, 0, :]), reads=["IK"], writes=["IK"])
                T.op("pool", lambda: nc.gpsimd.memset(IK[32:64, 0, :], 0.0), reads=["IK"], writes=["IK"])
                T.op("pool", lambda: nc.gpsimd.memset(IK[64:128, 0, :], 0.0), reads=["IK"], writes=["IK"])
                _chk("d3")

                def ev(g, view, pk):
                    T.op("act", lambda: nc.scalar.copy(out=DV[:, 4 * g:4 * g + 4, 0:64], in_=view[:, :, 0:64]),
                         reads=[pk], writes=["DV"])
                    T.op("dve", lambda: nc.vector.tensor_copy(out=IW[:, 4 * g:4 * g + 4, 0, :], in_=view[:, :, 64:72]),
                         reads=[pk], writes=["IW"])
                proj_tok(l, wst, wbf, "dvw", ev)
                _chk("d4")
                T.op("dve", lambda: nc.vector.tensor_scalar(out=IW[:, :, 2, :], in0=IW[:, :, 0, :], scalar1=0.0, scalar2=2.0,
                                                            op0=ALU.is_ge, op1=ALU.mult),
                     reads=["IW"], writes=["IW"])
                T.op("dve", lambda: nc.vector.tensor_scalar(out=IW[:, :, 2, :], in0=IW[:, :, 2, :], scalar1=-1.0, scalar2=None,
                                                            op0=ALU.add),
                     reads=["IW"], writes=["IW"])
                T.op("dve", lambda: nc.vector.tensor_tensor(out=IW[:, :, 1, :], in0=IW[:, :, 0, :], in1=IW[:, :, 2, :], op=ALU.mult),
                     reads=["IW"], writes=["IW"])
                rl_ctr = [0]
                K_BIS = 16
                _chk("dsa_p")

                def indexer(i):
                    b_ = i % 2
                    L = (i + 1) * P
                    ik_ = f"ISC{b_}"
                    for c0 in range(0, L, 512):
                        w = min(512, L - c0)
                        for h in range(8):
                            bank = pbank[0] % 3
                            pbank[0] += 1
                            T.op("pe", lambda: nc.tensor.matmul(
                                PS[bank][:, 0:w], lhsT=IQ[:, h // 3, i * P:(i + 1) * P],
                                rhs=IK[:, h % 3, c0:c0 + w], start=True, stop=True),
                                reads=[f"IQ{h // 3}", "IK"], writes=[pskey[bank]])
                            rs = rl_ctr[0] % 2
                            rl_ctr[0] += 1
                            T.op("act", lambda: nc.scalar.activation(
                                out=RL[:, rs, 0:w], in_=PS[bank][:, 0:w], func=AF.Relu, scale=IW[:, i, 1, h:h + 1]),
                                reads=[pskey[bank], "IW"], writes=[f"RL{rs}"])
                            if h == 0:
                                T.op("dve", lambda: nc.vector.tensor_scalar(
                                    out=ISC[:, b_, c0:c0 + w], in0=RL[:, rs, 0:w], scalar1=IW[:, i, 2, h:h + 1], scalar2=None,
                                    op0=ALU.mult), reads=[f"RL{rs}", "IW"], writes=[ik_])
                            else:
                                T.op("dve", lambda: nc.vector.scalar_tensor_tensor(
                                    out=ISC[:, b_, c0:c0 + w], in0=RL[:, rs, 0:w], scalar=IW[:, i, 2, h:h + 1],
                                    in1=ISC[:, b_, c0:c0 + w], op0=ALU.mult, op1=ALU.add),
                                    reads=[f"RL{rs}", "IW", ik_], writes=[ik_])
                    T.op("dve", lambda: nc.vector.tensor_reduce(out=bis[:, b_, 0:1], in_=ISC[:, b_, 0:L], axis=AX.X, op=ALU.max,
                                                                apply_absolute_value=True),
                         reads=[ik_], writes=[f"bisM{b_}"])
                    T.op("dve", lambda: nc.vector.tensor_scalar(out=bis[:, b_, 0:1], in0=bis[:, b_, 0:1], scalar1=1.001, scalar2=1e-3,
                                                                op0=ALU.mult, op1=ALU.add),
                         reads=[f"bisM{b_}"], writes=[f"bisM{b_}"])
                    T.op("dve", lambda: nc.vector.tensor_scalar(out=Hp[:, b_, :], in0=cstf[:, 6, 0:24], scalar1=bis[:, b_, 0:1],
                                                                scalar2=None, op0=ALU.mult),
                         reads=[f"bisM{b_}", "cstf"], writes=[f"Hp{b_}"])
                    T.op("dve", lambda: nc.vector.memset(bis[:, b_, 1:2], 0.0), writes=[f"bismid{b_}"])
                    T.op("dve", lambda: nc.vector.tensor_scalar(out=Hn[:, b_, :], in0=Hp[:, b_, :], scalar1=-0.5, scalar2=None,
                                                                op0=ALU.mult),
                         reads=[f"Hp{b_}"], writes=[f"Hn{b_}"])
                    T.op("dve", lambda: nc.vector.memset(bis[:, b_, 5:6], 0.0), writes=[f"bisnm{b_}"])
                    T.op("dve", lambda: nc.vector.tensor_tensor(out=ISC[:, b_, i * P:L], in0=ISC[:, b_, i * P:L], in1=causneg, op=ALU.add),
                         reads=[ik_, "cstf"], writes=[ik_])

                def topk_g(i, eng="dve"):
                    b_ = i % 2
                    L = (i + 1) * P
                    ik_ = f"ISC{b_}"
                    if eng == "act":
                        for k in range(K_BIS):
                            T.op("act", lambda: nc.scalar.activation(
                                out=WKa[:, 0:L], in_=ISC[:, b_, 0:L], func=AF.Sign, bias=bis[:, b_, 5:6], scale=1.0,
                                accum_out=bis[:, b_, 2:3]),
                                reads=[ik_, f"bisnm{b_}"], writes=["ropetmp0", "ropetmp1", f"biscnt{b_}"])
                            T.op("act", lambda: nc.scalar.activation(
                                out=bis[:, b_, 6:7], in_=bis[:, b_, 2:3], func=AF.Sign, bias=float(L - 512) + 0.5, scale=1.0),
                                reads=[f"biscnt{b_}"], writes=[f"bissg{b_}"])
                            T.op("act", lambda: nc.scalar.activation(
                                out=bis[:, b_, 5:6], in_=bis[:, b_, 6:7], func=AF.Identity, scale=Hn[:, b_, k:k + 1],
                                bias=bis[:, b_, 5:6]),
                                reads=[f"bissg{b_}", f"Hn{b_}", f"bisnm{b_}"], writes=[f"bisnm{b_}"])
                            yield
                        T.op("act", lambda: nc.scalar.activation(
                            out=bis[:, b_, 4:5], in_=bis[:, b_, 5:6], func=AF.Identity, scale=-1.0,
                            bias=Hn[:, b_, K_BIS - 1:K_BIS]),
                            reads=[f"bisnm{b_}", f"Hn{b_}"], writes=[f"bisthr{b_}"])
                        T.op("dve", lambda: nc.vector.tensor_scalar(out=MB[:, b_, 0:L], in0=ISC[:, b_, 0:L], scalar1=bis[:, b_, 4:5],
                                                                    scalar2=-1.0, op0=ALU.is_ge, op1=ALU.add),
                             reads=[ik_, f"bisthr{b_}"], writes=[f"MB{b_}"])
                        yield
                        return
                    for k in range(K_BIS):
                        T.op("dve", lambda: nc.vector.tensor_scalar(
                            out=WKb[:, 0:L], in0=ISC[:, b_, 0:L], scalar1=bis[:, b_, 1:2], scalar2=None,
                            op0=ALU.is_ge, op1=ALU.add, accum_out=bis[:, b_, 2:3]),
                            reads=[ik_, f"bismid{b_}"], writes=["WKb", f"biscnt{b_}"])
                        T.op("dve", lambda: nc.vector.tensor_scalar(
                            out=bis[:, b_, 3:4], in0=bis[:, b_, 2:3], scalar1=256.0, scalar2=-0.5, op0=ALU.is_ge, op1=ALU.add),
                            reads=[f"biscnt{b_}"], writes=[f"biscm{b_}"])
                        T.op("dve", lambda: nc.vector.scalar_tensor_tensor(
                            out=bis[:, b_, 1:2], in0=bis[:, b_, 3:4], scalar=Hp[:, b_, k:k + 1], in1=bis[:, b_, 1:2],
                            op0=ALU.mult, op1=ALU.add),
                            reads=[f"biscm{b_}", f"Hp{b_}", f"bismid{b_}"], writes=[f"bismid{b_}"])
                        yield
                    T.op("dve", lambda: nc.vector.tensor_tensor(out=bis[:, b_, 4:5], in0=bis[:, b_, 1:2],
                                                                in1=Hp[:, b_, K_BIS:K_BIS + 1], op=ALU.subtract),
                         reads=[f"bismid{b_}", f"Hp{b_}"], writes=[f"bisthr{b_}"])
                    T.op("dve", lambda: nc.vector.tensor_scalar(out=MB[:, b_, 0:L], in0=ISC[:, b_, 0:L], scalar1=bis[:, b_, 4:5],
                                                                scalar2=-1.0, op0=ALU.is_ge, op1=ALU.add),
                         reads=[ik_, f"bisthr{b_}"], writes=[f"MB{b_}"])
                    yield

                def dsa_attn_g(i):
                    b_ = i % 2
                    ktl = [(j, 0) for j in range(i + 1)]

                    def qk(j, c0):
                        return (DK[:, j * P:(j + 1) * P], DQ[:, :, i * P:(i + 1) * P], ["DK", "DQ"])

                    def extra(j, c0):
                        if i >= 2:
                            return [(MB[:, b_, j * P:(j + 1) * P], ID4[:, :, :], [f"MB{b_}", "ID4"], (0, 512))]
                        if j == i:
                            return [(caus01, ID4[:, :, :], ["cstb", "ID4"], (0, 512))]
                        return []

                    def fin(abank):
                        def blk(b, src, skey, rs, rkey, rcol=None):
                            T.op("dve", lambda: nc.vector.tensor_scalar(
                                out=mix[:, i, 64 * b:64 * (b + 1)], in0=src, scalar1=rs, scalar2=None, op0=ALU.mult),
                                reads=[skey, rkey], writes=[f"mix{i}"])
                        finish_wide(abank, 512, 4, blk)
                    yield from attn_wide_g(PT, 512, ktl, qk, extra, lambda j: DV[:, j, :], "DV", fin)

                gens = {}

                def step(g_):
                    if g_ is None:
                        return True
                    try:
                        next(g_)
                        return False
                    except StopIteration:
                        return True

                for i in range(NT):
                    if i == 3:
                        _chk("dsa_i3")
                    if 2 <= i + 2 < NT:
                        indexer(i + 2)
                        gens[i + 2] = topk_g(i + 2, "dve" if (i % 2 == 0) else "act")
                    A_ = dsa_attn_g(i)
                    B1 = gens.pop(i + 1, None)
                    B2 = gens.get(i + 2)
                    dA = dB1 = False
                    c2 = 0
                    n2 = K_BIS // 2 if B2 is not None else 0
                    while not (dA and dB1 and c2 >= n2):
                        if not dA:
                            dA = step(A_)
                        if not dB1:
                            dB1 = step(B1)
                        if c2 < n2:
                            step(B2)
                            c2 += 1
                flush_fin()
                T.barrier()
            _chk("dsa")

            with ExitStack() as esp:
                NQ = esp.enter_context(sbt("NQ", [P, 3, S], BF16))
                NQR = esp.enter_context(sbt("NQR", [P, 3, S], BF16))
                KST = esp.enter_context(sbt("KST", [P, 2, S], BF16))
                KWT = esp.enter_context(sbt("KWT", [P, 2, S], BF16))
                VS = esp.enter_context(sbt("VS", [P, NT, 2, 65], BF16))
                VW = esp.enter_context(sbt("VW", [P, NT, 2, 65], BF16))
                NG = esp.enter_context(sbt("NG", [P, NT, 18], F32))
                KCC = esp.enter_context(sbt("KCC", [P, 2, P], BF16))
                VCC = esp.enter_context(sbt("VCC", [P, 2, 64], BF16))
                esq = ExitStack()
                KCT = esq.enter_context(sbt("KCT", [P, S], BF16))
                VCT = esq.enter_context(sbt("VCT", [P, S], BF16))
                rtmp = esq.enter_context(sbt("rtmp2", [P, 2, 512], F32))
                W1f = esq.enter_context(sbt("W1f", [P, 2048], F32))
                W1b = esq.enter_context(sbt("W1b", [P, 2048], BF16))
                W2f = esq.enter_context(sbt("W2f", [P, 2, 64], F32))
                W2b = esq.enter_context(sbt("W2b", [P, 2, 64], BF16))
                peTf = esq.enter_context(sbt("peTf", [P, 2, 32], F32))
                peTb = esq.enter_context(sbt("peTb", [P, 2, 32], BF16))
                cb = esq.enter_context(sbt("cmpbias", [P, 2], F32))
                HT = esq.enter_context(sbt("HT", [P, 2, P], BF16))
                T.op("pool", lambda: nc.gpsimd.memset(KCC[:], 0.0), writes=["KCC"])
                plan_w(["nq0", "nq1", "nq2", "nks", "nkw", "nkc", "nvc", "nvs", "nvw", "ng"])
                T.op("pool", lambda: nc.gpsimd.memset(VS[:, :, :, 64:65], 1.0), writes=["VS"])
                T.op("pool", lambda: nc.gpsimd.memset(VW[:, :, :, 64:65], 1.0), writes=["VW"])
                for i in range(3):
                    proj_rope(l, wst, wbf, f"nq{i}", NQR[:, i, :], f"NQR{i}", cos64, sin64, ("cos64", "sin64"), rtmp,
                              raw_dst=NQ[:, i, :], raw_key=f"NQ{i}")
                T.op("pool", lambda: nc.gpsimd.memset(KST[64:128, 0, :], 0.0), writes=["KST"])
                T.op("pool", lambda: nc.gpsimd.memset(KST[0:64, 1, :], 0.0), writes=["KST"])
                T.op("pool", lambda: nc.gpsimd.memset(KWT[64:128, 0, :], 0.0), writes=["KWT"])
                T.op("pool", lambda: nc.gpsimd.memset(KWT[0:64, 1, :], 0.0), writes=["KWT"])
                proj_rope(l, wst, wbf, "nks", KST, "KST", cos64, sin64, ("cos64", "sin64"), rtmp, halves=True)
                proj_rope(l, wst, wbf, "nkw", KWT, "KWT", cos64, sin64, ("cos64", "sin64"), rtmp, halves=True)
                proj_plain(l, wst, wbf, "nkc", KCT, "KCT")
                proj_plain(l, wst, wbf, "nvc", VCT, "VCT")
                for nm, dst, dk_ in (("nvs", VS, "VS"), ("nvw", VW, "VW")):
                    def ev(g, view, pk, dst=dst, dk_=dk_):
                        T.op("act", lambda: nc.scalar.copy(out=dst[:, 4 * g:4 * g + 4, :, 0:64],
                                                           in_=view.rearrange("p q (h d) -> p q h d", h=2)),
                             reads=[pk], writes=[dk_])
                    proj_tok(l, wst, wbf, nm, ev)

                def ev(g, view, pk):
                    T.op("act", lambda: nc.scalar.activation(out=NG[:, 4 * g:4 * g + 4, :], in_=view[:, :, 0:18], func=AF.Exp, scale=-1.0),
                         reads=[pk], writes=["NG"])
                proj_tok(l, wst, wbf, "ng", ev)
                T.op("dve", lambda: nc.vector.tensor_scalar(out=NG[:], in0=NG[:], scalar1=1.0, scalar2=None, op0=ALU.add),
                     reads=["NG"], writes=["NG"])
                T.op("dve", lambda: nc.vector.reciprocal(out=NG[:], in_=NG[:]), reads=["NG"], writes=["NG"])

                T.dma(lambda: nc.sync.dma_start(out=W2f[:], in_=w2_d[l, :, :, :].rearrange("a p m -> p a m")), "W2f", writes=["W2f"])
                T.op("pool", lambda: nc.gpsimd.tensor_copy(out=W2b[:], in_=W2f[:]), reads=["W2f"], writes=["W2b"])
                T.dma(lambda: nc.sync.dma_start(out=peTf[:], in_=peT_d[l, :, :, :].rearrange("a p m -> p a m")), "peTf", writes=["peTf"])
                T.op("pool", lambda: nc.gpsimd.tensor_copy(out=peTb[:], in_=peTf[:]), reads=["peTf"], writes=["peTb"])
                for a, (src, skey) in enumerate(((KCT, "KCT"), (VCT, "VCT"))):
                    T.dma(lambda a=a: nc.sync.dma_start(out=W1f[:], in_=w1_d[l, a, :, :]), "W1f", writes=["W1f"])
                    T.op("dve", lambda a=a: nc.vector.tensor_copy(out=W1b[:], in_=W1f[:]), reads=["W1f"], writes=["W1b"])
                    for g in range(2):
                        hp = 64 * g
                        for ll in range(32):
                            T.op("pe", lambda ll=ll: nc.tensor.matmul(
                                PS[0][hp:hp + 64, 0:1], lhsT=W1b[hp:hp + 64, ll * 64:(ll + 1) * 64],
                                rhs=peTb[hp:hp + 64, a, ll:ll + 1], start=(ll == 0), stop=(ll == 31)),
                                reads=["W1b", "peTb"], writes=["ps0"])
                        T.op("dve", lambda: nc.vector.tensor_copy(out=cb[hp:hp + 64, 0:1], in_=PS[0][hp:hp + 64, 0:1]),
                             reads=["ps0"], writes=["cb"])
                        for ll in range(32):
                            rhs = src[hp:hp + 64, ll:ll + 16 * (NC_CMP - 1) + 1:16]
                            T.op("pe", lambda ll=ll, rhs=rhs: nc.tensor.matmul(
                                PS[1][hp:hp + 64, 0:NC_CMP], lhsT=W1b[hp:hp + 64, ll * 64:(ll + 1) * 64],
                                rhs=rhs, start=(ll == 0), stop=(ll == 31)),
                                reads=["W1b", skey], writes=["ps1"])
                        T.op("act", lambda: nc.scalar.activation(out=HT[hp:hp + 64, a, 0:NC_CMP], in_=PS[1][hp:hp + 64, 0:NC_CMP],
                                                                func=AF.Silu, bias=cb[hp:hp + 64, 0:1], scale=1.0),
                             reads=["ps1", "cb"], writes=["HT"])
                        if a == 0:
                            T.op("pe", lambda: nc.tensor.matmul(PS[2][hp:hp + 64, 0:NC_CMP], lhsT=W2b[hp:hp + 64, 0, :],
                                                                rhs=HT[hp:hp + 64, 0, 0:NC_CMP], start=True, stop=True),
                                 reads=["W2b", "HT"], writes=["ps2"])
                            T.op("act", lambda: nc.scalar.copy(out=KCC[hp:hp + 64, g, 0:NC_CMP], in_=PS[2][hp:hp + 64, 0:NC_CMP]),
                                 reads=["ps2"], writes=["KCC"])
                        else:
                            T.op("pe", lambda: nc.tensor.matmul(PS[3][0:NC_CMP, 0:64], lhsT=HT[hp:hp + 64, 1, 0:NC_CMP],
                                                                rhs=W2b[hp:hp + 64, 1, :], start=True, stop=True),
                                 reads=["W2b", "HT"], writes=["ps3"])
                            T.op("act", lambda: nc.scalar.copy(out=VCC[0:NC_CMP, g, :], in_=PS[3][0:NC_CMP, 0:64]),
                                 reads=["ps3"], writes=["VCC"])
                T.barrier()
                esq.close()
                MBN = esp.enter_context(sbt("MBN", [P, 2, 2, S], BF16))
                cmk = esp.enter_context(sbt("cmk", [P, 2, P], F32))
                cmkb = esp.enter_context(sbt("cmkb", [P, 2, P], BF16))
                fbt = esp.enter_context(sbt("fbt", [P, 2, 32], F32))
                pn = esp.enter_context(sbt("pnorm", [P, 6, P], F32))
                pT = esp.enter_context(sbt("pT", [P, 6, P], BF16))
                PP = esp.enter_context(sbt("PP", [P, 2, 132], F32))
                sc = esp.enter_context(sbt("selsc", [P, 2, 2, 32], F32))
                sm = esp.enter_context(sbt("selm", [P, 16], F32))
                mbk = esp.enter_context(sbt("mbk", [P, 2, 32], BF16))
                ON = esp.enter_context(sbt("ON", [P, 2, 6, 64], F32))
                rs2 = esp.enter_context(sbt("rs2", [P, 16], F32))
                T.op("dve", lambda: nc.vector.memset(PP[:, 0, :], 0.0), writes=["PP0"])
                T.op("dve", lambda: nc.vector.memset(PP[:, 1, :], 0.0), writes=["PP1"])
                T.op("dve", lambda: nc.vector.memset(pn[:], 0.0), writes=[f"pn{h_}" for h_ in range(6)])

                def nsa_cmp_g(i):
                    par = i % 2
                    ms = i % 2
                    T.dma(lambda i=i, ms=ms: nc.sync.dma_start(out=cmk[:, ms, :], in_=cmpm_d[i, :, :]), f"cmk{ms}", writes=[f"cmk{ms}"])
                    T.op("pool", lambda ms=ms: nc.gpsimd.tensor_copy(out=cmkb[:, ms, :], in_=cmk[:, ms, :]),
                         reads=[f"cmk{ms}"], writes=[f"cmkb{ms}"])
                    if i >= 8:
                        T.dma(lambda i=i, ms=ms: nc.sync.dma_start(out=fbt[:, ms, :], in_=fb_d[i, :, :]), f"fbt{ms}", writes=[f"fbt{ms}"])
                    for hd in range(6):
                        g, jj = hd // 3, hd % 3
                        hp = 64 * g
                        bank = hd // 4
                        cs = (hd % 4) * P
                        T.op("pe", lambda: nc.tensor.matmul(
                            PS[bank][:, cs:cs + P], lhsT=NQ[:, jj, i * P:(i + 1) * P], rhs=KCC[:, g, :],
                            start=True, stop=False), reads=[f"NQ{jj}", "KCC"], writes=[pskey[bank]])
                        T.op("pe", lambda: nc.tensor.matmul(
                            PS[bank][:, cs:cs + P], lhsT=identbig, rhs=cmkb[:, ms, :], start=False, stop=True),
                            reads=["cstb", f"cmkb{ms}"], writes=[pskey[bank]])
                        if hd % 2 == 1:
                            yield
                    for hd in range(6):
                        bank = hd // 4
                        cs = (hd % 4) * P
                        T.op("act", lambda: nc.scalar.activation(
                            out=pn[:, hd, 0:NC_CMP], in_=PS[bank][:, cs:cs + NC_CMP], func=AF.Exp, scale=0.125,
                            accum_out=rs2[:, hd:hd + 1]),
                            reads=[pskey[bank]], writes=[f"pn{hd}", f"rs2_{hd}"])
                        if hd % 2 == 1:
                            yield
                    rk = [f"rs2_{h_}" for h_ in range(6)]
                    T.op("dve", lambda: nc.vector.tensor_scalar(out=rs2[:, 0:6], in0=rs2[:, 0:6], scalar1=1e-30, scalar2=None, op0=ALU.max),
                         reads=rk, writes=rk)
                    T.op("dve", lambda: nc.vector.reciprocal(out=rs2[:, 0:6], in_=rs2[:, 0:6]), reads=rk, writes=rk)
                    for hd in range(6):
                        T.op("dve", lambda: nc.vector.tensor_scalar(
                            out=pn[:, hd, 0:NC_CMP], in0=pn[:, hd, 0:NC_CMP], scalar1=rs2[:, hd:hd + 1], scalar2=None,
                            op0=ALU.mult), reads=[f"pn{hd}", f"rs2_{hd}"], writes=[f"pn{hd}"])
                    yield
                    if i >= 8:
                        for g in range(2):
                            T.op("dve", lambda: nc.vector.tensor_reduce(
                                out=PP[:, g, 1:1 + NC_CMP], in_=pn[:, 3 * g:3 * g + 3, 0:NC_CMP].rearrange("p j c -> p c j"),
                                axis=AX.X, op=ALU.add),
                                reads=[f"pn{3 * g}", f"pn{3 * g + 1}", f"pn{3 * g + 2}"], writes=[f"PP{g}"])
                        yield
                    for hd in range(6):
                        tb = hd // 4
                        cs = (hd % 4) * P
                        T.op("pe", lambda: nc.tensor.transpose(PS[tb][:, cs:cs + P], pn[:, hd, :], ident_f),
                             reads=[f"pn{hd}", "cstf"], writes=[pskey[tb]])
                    T.op("act", lambda: nc.scalar.copy(out=pT[:, 0:4, :], in_=PS[0][:, :].rearrange("p (h t) -> p h t", h=4)),
                         reads=["ps0"], writes=[f"pT{h_}" for h_ in range(4)])
                    T.op("act", lambda: nc.scalar.copy(out=pT[:, 4:6, :], in_=PS[1][:, 0:2 * P].rearrange("p (h t) -> p h t", h=2)),
                         reads=["ps1"], writes=["pT4", "pT5"])
                    yield
                    for hd in range(6):
                        g = hd // 3
                        T.op("pe", lambda: nc.tensor.matmul(
                            PS[0][:, hd * 64:(hd + 1) * 64], lhsT=pT[0:NC_CMP, hd, :], rhs=VCC[0:NC_CMP, g, :], start=True, stop=True),
                            reads=[f"pT{hd}", "VCC"], writes=["ps0"])
                    T.op("dve", lambda: nc.vector.tensor_tensor(
                        out=ON[:, par, :, :], in0=PS[0][:, 0:384].rearrange("p (h d) -> p h d", h=6),
                        in1=NG[:, i, 0:6].unsqueeze(2).to_broadcast([P, 6, 64]), op=ALU.mult),
                        reads=["ps0", "NG"], writes=[f"ON{par}_{h_}" for h_ in range(6)])
                    if 0 not in NSA_BR[0]:
                        T.op("dve", lambda: nc.vector.memset(ON[:, par, :, :], 0.0), writes=[f"ON{par}_{h_}" for h_ in range(6)])
                    yield
                    for g in range(2):
                        if i >= 8:
                            T.op("dve", lambda: nc.vector.tensor_reduce(
                                out=sc[:, g, 0, :], in_=PP[:, g, 0:128].rearrange("p (n k) -> p n k", k=4), axis=AX.X, op=ALU.add),
                                reads=[f"PP{g}"], writes=[f"sc{g}"])
                            T.op("dve", lambda: nc.vector.tensor_tensor(out=sc[:, g, 0, :], in0=sc[:, g, 0, :],
                                                                        in1=PP[:, g, 4:132:4], op=ALU.add),
                                 reads=[f"PP{g}", f"sc{g}"], writes=[f"sc{g}"])
                            T.op("dve", lambda ms=ms: nc.vector.tensor_tensor(out=sc[:, g, 0, :], in0=sc[:, g, 0, :],
                                                                              in1=fbt[:, ms, :], op=ALU.add),
                                 reads=[f"fbt{ms}", f"sc{g}"], writes=[f"sc{g}"])
                            T.op("dve", lambda: nc.vector.tensor_copy(out=sc[:, g, 1, :], in_=sc[:, g, 0, :]),
                                 reads=[f"sc{g}"], writes=[f"scw{g}"])
                            T.op("dve", lambda: nc.vector.max(out=sm[:, 0:8], in_=sc[:, g, 1, :]), reads=[f"scw{g}"], writes=["sm"])
                            T.op("dve", lambda: nc.vector.match_replace(out=sc[:, g, 1, :], in_to_replace=sm[:, 0:8],
                                                                        in_values=sc[:, g, 1, :], imm_value=-3.0e9),
                                 reads=[f"scw{g}", "sm"], writes=[f"scw{g}"])
                            T.op("dve", lambda: nc.vector.max(out=sm[:, 8:16], in_=sc[:, g, 1, :]), reads=[f"scw{g}"], writes=["sm"])
                            T.op("dve", lambda: nc.vector.tensor_scalar(out=mbk[:, g, :], in0=sc[:, g, 0, :], scalar1=sm[:, 15:16],
                                                                        scalar2=-1.0, op0=ALU.is_ge, op1=ALU.add),
                                 reads=[f"sc{g}", "sm"], writes=[f"mbk{g}"])
                            L = (i + 1) * P
                            nb = L // 64
                            T.op("act", lambda nb=nb, L=L: nc.scalar.copy(
                                out=MBN[:, par, g, 0:L].rearrange("p (n k) -> p n k", k=64),
                                in_=mbk[:, g, 0:nb].unsqueeze(2).to_broadcast([P, nb, 64])),
                                reads=[f"mbk{g}"], writes=[f"MBN{par}_{g}"])
                            T.op("pool", lambda L=L: nc.gpsimd.tensor_tensor(out=MBN[:, par, g, i * P:L], in0=MBN[:, par, g, i * P:L],
                                                                             in1=caus01, op=ALU.add),
                                 reads=["cstb", f"MBN{par}_{g}"], writes=[f"MBN{par}_{g}"])
                    yield

                def nsa_attn_g(i):
                    par = i % 2
                    for g in range(2):
                        hp = 64 * g
                        for br in (1, 2):
                            if br not in NSA_BR[0]:
                                continue
                            KT_, kkey = (KST, "KST") if br == 1 else (KWT, "KWT")
                            VX, vkey = (VS, "VS") if br == 1 else (VW, "VW")
                            kt = list(range(i + 1)) if br == 1 else list(range(max(0, i - 4), i + 1))
                            ktl = [(j, 0) for j in kt]

                            def qk(j, c0, KT_=KT_, kkey=kkey, g=g):
                                return (KT_[:, g, j * P:(j + 1) * P], NQR[:, :, i * P:(i + 1) * P],
                                        [kkey, "NQR0", "NQR1", "NQR2"])

                            def extra(j, c0, br=br, g=g):
                                id3 = ID4[:, 0:3, :]
                                if br == 1:
                                    if i >= 8:
                                        return [(MBN[:, par, g, j * P:(j + 1) * P], id3, [f"MBN{par}_{g}", "ID4"], (0, 384))]
                                    return [(caus01, id3, ["cstb", "ID4"], (0, 384))] if j == i else []
                                ex = []
                                if j == i:
                                    ex.append((caus01, id3, ["cstb", "ID4"], (0, 384)))
                                if j == i - 4:
                                    ex.append((band01, id3, ["cstb", "ID4"], (0, 384)))
                                return ex

                            def fin(abank, br=br, g=g):
                                def blk(b, src, skey, rs, rkey, rcol=None):
                                    hd = 3 * g + b
                                    gi = br * 6 + hd
                                    if b == 0:
                                        rs3 = rsm[:, rcol:rcol + 3]
                                        T.op("dve", lambda: nc.vector.tensor_tensor(out=rs3, in0=rs3, in1=NG[:, i, gi:gi + 3], op=ALU.mult),
                                             reads=[rkey, "NG"], writes=[rkey])
                                    T.op("dve", lambda: nc.vector.scalar_tensor_tensor(
                                        out=ON[:, par, hd, :], in0=src, scalar=rs, in1=ON[:, par, hd, :],
                                        op0=ALU.mult, op1=ALU.add),
                                        reads=[skey, rkey, f"ON{par}_{hd}"], writes=[f"ON{par}_{hd}"])
                                finish_wide(abank, 384, 3, blk)
                            yield from attn_wide_g(PT, 384, ktl, qk, extra, lambda j, VX=VX, g=g: VX[:, j, g, :], vkey, fin)
                    pending_fin.append(lambda: T.op(
                        "dve", lambda: nc.vector.tensor_copy(out=mix[:, i, 640:1024], in_=ON[:, par, :, :].rearrange("p h d -> p (h d)")),
                        reads=[f"ON{par}_{h_}" for h_ in range(6)], writes=[f"mix{i}"]))
                    yield

                fw_banks[0] = [2]
                interleave(nsa_cmp_g(0), iter(()))
                for i in range(NT):
                    interleave(nsa_attn_g(i), nsa_cmp_g(i + 1) if i + 1 < NT else iter(()))
                flush_fin()
                fw_banks[0] = [0, 1, 2]
                T.barrier()
            _chk("nsa")

            if "mix" in dbg and l == dbg_layer[0]:
                for i in range(NT):
                    T.dma(lambda i=i: nc.sync.dma_start(out=dbg_d["mix"][i * P:(i + 1) * P, :], in_=mix[:, i, :]),
                          "store", reads=[f"mix{i}"], writes=[f"dbgmix{i}"])
            with ExitStack() as esp:
                WG = esp.enter_context(sbt("WG", [P, KC, D], BF16))
                WO = esp.enter_context(sbt("WO", [P, KC, D], BF16))
                lng = esp.enter_context(sbt("lng", [P, D], F32))
                lnb = esp.enter_context(sbt("lnb", [P, D], F32))
                xres = esp.enter_context(sbt("xres", [P, 2, D], F32))
                Gt = esp.enter_context(sbt("Gt", [P, D], F32))
                mg = esp.enter_context(sbt("mg", [P, D], F32))
                mgT = esp.enter_context(sbt("mgT", [P, 2, KC, P], BF16))
                zt = esp.enter_context(sbt("zt", [P, D], F32))
                xo = esp.enter_context(sbt("xo", [P, 2, D], F32))
                st6 = esp.enter_context(sbt("st6", [P, 2, 6], F32))
                mv = esp.enter_context(sbt("mv", [P, 4], F32))
                mhalf = esp.enter_context(sbt("mhalf", [P, 2], F32))
                T.op("pool", lambda: nc.gpsimd.memset(mhalf[:], -0.5), writes=["mhalf"])
                T.dma(lambda: nc.sync.dma_start(out=lng[:], in_=lng_d[l, :, :]), "lng", writes=["lng"])
                T.dma(lambda: nc.sync.dma_start(out=lnb[:], in_=lnb_d[l, :, :]), "lnb", writes=["lnb"])
                go = UOFF["g0"][0]
                for kc in range(KC):
                    slot = wslot_ctr[0] % 2
                    wslot_ctr[0] += 1
                    src = wperm_d[l, kc * P:(kc + 1) * P, go:go + D]
                    T.dma(lambda src=src, slot=slot: nc.sync.dma_start(out=wst[:, slot, :, :].rearrange("p a b -> p (a b)"), in_=src),
                          f"wst{slot}", writes=[f"wst{slot}"])
                    T.op("dve", lambda kc=kc, slot=slot: nc.vector.tensor_copy(out=WG[:, kc, :], in_=wst[:, slot, :, :].rearrange("p a b -> p (a b)")),
                         reads=[f"wst{slot}"], writes=[f"WG{kc}"])
                for kc in range(KC):
                    slot = wslot_ctr[0] % 2
                    wslot_ctr[0] += 1
                    src = wout_d[l, kc * P:(kc + 1) * P, :]
                    T.dma(lambda src=src, slot=slot: nc.sync.dma_start(out=wst[:, slot, :, :].rearrange("p a b -> p (a b)"), in_=src),
                          f"wst{slot}", writes=[f"wst{slot}"])
                    T.op("dve", lambda kc=kc, slot=slot: nc.vector.tensor_tensor(
                        out=WO[:, kc, :], in0=wst[:, slot, :, :].rearrange("p a b -> p (a b)"), in1=g1bc[:, l, :], op=ALU.mult),
                        reads=[f"wst{slot}", "g1bc"], writes=[f"WO{kc}"])
                xsrc = x_d if l == 0 else x1_d

                def epiA(i):
                    xs_ = i % 2
                    mp = i % 2
                    T.dma(lambda: nc.sync.dma_start(out=xres[:, xs_, :], in_=xsrc[i * P:(i + 1) * P, :]),
                          f"xres{xs_}", writes=[f"xres{xs_}"])
                    for hf in range(2):
                        for kc in range(KC):
                            T.op("pe", lambda: nc.tensor.matmul(
                                PS[hf][:, :], lhsT=uT[:, kc, i * P:(i + 1) * P], rhs=WG[:, kc, hf * 512:(hf + 1) * 512],
                                start=(kc == 0), stop=(kc == KC - 1)), reads=[f"uT{i}", f"WG{kc}"], writes=[pskey[hf]])
                        T.op("act", lambda: nc.scalar.activation(out=Gt[:, hf * 512:(hf + 1) * 512], in_=PS[hf][:, :], func=AF.Silu),
                             reads=[pskey[hf]], writes=["Gt"])
                    T.op("dve", lambda: nc.vector.tensor_tensor(out=mg[:], in0=Gt[:], in1=mix[:, i, :], op=ALU.mult),
                         reads=["Gt", f"mix{i}"], writes=["mg"])

                def epiA2(i):
                    mp = i % 2
                    for hf in range(2):
                        bank = 2 + hf
                        for q in range(4):
                            fc = hf * 4 + q
                            T.op("pe", lambda: nc.tensor.transpose(
                                PS[bank][:, q * P:(q + 1) * P], mg[:, fc * P:(fc + 1) * P], ident_f),
                                reads=["mg", "cstf"], writes=[pskey[bank]])
                        T.op("act", lambda: nc.scalar.copy(
                            out=mgT[:, mp, hf * 4:hf * 4 + 4, :], in_=PS[bank][:, :].rearrange("p (q t) -> p q t", q=4)),
                            reads=[pskey[bank]], writes=[f"mgT{mp}"])

                def epiB(i):
                    xs_ = i % 2
                    mp = i % 2
                    for hf in range(2):
                        bank = 4 + hf
                        for fc in range(KC):
                            T.op("pe", lambda: nc.tensor.matmul(
                                PS[bank][:, :], lhsT=mgT[:, mp, fc, :], rhs=WO[:, fc, hf * 512:(hf + 1) * 512],
                                start=(fc == 0), stop=(fc == KC - 1)), reads=[f"mgT{mp}", f"WO{fc}"], writes=[pskey[bank]])
                        sl = slice(hf * 512, (hf + 1) * 512)
                        T.op("dve", lambda: nc.vector.scalar_tensor_tensor(
                            out=zt[:, sl], in0=xres[:, xs_, sl], scalar=float(ALPHA), in1=PS[bank][:, :], op0=ALU.mult, op1=ALU.add),
                            reads=[f"xres{xs_}", pskey[bank]], writes=["zt"])
                        T.op("dve", lambda: nc.vector.bn_stats(out=st6[:, hf, :], in_=zt[:, sl]), reads=["zt"], writes=["st6"])
                    T.op("dve", lambda: nc.vector.bn_aggr(out=mv[:, 0:2], in_=st6[:].rearrange("p a b -> p (a b)")), reads=["st6"], writes=["mv"])
                    T.op("dve", lambda: nc.vector.tensor_scalar(out=mv[:, 2:3], in0=mv[:, 1:2], scalar1=float(LN_EPS), scalar2=None, op0=ALU.add),
                         reads=["mv"], writes=["mv2"])
                    T.op("pool", lambda: nc.gpsimd.tensor_tensor(out=mv[:, 3:4], in0=mv[:, 2:3], in1=mhalf[:, 0:1], op=ALU.pow),
                         reads=["mv2", "mhalf"], writes=["mv3"])
                    os_ = i % 2
                    T.op("dve", lambda: nc.vector.scalar_tensor_tensor(out=xo[:, os_, :], in0=zt[:], scalar=mv[:, 0:1], in1=lng[:],
                                                                       op0=ALU.subtract, op1=ALU.mult),
                         reads=["zt", "mv", "lng"], writes=[f"xo{os_}"])
                    T.op("dve", lambda: nc.vector.scalar_tensor_tensor(out=xo[:, os_, :], in0=xo[:, os_, :], scalar=mv[:, 3:4], in1=lnb[:],
                                                                       op0=ALU.mult, op1=ALU.add),
                         reads=[f"xo{os_}", "mv3", "lnb"], writes=[f"xo{os_}"])
                    dst = out_d if l == n_layers - 1 else x1_d
                    T.dma(lambda: nc.sync.dma_start(out=dst[i * P:(i + 1) * P, :], in_=xo[:, os_, :]),
                          f"store{os_}", reads=[f"xo{os_}"], writes=[f"dst{l}_{i}"])

                epiA(0)
                epiA2(0)
                for i in range(NT):
                    if i + 1 < NT:
                        epiA(i + 1)
                    epiB(i)
                    if l < n_layers - 1 and i >= 1:
                        make_uT_tile(xo[:, (i - 1) % 2, :], f"xo{(i - 1) % 2}", l + 1, i - 1, bank0=6)
                    if i + 1 < NT:
                        epiA2(i + 1)
                if l < n_layers - 1:
                    make_uT_tile(xo[:, (NT - 1) % 2, :], f"xo{(NT - 1) % 2}", l + 1, NT - 1, bank0=6)
                T.barrier()


dbg_layer = [0]
NSA_BR = [(0, 1, 2)]


def _consts():
    t = np.arange(S, dtype=np.float32)
    out = {}
    for nm, dim in (("64", 64), ("32", 32)):
        half = dim // 2
        inv = (10000.0 ** (-np.arange(half, dtype=np.float32) / half)).astype(np.float32)
        ang = t[None, :] * inv[:, None]
        cos = np.cos(ang).astype(np.float32)
        sin = np.sin(ang).astype(np.float32)
        c_full = np.concatenate([cos, cos], 0)
        s_full = np.concatenate([-sin, sin], 0)
        out["cos" + nm] = np.ascontiguousarray(np.tile(c_full, (P // dim, 1))).astype(ml_dtypes.bfloat16)
        out["sin" + nm] = np.ascontiguousarray(np.tile(s_full, (P // dim, 1))).astype(ml_dtypes.bfloat16)
    a = np.arange(P)
    tt, ss = a[:, None], a[None, :]
    cst = np.zeros((P, 10, P), np.float32)
    for m in range(64):
        cst[64 + m, 9, m] = 1.0
    for m in range(P):
        cst[(m // 64) * 64 + ((m % 64) + 32) % 64, 7, m] = 1.0
        cst[(m // 32) * 32 + ((m % 32) + 16) % 32, 8, m] = 1.0
    cst[:, 6, :] = (2.0 ** (-np.arange(P, dtype=np.float64).clip(0, 60)))[None, :]
    cst[:, 0, :] = np.eye(P)
    cst[:, 1, :] = np.where(ss <= tt, 0.0, -1.0)
    cst[:, 2, :] = np.where(ss > tt, 0.0, -1.0)
    cst[:, 3, :] = np.where(ss <= tt, 0.0, NEG)
    cst[:, 4, :] = np.eye(P) * MASKV
    cst[:, 5, :] = 1.0
    out["cst"] = cst
    out["cst_b"] = cst.astype(ml_dtypes.bfloat16)
    cm = np.zeros((NT, P, P), np.float32)
    fb = np.zeros((NT, P, 32), np.float32)
    cidx = np.arange(P)
    nidx = np.arange(32)
    for i in range(NT):
        tpos = i * P + a
        valid = (16 * cidx[None, :] + 31 <= tpos[:, None]) & (cidx[None, :] < NC_CMP)
        cm[i] = np.where(valid, 0.0, -1.0)
        cur = tpos // 64
        forced = (nidx[None, :] == 0) | (nidx[None, :] == cur[:, None]) | (nidx[None, :] == cur[:, None] - 1)
        future = (64 * nidx[None, :]) > tpos[:, None]
        fb[i] = np.where(forced, 1.0e9, np.where(future, -1.0e9, 0.0))
    out["cmp_mask"] = cm
    out["sel_fb"] = fb
    return out


_NC_CACHE = {}


def _prep_shared(w_ada, b_ada, w_in, b_f, cmp_pe, cmp_w1, cmp_w2, w_out, ln_g, ln_b):
    f = lambda a: np.ascontiguousarray(np.asarray(a, dtype=np.float32))
    sh = dict(_consts())
    sh["w_ada"] = f(w_ada)
    sh["b_ada"] = f(b_ada)
    sh["b_gate_bc"] = f(np.broadcast_to(np.asarray(b_ada)[:, None, 2 * D:3 * D], (DEPTH, P, D)))
    sh["w_perm"] = f(np.asarray(w_in)[:, :, PERM])
    bfp = np.zeros((DEPTH, P, 8), np.float32)
    bfp[:, 0:6, 0] = np.asarray(b_f)
    sh["b_f"] = bfp
    peT = np.asarray(cmp_pe).transpose(0, 1, 3, 2)
    sh["cmp_peT"] = f(np.concatenate([peT, peT], axis=2))
    w1 = np.asarray(cmp_w1).reshape(DEPTH, 2, 32, 64, 64).transpose(0, 1, 3, 2, 4).reshape(DEPTH, 2, 64, 32 * 64)
    sh["cmp_w1r"] = f(np.concatenate([w1, w1], axis=2))
    w2 = np.asarray(cmp_w2)
    sh["cmp_w2r"] = f(np.concatenate([w2, w2], axis=2))
    sh["w_out"] = f(w_out)
    sh["ln_g_bc"] = f(np.broadcast_to(np.asarray(ln_g)[:, None, :], (DEPTH, P, D)))
    sh["ln_b_bc"] = f(np.broadcast_to(np.asarray(ln_b)[:, None, :], (DEPTH, P, D)))
    return sh


def kernel(x, c, w_ada, b_ada, w_in, b_f, cmp_pe, cmp_w1, cmp_w2, w_out, ln_g, ln_b):
    x = np.asarray(x, dtype=np.float32)
    c = np.asarray(c, dtype=np.float32)
    B = x.shape[0]
    if "nc" not in _NC_CACHE:
        _NC_CACHE["nc"] = build_program()
    nc = _NC_CACHE["nc"]
    sh = _prep_shared(w_ada, b_ada, w_in, b_f, cmp_pe, cmp_w1, cmp_w2, w_out, ln_g, ln_b)
    in_maps = []
    for b in range(B):
        m = dict(sh)
        m["x"] = np.ascontiguousarray(x[b])
        ccol = np.ascontiguousarray(c[b].reshape(KC, P).T)
        m["c_col"] = ccol
        m["c_bc"] = np.ascontiguousarray(np.broadcast_to(ccol[:, :, None], (P, KC, P)))
        in_maps.append(m)
    res = run_bass_kernel_spmd(nc, in_maps, core_ids=list(range(B)))
    return np.stack([np.asarray(r["out"], dtype=np.float32) for r in res.results], axis=0)
```
